# Optimizing a Trainium2 kernel written in Bass

```python
import math
import jax
import jax.numpy as jnp
from jax import lax
import numpy as np

D_MODEL = 2048
BATCH = 4
SEQ = 2048
DEPTH = 4

GRID_W = 64
CTX_LEN = 256
N_EVEN = (DEPTH + 1) // 2
N_ODD = DEPTH // 2
EPS = 1e-6

BRANCH = D_MODEL // 2
S5_GROUP = 16
S5_GROUPS = BRANCH // S5_GROUP
S5_STATE = 64
S5_DT_MIN = 1e-3
S5_DT_MAX = 1e-1
DA_HEAD = 64
DA_HEADS = BRANCH // (2 * DA_HEAD)
DA_VDIM = 2 * DA_HEAD
Q_BLOCK = 128
ROPE_BASE = 10000.0
GLA_HEADS = 4
GLA_KEY = D_MODEL // 2
GLA_VAL = D_MODEL
GLA_DK = GLA_KEY // GLA_HEADS
GLA_DV = GLA_VAL // GLA_HEADS
GLA_RANK = 16
GLA_TAU = 16.0
GLA_CHUNK = 64

kernel_name = "hybrid_s5_diffattn_gla_prefix_dit"


def rms_norm(x, g):
    xf = x.astype(jnp.float32)
    xf = xf * lax.rsqrt(jnp.mean(xf * xf, axis=-1, keepdims=True) + EPS)
    return (xf * g.astype(jnp.float32)).astype(x.dtype)


def modulation(cond, w, b, n_chunks):
    width = n_chunks * D_MODEL
    m = jax.nn.silu(cond) @ w[:, :width] + b[:width]
    return jnp.split(m, n_chunks, axis=-1)


def modulate(x, g, shift, scale):
    return rms_norm(x, g) * (1 + scale) + shift


def _flip(t, axis, on):
    return jnp.flip(t, axis=axis) if on else t


def axial_rope(n_tokens):
    rows = n_tokens // GRID_W
    row = jnp.repeat(jnp.arange(rows, dtype=jnp.float32), GRID_W)
    col = jnp.tile(jnp.arange(GRID_W, dtype=jnp.float32), rows)
    n_freq = DA_HEAD // 4
    inv_freq = ROPE_BASE ** (-jnp.arange(n_freq, dtype=jnp.float32) / n_freq)
    ang_r = row[:, None] * inv_freq
    ang_c = col[:, None] * inv_freq
    ang = jnp.concatenate([ang_r, ang_r, ang_c, ang_c], axis=-1)
    return jnp.cos(ang), jnp.sin(ang)


def apply_rope(x, cos, sin):
    a, b, c_, d = jnp.split(x, 4, axis=-1)
    rot = jnp.concatenate([-b, a, -d, c_], axis=-1)
    cos = cos[:, None, None, :].astype(x.dtype)
    sin = sin[:, None, None, :].astype(x.dtype)
    return x * cos + rot * sin


def _diff_block(qb, keys, vals, lam):
    s = jnp.einsum("bqhme,bkhme->bhmqk", qb, keys).astype(jnp.float32) * (DA_HEAD ** -0.5)
    p = jax.nn.softmax(s, axis=-1)
    w = p[:, :, 0] - lam * p[:, :, 1]
    return jnp.einsum("bhqk,bkhv->bqhv", w.astype(vals.dtype), vals)


def diff_attention(q_lat, k_lat, v_lat, q_ctx, k_ctx, v_ctx, qn_g, kn_g, lam_vecs, subln_g, lam_init):
    B, L, _ = q_lat.shape

    def heads_qk(t, g):
        return rms_norm(t.reshape(t.shape[0], t.shape[1], DA_HEADS, 2, DA_HEAD), g)

    def heads_v(t):
        return t.reshape(t.shape[0], t.shape[1], DA_HEADS, DA_VDIM)

    cos, sin = axial_rope(L)
    q = apply_rope(heads_qk(q_lat, qn_g), cos, sin)
    k = apply_rope(heads_qk(k_lat, kn_g), cos, sin)
    kc = heads_qk(k_ctx, kn_g)
    vc = heads_v(v_ctx)
    keys = jnp.concatenate([kc, k], axis=1)
    vals = jnp.concatenate([vc, heads_v(v_lat)], axis=1)
    lv = lam_vecs.astype(jnp.float32)
    lam = jnp.exp(jnp.sum(lv[0] * lv[1])) - jnp.exp(jnp.sum(lv[2] * lv[3])) + lam_init
    nb = L // Q_BLOCK
    qb = q.reshape(B, nb, Q_BLOCK, DA_HEADS, 2, DA_HEAD).swapaxes(0, 1)
    o = lax.map(lambda blk: _diff_block(blk, keys, vals, lam), qb)
    o = o.swapaxes(0, 1).reshape(B, L, DA_HEADS, DA_VDIM)

    def finish(t):
        t = rms_norm(t, subln_g) * (1 - lam_init)
        return t.reshape(t.shape[0], t.shape[1], BRANCH)

    y_lat = finish(o)
    y_ctx = None
    if q_ctx is not None:
        qc = heads_qk(q_ctx, qn_g)
        y_ctx = finish(_diff_block(qc, kc, vc, lam))
    return y_lat, y_ctx


def s5_discretize(a_re, a_im, log_dt, b_re, b_im):
    a_re, a_im = a_re.astype(jnp.float32), a_im.astype(jnp.float32)
    b_re, b_im = b_re.astype(jnp.float32), b_im.astype(jnp.float32)
    dt = jnp.exp(log_dt.astype(jnp.float32))[:, None]
    mag = jnp.exp(a_re * dt)
    ang = a_im * dt
    lb_re, lb_im = mag * jnp.cos(ang), mag * jnp.sin(ang)
    nr, ni = lb_re - 1.0, lb_im
    den = a_re * a_re + a_im * a_im
    f_re = (nr * a_re + ni * a_im) / den
    f_im = (ni * a_re - nr * a_im) / den
    bb_re = f_re[..., None] * b_re - f_im[..., None] * b_im
    bb_im = f_re[..., None] * b_im + f_im[..., None] * b_re
    return lb_re, lb_im, bb_re, bb_im


def _s5_combine(e1, e2):
    ar1, ai1, br1, bi1 = e1
    ar2, ai2, br2, bi2 = e2
    return (ar1 * ar2 - ai1 * ai2, ar1 * ai2 + ai1 * ar2,
            ar2 * br1 - ai2 * bi1 + br2, ar2 * bi1 + ai2 * br1 + bi2)


def s5_scan(u, h0_re, h0_im, lb_re, lb_im, bb_re, bb_im):
    bu_re = jnp.einsum("bngc,gpc->bngp", u, bb_re)
    bu_im = jnp.einsum("bngc,gpc->bngp", u, bb_im)
    bu_re = bu_re.at[:, 0].add(lb_re * h0_re - lb_im * h0_im)
    bu_im = bu_im.at[:, 0].add(lb_re * h0_im + lb_im * h0_re)
    a_re = jnp.broadcast_to(lb_re, bu_re.shape)
    a_im = jnp.broadcast_to(lb_im, bu_im.shape)
    _, _, h_re, h_im = lax.associative_scan(_s5_combine, (a_re, a_im, bu_re, bu_im), axis=1)
    return h_re, h_im


def s5_readout(h_re, h_im, c_re, c_im):
    return jnp.einsum("bngp,gcp->bngc", h_re, c_re) - jnp.einsum("bngp,gcp->bngc", h_im, c_im)


def s5_mixer(u_lat, u_ctx, a_re, a_im, log_dt, b_re, b_im, c_re, c_im, d_skip, w_glu, with_ctx_out):
    dtype = u_lat.dtype

    def grp(u):
        return u.astype(jnp.float32).reshape(u.shape[0], u.shape[1], S5_GROUPS, S5_GROUP)

    ul, uc = grp(u_lat), grp(u_ctx)
    zero = jnp.zeros((ul.shape[0], S5_GROUPS, S5_STATE), jnp.float32)
    ys_lat, ys_ctx = [], []
    for d in range(2):
        rev = d == 1
        lb_re, lb_im, bb_re, bb_im = s5_discretize(a_re[d], a_im[d], log_dt[d], b_re[d], b_im[d])
        cr, ci = c_re[d].astype(jnp.float32), c_im[d].astype(jnp.float32)
        hc_re, hc_im = s5_scan(_flip(uc, 1, rev), zero, zero, lb_re, lb_im, bb_re, bb_im)
        hl_re, hl_im = s5_scan(_flip(ul, 1, rev), hc_re[:, -1], hc_im[:, -1], lb_re, lb_im, bb_re, bb_im)
        ys_lat.append(_flip(s5_readout(hl_re, hl_im, cr, ci), 1, rev))
        if with_ctx_out:
            ys_ctx.append(_flip(s5_readout(hc_re, hc_im, cr, ci), 1, rev))
    dg = d_skip.astype(jnp.float32).reshape(S5_GROUPS, S5_GROUP)
    wg = w_glu.astype(jnp.float32)

    def finish(y, u):
        y = (y + dg * u).reshape(u.shape[0], u.shape[1], BRANCH)
        g = jax.nn.gelu(y)
        return (g * jax.nn.sigmoid(g @ wg)).astype(dtype)

    y_lat = finish(ys_lat[0] + ys_lat[1], ul)
    y_ctx = finish(ys_ctx[0] + ys_ctx[1], uc) if with_ctx_out else None
    return y_lat, y_ctx


def gla_chunk_scan(q, k, v, log_a, s0):
    B, H, N, _ = k.shape
    nc = N // GLA_CHUNK

    def chunks(t):
        return t.reshape(B, H, nc, GLA_CHUNK, t.shape[-1]).transpose(2, 0, 1, 3, 4)

    upto = jnp.tril(jnp.ones((GLA_CHUNK, GLA_CHUNK), dtype=bool))

    def step(s, inp):
        kk, vv, aa = inp[0], inp[1], inp[2]
        b = jnp.cumsum(aa, axis=2)
        b_end = b[:, :, -1:, :]
        s_new = (jnp.exp(b_end[:, :, 0, :, None]) * s
                 + jnp.einsum("bhsd,bhsv->bhdv", kk * jnp.exp(b_end - b), vv))
        if q is None:
            return s_new, None
        q_dec = inp[3] * jnp.exp(b)
        scores = jnp.where(upto, jnp.einsum("bhtd,bhsd->bhts", q_dec, kk * jnp.exp(-b)), 0.0)
        o = jnp.einsum("bhts,bhsv->bhtv", scores, vv) + jnp.einsum("bhtd,bhdv->bhtv", q_dec, s)
        return s_new, o

    xs = [chunks(k), chunks(v), chunks(log_a)]
    if q is not None:
        xs.append(chunks(q))
    s_fin, o = lax.scan(step, s0, tuple(xs))
    if q is not None:
        o = o.transpose(1, 2, 0, 3, 4).reshape(B, H, N, v.shape[-1])
    return o, s_fin


def _gla_heads(t, dh):
    return t.astype(jnp.float32).reshape(t.shape[0], t.shape[1], GLA_HEADS, dh).transpose(0, 2, 1, 3)


def _gla_log_decay(h, wa1, wa2, ba):
    logits = ((h @ wa1) @ wa2 + ba).astype(jnp.float32)
    return _gla_heads(jax.nn.log_sigmoid(logits), GLA_DK) / GLA_TAU


def _gla_out(o, g, dtype):
    o = rms_norm(o.transpose(0, 2, 1, 3), g)
    return o.reshape(o.shape[0], o.shape[1], GLA_VAL).astype(dtype)


def gla_mixer(h_lat, h_ctx, q_lat, k_lat, v_lat, q_ctx, k_ctx, v_ctx, wa1, wa2, ba, norm_g):
    dtype = v_lat.dtype
    scale = GLA_DK ** -0.5
    ql = _gla_heads(q_lat, GLA_DK) * scale
    kl, vl = _gla_heads(k_lat, GLA_DK), _gla_heads(v_lat, GLA_DV)
    kc, vc = _gla_heads(k_ctx, GLA_DK), _gla_heads(v_ctx, GLA_DV)
    qc = None if q_ctx is None else _gla_heads(q_ctx, GLA_DK) * scale
    s0 = jnp.zeros((kl.shape[0], GLA_HEADS, GLA_DK, GLA_DV), jnp.float32)
    outs_lat, outs_ctx = [], []
    for d in range(2):
        rev = d == 1
        al = _flip(_gla_log_decay(h_lat, wa1[d], wa2[d], ba[d]), 2, rev)
        ac = _flip(_gla_log_decay(h_ctx, wa1[d], wa2[d], ba[d]), 2, rev)
        oc, s_ctx = gla_chunk_scan(None if qc is None else _flip(qc, 2, rev),
                                   _flip(kc, 2, rev), _flip(vc, 2, rev), ac, s0)
        ol, _ = gla_chunk_scan(_flip(ql, 2, rev), _flip(kl, 2, rev), _flip(vl, 2, rev), al, s_ctx)
        outs_lat.append(_flip(ol, 2, rev))
        if qc is not None:
            outs_ctx.append(_flip(oc, 2, rev))
    y_lat = _gla_out(outs_lat[0] + outs_lat[1], norm_g, dtype)
    y_ctx = None if qc is None else _gla_out(outs_ctx[0] + outs_ctx[1], norm_g, dtype)
    return y_lat, y_ctx


def _merge_even(s5_out, zs, da_out, zd, w_out):
    return jnp.concatenate([s5_out * jax.nn.silu(zs), da_out * jax.nn.silu(zd)], axis=-1) @ w_out


def even_layer(x_lat, x_ctx, c, c_ctx, ada_w, ada_b, norm_g, w_in, w_out,
               a_re, a_im, log_dt, b_re, b_im, c_re, c_im, d_skip, w_glu,
               qn_g, kn_g, lam_vecs, subln_g, lam_init, with_ctx_out):
    shift, scale, gate = modulation(c[:, None, :], ada_w, ada_b, 3)
    h_lat = modulate(x_lat, norm_g, shift, scale)
    u_l, zs_l, q_l, k_l, v_l, zd_l = jnp.split(h_lat @ w_in, 6, axis=-1)
    mod_c = modulation(c_ctx, ada_w, ada_b, 3 if with_ctx_out else 2)
    h_ctx = modulate(x_ctx, norm_g, mod_c[0], mod_c[1])
    if with_ctx_out:
        u_c, zs_c, q_c, k_c, v_c, zd_c = jnp.split(h_ctx @ w_in, 6, axis=-1)
    else:
        q_c = None
        u_c = h_ctx @ w_in[:, 0:BRANCH]
        k_c = h_ctx @ w_in[:, 3 * BRANCH:4 * BRANCH]
        v_c = h_ctx @ w_in[:, 4 * BRANCH:5 * BRANCH]
    s5_l, s5_c = s5_mixer(u_l, u_c, a_re, a_im, log_dt, b_re, b_im, c_re, c_im, d_skip, w_glu, with_ctx_out)
    da_l, da_c = diff_attention(q_l, k_l, v_l, q_c, k_c, v_c, qn_g, kn_g, lam_vecs, subln_g, lam_init)
    x_lat = x_lat + gate * _merge_even(s5_l, zs_l, da_l, zd_l, w_out)
    if with_ctx_out:
        x_ctx = x_ctx + mod_c[2] * _merge_even(s5_c, zs_c, da_c, zd_c, w_out)
    return x_lat, x_ctx


def odd_layer(x_lat, x_ctx, c, c_ctx, ada_w, ada_b, norm_g, w_in, w_out,
              wa1, wa2, ba, gla_norm_g, with_ctx_out):
    shift, scale, gate = modulation(c[:, None, :], ada_w, ada_b, 3)
    h_lat = modulate(x_lat, norm_g, shift, scale)
    cuts = (GLA_KEY, 2 * GLA_KEY, 2 * GLA_KEY + GLA_VAL)
    q_l, k_l, v_l, z_l = jnp.split(h_lat @ w_in, cuts, axis=-1)
    mod_c = modulation(c_ctx, ada_w, ada_b, 3 if with_ctx_out else 2)
    h_ctx = modulate(x_ctx, norm_g, mod_c[0], mod_c[1])
    if with_ctx_out:
        q_c, k_c, v_c, z_c = jnp.split(h_ctx @ w_in, cuts, axis=-1)
    else:
        q_c = None
        k_c = h_ctx @ w_in[:, cuts[0]:cuts[1]]
        v_c = h_ctx @ w_in[:, cuts[1]:cuts[2]]
    y_l, y_c = gla_mixer(h_lat, h_ctx, q_l, k_l, v_l, q_c, k_c, v_c, wa1, wa2, ba, gla_norm_g)
    x_lat = x_lat + gate * ((y_l * jax.nn.silu(z_l)) @ w_out)
    if with_ctx_out:
        x_ctx = x_ctx + mod_c[2] * ((y_c * jax.nn.silu(z_c)) @ w_out)
    return x_lat, x_ctx


def setup_inputs(seed: int = 0) -> dict:
    key = jax.random.key(seed)
    ks = iter(jax.random.split(key, 40))
    f32 = jnp.float32
    D = D_MODEL

    def nrm(shape, scale):
        return jax.random.normal(next(ks), shape, f32) * scale

    n_idx = jnp.arange(S5_STATE, dtype=f32)
    s5_shape = (N_EVEN, 2, S5_GROUPS, S5_STATE)
    return {
        "x": nrm((BATCH, SEQ, D), 1.0),
        "c": nrm((BATCH, D), 1.0),
        "ctx": nrm((BATCH, CTX_LEN, D), 1.0),
        "c_ctx": nrm((D,), 1.0),
        "ada_w": nrm((DEPTH, D, 3 * D), 0.2 * D ** -0.5),
        "ada_b": nrm((DEPTH, 3 * D), 0.01),
        "norm_g": 1.0 + nrm((DEPTH, D), 0.05),
        "ev_w_in": nrm((N_EVEN, D, 3 * D), D ** -0.5),
        "ev_w_out": nrm((N_EVEN, D, D), D ** -0.5),
        "s5_a_re": -0.5 + nrm(s5_shape, 0.01),
        "s5_a_im": math.pi * n_idx + nrm(s5_shape, 0.01),
        "s5_log_dt": jax.random.uniform(next(ks), (N_EVEN, 2, S5_GROUPS), f32,
                                        math.log(S5_DT_MIN), math.log(S5_DT_MAX)),
        "s5_b_re": nrm((N_EVEN, 2, S5_GROUPS, S5_STATE, S5_GROUP), (2 * S5_GROUP) ** -0.5),
        "s5_b_im": nrm((N_EVEN, 2, S5_GROUPS, S5_STATE, S5_GROUP), (2 * S5_GROUP) ** -0.5),
        "s5_c_re": nrm((N_EVEN, 2, S5_GROUPS, S5_GROUP, S5_STATE), 0.5 ** 0.5),
        "s5_c_im": nrm((N_EVEN, 2, S5_GROUPS, S5_GROUP, S5_STATE), 0.5 ** 0.5),
        "s5_d": nrm((N_EVEN, BRANCH), 1.0),
        "s5_w_glu": nrm((N_EVEN, BRANCH, BRANCH), BRANCH ** -0.5),
        "da_qn_g": 1.0 + nrm((N_EVEN, DA_HEAD), 0.05),
        "da_kn_g": 1.0 + nrm((N_EVEN, DA_HEAD), 0.05),
        "da_lam": nrm((N_EVEN, 4, DA_HEAD), 0.1),
        "da_subln_g": 1.0 + nrm((N_EVEN, DA_VDIM), 0.05),
        "od_w_in": nrm((N_ODD, D, 3 * D), D ** -0.5),
        "od_w_out": nrm((N_ODD, GLA_VAL, D), GLA_VAL ** -0.5),
        "gla_wa1": nrm((N_ODD, 2, D, GLA_RANK), D ** -0.5),
        "gla_wa2": nrm((N_ODD, 2, GLA_RANK, GLA_KEY), GLA_RANK ** -0.5),
        "gla_ba": nrm((N_ODD, 2, GLA_KEY), 0.1),
        "gla_norm_g": 1.0 + nrm((N_ODD, GLA_DV), 0.05),
    }


def reference(x, c, ctx, c_ctx, ada_w, ada_b, norm_g, ev_w_in, ev_w_out,
              s5_a_re, s5_a_im, s5_log_dt, s5_b_re, s5_b_im, s5_c_re, s5_c_im, s5_d, s5_w_glu,
              da_qn_g, da_kn_g, da_lam, da_subln_g,
              od_w_in, od_w_out, gla_wa1, gla_wa2, gla_ba, gla_norm_g):
    x_lat, x_ctx = x, ctx
    for i in range(DEPTH):
        j = i // 2
        with_ctx_out = i < DEPTH - 1
        if i % 2 == 0:
            lam_init = 0.8 - 0.6 * math.exp(-0.3 * i)
            x_lat, x_ctx = even_layer(
                x_lat, x_ctx, c, c_ctx, ada_w[i], ada_b[i], norm_g[i], ev_w_in[j], ev_w_out[j],
                s5_a_re[j], s5_a_im[j], s5_log_dt[j], s5_b_re[j], s5_b_im[j], s5_c_re[j], s5_c_im[j],
                s5_d[j], s5_w_glu[j], da_qn_g[j], da_kn_g[j], da_lam[j], da_subln_g[j],
                lam_init, with_ctx_out)
        else:
            x_lat, x_ctx = odd_layer(
                x_lat, x_ctx, c, c_ctx, ada_w[i], ada_b[i], norm_g[i], od_w_in[j], od_w_out[j],
                gla_wa1[j], gla_wa2[j], gla_ba[j], gla_norm_g[j], with_ctx_out)
    return x_lat
```

```python
import math
import numpy as np
import concourse.bass as bass
import concourse.mybir as mybir
from concourse.bass_utils import run_bass_kernel_spmd

F32 = mybir.dt.float32
F32R = mybir.dt.float32r
I32 = mybir.dt.int32
AF = mybir.ActivationFunctionType
ALU = mybir.AluOpType

D = 2048
NT = 2304
NCTX = 256
DEPTH = 4
EPS = 1e-6
PI = math.pi
BLKS = [(0, 256, 1)] + [(256 + 512 * i, 512, 0) for i in range(4)]
TC = 128


class Buf:
    __slots__ = ("w", "r")

    def __init__(self):
        self.w = None
        self.r = {}


class T:
    __slots__ = ("ap", "b")

    def __init__(self, ap):
        self.ap = ap
        self.b = Buf()

    def __getitem__(self, k):
        return self.ap[k]


class FW:
    NDMA = 24
    SEM_ROLL = 30000

    def __init__(self, nc):
        self.nc = nc
        self.engs = {"pe": nc.tensor, "act": nc.scalar, "dve": nc.vector, "pool": nc.gpsimd, "sp": nc.sync}
        self.ops = {e: [] for e in self.engs}
        self.sems, self.cnt, self.cur, self.owner = {}, {}, {}, {}
        self.seen = {e: {} for e in self.engs}
        self.nsem = 0
        for e in self.engs:
            self._new_csem(e)
        self.dma_keys = []
        for i in range(self.NDMA):
            k = f"dma{i}"
            self.sems[k] = nc.alloc_semaphore(k)
            self.cnt[k] = 0
            self.dma_keys.append(k)
        self.dma_rr = 0
        self.out_tickets = []
        self.n_ins = 0

    def _new_csem(self, e):
        k = f"c_{e}_{self.nsem}"
        self.nsem += 1
        self.sems[k] = self.nc.alloc_semaphore(k)
        self.cnt[k] = 0
        self.cur[e] = k
        self.owner[k] = e

    def _waits(self, e, reads, writes, extra=()):
        need = {}

        def add(k, v):
            if need.get(k, 0) < v:
                need[k] = v
        for t in reads:
            if t.b.w is not None:
                add(*t.b.w)
        for t in writes:
            if t.b.w is not None:
                add(*t.b.w)
            for k, v in t.b.r.items():
                add(k, v)
        for k, v in extra:
            add(k, v)
        out = []
        seen = self.seen[e]
        for k, v in need.items():
            if seen.get(k, 0) >= v:
                continue
            if self.owner.get(k) == e and e == "pe":
                continue
            seen[k] = v
            out.append((k, v))
        return out

    def _mark(self, t, reads, writes):
        k, v = t
        for x in reads:
            if x.b.r.get(k, 0) < v:
                x.b.r[k] = v
        for x in writes:
            x.b.w = t
            x.b.r = {}

    def op(self, e, fn, R=(), W=()):
        waits = self._waits(e, R, W)
        k = self.cur[e]
        if self.cnt[k] >= self.SEM_ROLL:
            self._new_csem(e)
            k = self.cur[e]
        self.cnt[k] += 1
        t = (k, self.cnt[k])
        self.ops[e].append((waits, fn, (k, 1)))
        self._mark(t, R, W)
        self.n_ins += 1
        return t

    def dma(self, out, in_, R=(), W=(), q="sp", is_output=False, **kw):
        k = self.dma_keys[self.dma_rr % self.NDMA]
        self.dma_rr += 1
        extra = [(k, self.cnt[k])] if self.cnt[k] > 0 else []
        waits = self._waits(q, R, W, extra)
        self.cnt[k] += 16
        t = (k, self.cnt[k])

        def fn(eng, out=out, in_=in_, kw=kw):
            return eng.dma_start(out=out, in_=in_, **kw)
        self.ops[q].append((waits, fn, (k, 16)))
        self._mark(t, R, W)
        if is_output:
            self.out_tickets.append(t)
        self.n_ins += 1
        return t

    def barrier(self):
        allv = [(k, v) for k, v in self.cnt.items() if v > 0]
        for e in self.engs:
            seen = self.seen[e]
            waits = []
            for k, v in allv:
                if self.owner.get(k) == e and e == "pe":
                    continue
                if seen.get(k, 0) < v:
                    seen[k] = v
                    waits.append((k, v))
            if waits:
                self.ops[e].append((waits, None, None))

    def finish(self):
        self.barrier()
        nc, sems = self.nc, self.sems
        with nc.Block() as block:
            def mk(e):
                def body(eng):
                    for waits, fn, inc in self.ops[e]:
                        for k, v in waits:
                            eng.wait_ge(sems[k], v)
                        if fn is not None:
                            fn(eng).then_inc(sems[inc[0]], inc[1])
                return body
            block.sync(mk("sp"))
            block.scalar(mk("act"))
            block.vector(mk("dve"))
            block.gpsimd(mk("pool"))
            block.tensor(mk("pe"))


class KB:
    ARENA_WORDS = 52000

    def __init__(self, nlayers=DEPTH, debug=()):
        self.nlayers = nlayers
        self.debug = set(debug)
        nc = self.nc = bass.Bass("TRN2", target_bir_lowering=False)
        self.fw = FW(nc)
        arena = nc.alloc_sbuf_tensor("arena", [128, self.ARENA_WORDS], F32)
        self.arena_addr = int(nc.lookup_mloc(arena).addr)
        self.ntile = 0
        self.base = 0
        self.off = 0
        self.psum = nc.alloc_psum_tensor("psum", [128, 4096], F32).ap()
        self.PS = [T(self.psum[:, i * 512:(i + 1) * 512]) for i in range(8)]
        self.din = {}
        self.rr = 0

    def inp(self, name, shape):
        ap = self.nc.dram_tensor(name, list(shape), F32, kind="ExternalInput").ap()
        self.din[name] = ap
        return ap

    def scratch(self, name, shape):
        kind = "ExternalOutput" if name in self.debug else "Internal"
        return self.nc.dram_tensor(name, list(shape), F32, kind=kind).ap()

    def tile(self, shape, dt=F32):
        n = int(np.prod(shape[1:]))
        assert self.off + n <= self.ARENA_WORDS, ("SBUF overflow", self.off, n)
        self.ntile += 1
        h = self.nc.alloc_sbuf_tensor_at(f"t{self.ntile}", [int(x) for x in shape], dt, offset=self.arena_addr + 4 * self.off)
        self.off += (n + 7) // 8 * 8
        return T(h.ap())

    def new_phase(self):
        self.fw.barrier()
        self.off = self.base

    def op(self, e, fn, R=(), W=()):
        return self.fw.op(e, fn, R, W)

    def mm(self, out, o_ap, lt, l_ap, rt, r_ap, start=True, stop=True):
        self.fw.op("pe", lambda e: e.matmul(o_ap, lhsT=l_ap, rhs=r_ap, start=start, stop=stop), [lt, rt], [out])

    def tp(self, out, o_ap, it, i_ap, ident):
        self.fw.op("pe", lambda e: e.transpose(o_ap, i_ap, ident.ap[0:i_ap.shape[0], 0:i_ap.shape[0]]), [it, ident], [out])

    def tt(self, e, out, o_ap, a, a_ap, b, b_ap, op):
        self.fw.op(e, lambda g: g.tensor_tensor(o_ap, a_ap, b_ap, op), [a, b], [out])

    def ts(self, e, out, o_ap, a, a_ap, s1, s2, op0, op1=None, R=()):
        if op1 is None:
            self.fw.op(e, lambda g: g.tensor_scalar(o_ap, a_ap, s1, None, op0), [a] + list(R), [out])
        else:
            self.fw.op(e, lambda g: g.tensor_scalar(o_ap, a_ap, s1, s2, op0, op1), [a] + list(R), [out])

    def stt(self, e, out, o_ap, a, a_ap, sc, b, b_ap, op0, op1, R=()):
        e = "dve"
        self.fw.op(e, lambda g: g.scalar_tensor_tensor(o_ap, a_ap, sc, b_ap, op0, op1), [a, b] + list(R), [out])

    def act(self, out, o_ap, a, a_ap, func, bias=None, scale=None, R=()):
        kw = {}
        if bias is not None:
            kw["bias"] = bias
        if scale is not None:
            kw["scale"] = scale
        self.fw.op("act", lambda g: g.activation(o_ap, a_ap, func, **kw), [a] + list(R), [out])

    def cp(self, e, out, o_ap, a, a_ap):
        if e == "act":
            self.fw.op("act", lambda g: g.copy(o_ap, a_ap), [a], [out])
        else:
            self.fw.op(e, lambda g: g.tensor_copy(o_ap, a_ap), [a], [out])

    def alt(self, engines=("dve", "act")):
        self.rr += 1
        return engines[self.rr % len(engines)]

    def ld(self, out, o_ap, src, q="sp"):
        self.fw.dma(o_ap, src, W=[out], q=q)

    def st(self, dst, it, i_ap, q="act", is_output=False):
        self.fw.dma(dst, i_ap, R=[it], q=q, is_output=is_output)

    def rsqrt_from(self, out, o_ap, src, s_ap, inv_n):
        self.act(out, o_ap, src, s_ap, AF.Ln, bias=self.epsT.ap[:, 0:1], scale=inv_n, R=[self.epsT])
        self.act(out, o_ap, out, o_ap, AF.Exp, scale=-0.5)

    def build(self):
        nc = self.nc
        L = self.nlayers
        xin = self.inp("xin", [NT, D])
        cT_d = self.inp("cT", [128, 16, 2])
        ada_w = self.inp("ada_w", [DEPTH, D, 3 * D])
        adabT = self.inp("adabT", [DEPTH, 128, 48])
        ngT = self.inp("ngT", [DEPTH, 128, 16])
        w_in = self.inp("w_in", [DEPTH, D, 3 * D])
        w_out = self.inp("w_out", [DEPTH, D, D])
        consts = self.inp("consts", [6, 128, 128])
        s5p = self.inp("s5p", [2, 2, 128, 3, 32])
        s5B = self.inp("s5B", [2, 2, 2, 32, 128, 128])
        s5C = self.inp("s5C", [2, 2, 2, 32, 128, 128])
        s5d = self.inp("s5d", [2, 128, 8])
        s5tau = self.inp("s5tau", [4, 128, TC])
        wglu = self.inp("wglu", [2, 1024, 1024])
        daq = self.inp("daq", [2, 128, 3])
        lamT = self.inp("lamT", [2, 64, 4])
        rope = self.inp("rope", [2, 128, 2048])
        wa1 = self.inp("wa1", [2, 2, D, 16])
        wa2 = self.inp("wa2", [2, 2, 16, 1024])
        baT = self.inp("baT", [2, 2, 128, 8])
        glag = self.inp("glag", [2, 128, 4])
        gmask = self.inp("gmask", [2, 128, NT])
        out_d = nc.dram_tensor("out", [2048, D], F32, kind="ExternalOutput").ap()

        self.XT = self.scratch("XT", [D, NT])
        self.PROJT = self.scratch("PROJT", [3 * D, NT])
        self.VTOK = self.scratch("VTOK", [NT, D])
        self.MRGT = self.scratch("MRGT", [D, NT])
        self.GT = self.scratch("GT", [1024, NT])
        self.LAT = self.scratch("LAT", [2, 1024, NT])

        self.cst = self.tile([128, 6, 128])
        self.ld(self.cst, self.cst.ap, consts.rearrange("c p n -> p c n"))
        self.ident = T(self.cst.ap[:, 0, :]); self.ident.b = self.cst.b
        self.onesr = self.tile([128, 128], F32R)
        self.blk64r = self.tile([128, 128], F32R)
        self.protr = self.tile([128, 128], F32R)
        self.cp("dve", self.onesr, self.onesr.ap, self.cst, self.cst.ap[:, 1, :])
        self.cp("dve", self.blk64r, self.blk64r.ap, self.cst, self.cst.ap[:, 2, :])
        self.cp("dve", self.protr, self.protr.ap, self.cst, self.cst.ap[:, 3, :])
        self.epsT = self.tile([128, 1])
        self.op("dve", lambda g: g.memset(self.epsT.ap, EPS), W=[self.epsT])
        self.modv = self.tile([128, 48, 2])
        self.gs = self.tile([128, 16, 2])
        self.cT = self.tile([128, 16, 2])
        self.sc = self.tile([128, 16, 2])
        self.ld(self.cT, self.cT.ap, cT_d)
        self.act(self.sc, self.sc.ap, self.cT, self.cT.ap, AF.Silu)
        self.base = self.off

        self.phase_in_transpose(xin)
        for l in range(L):
            j = l // 2
            self.phase_mod(l, ada_w, adabT, ngT)
            if l % 2 == 0:
                self.phase_inproj(l, w_in, tok_chunks=range(32, 40))
                if "stop_proj" in self.debug:
                    break
                self.phase_s5(j, s5p, s5B, s5C, s5d, s5tau, wglu)
                if "stop_s5" in self.debug:
                    break
                lam_init = 0.8 - 0.6 * math.exp(-0.3 * l)
                self.phase_da(j, daq, lamT, rope, lam_init)
                if "stop_da" in self.debug:
                    break
            else:
                self.phase_inproj(l, w_in, tok_chunks=range(16, 32), gla=(j, wa1, wa2, baT))
                self.phase_gla(j, glag, gmask)
                if "stop_gla" in self.debug:
                    break
            self.phase_outproj(l, w_out)
        self.phase_out_transpose(out_d)
        self.fw.finish()
        return nc

    def phase_in_transpose(self, xin):
        self.new_phase()
        xt = [self.tile([128, D]) for _ in range(2)]
        stg = [self.tile([128, 16, 128]) for _ in range(2)]
        XTv = self.XT.rearrange("(kc p) t -> p kc t", p=128)
        for tt in range(18):
            x = xt[tt % 2]
            s = stg[tt % 2]
            self.ld(x, x.ap, xin[tt * 128:(tt + 1) * 128, :])
            for g4 in range(4):
                ps = self.PS[g4 % 4]
                for i in range(4):
                    kc = g4 * 4 + i
                    self.tp(ps, ps.ap[:, i * 128:(i + 1) * 128], x, x.ap[:, kc * 128:(kc + 1) * 128], self.ident)
                e = "act" if g4 % 2 else "dve"
                self.cp(e, s, s.ap[:, g4 * 4:(g4 + 1) * 4, :], ps, ps.ap.rearrange("p (a b) -> p a b", b=128))
            self.st(XTv[:, :, tt * 128:(tt + 1) * 128], s, s.ap)

    def phase_out_transpose(self, out_d):
        self.new_phase()
        xt = [self.tile([128, 16, 128]) for _ in range(2)]
        stg = [self.tile([128, D]) for _ in range(2)]
        XTv = self.XT.rearrange("(kc p) t -> p kc t", p=128)
        for tt in range(16):
            x = xt[tt % 2]
            s = stg[tt % 2]
            t0 = NCTX + tt * 128
            self.ld(x, x.ap, XTv[:, :, t0:t0 + 128])
            for g4 in range(4):
                ps = self.PS[g4 % 4]
                for i in range(4):
                    kc = g4 * 4 + i
                    self.tp(ps, ps.ap[:, i * 128:(i + 1) * 128], x, x.ap[:, kc, :], self.ident)
                e = "act" if g4 % 2 else "dve"
                self.cp(e, s, s.ap[:, g4 * 512:(g4 + 1) * 512], ps, ps.ap)
            self.st(out_d[tt * 128:(tt + 1) * 128, :], s, s.ap, is_output=True)

    def phase_mod(self, l, ada_w, adabT, ngT):
        self.new_phase()
        wb = [self.tile([128, 16, 512]) for _ in range(2)]
        adab = self.tile([128, 48])
        ng = self.tile([128, 16])
        self.ld(adab, adab.ap, adabT[l])
        self.ld(ng, ng.ap, ngT[l])
        Wv = ada_w[l].rearrange("(kc p) n -> p kc n", p=128)
        pm = self.PS[0]
        for cc in range(12):
            w = wb[cc % 2]
            self.ld(w, w.ap, Wv[:, :, cc * 512:(cc + 1) * 512])
            for jj in range(4):
                j = cc * 4 + jj
                for kc in range(16):
                    self.mm(pm, pm.ap[:, 2 * j:2 * j + 2], w, w.ap[:, kc, jj * 128:(jj + 1) * 128],
                            self.sc, self.sc.ap[:, kc, :], start=(kc == 0), stop=(kc == 15))
        pm3 = pm.ap[:, 0:96].rearrange("p (j s) -> p j s", s=2)
        for s in range(2):
            self.tt("dve", self.modv, self.modv.ap[:, :, s], pm, pm3[:, :, s], adab, adab.ap, ALU.add)
        for s in range(2):
            self.stt("dve", self.gs, self.gs.ap[:, :, s], self.modv, self.modv.ap[:, 16:32, s], 1.0, ng, ng.ap, ALU.add, ALU.mult)

    def phase_inproj(self, l, w_in, tok_chunks=(), gla=None):
        self.new_phase()
        hR = self.tile([128, 16, NT], F32R)
        hb = [T(hR.ap) for _ in BLKS]
        self._wraw_B = self.tile([128, 16, 128])
        xc = [self.tile([128, 512]) for _ in range(3)]
        tm = [self.tile([128, 512]) for _ in range(2)]
        sq = [self.tile([128, 512], F32R) for _ in range(2)]
        rstd = self.tile([128, 512])
        XTv = self.XT.rearrange("(kc p) t -> p kc t", p=128)
        xi = 0
        for bi, (t0, n, s) in enumerate(BLKS):
            h = hb[bi]
            ps = self.PS[bi % 2]
            for kc in range(16):
                x = xc[xi % 3]; xi += 1
                self.ld(x, x.ap[:, 0:n], XTv[:, kc, t0:t0 + n])
                q = sq[kc % 2]
                self.act(q, q.ap[:, 0:n], x, x.ap[:, 0:n], AF.Square)
                self.mm(ps, ps.ap[:, 0:n], self.onesr, self.onesr.ap, q, q.ap[:, 0:n], start=(kc == 0), stop=(kc == 15))
            self.rsqrt_from(rstd, rstd.ap[:, 0:n], ps, ps.ap[:, 0:n], 1.0 / D)
            for kc in range(16):
                x = xc[xi % 3]; xi += 1
                self.ld(x, x.ap[:, 0:n], XTv[:, kc, t0:t0 + n])
                t = tm[kc % 2]
                self.tt("dve", t, t.ap[:, 0:n], x, x.ap[:, 0:n], rstd, rstd.ap[:, 0:n], ALU.mult)
                self.ts("dve", h, hR.ap[:, kc, t0:t0 + n], t, t.ap[:, 0:n],
                        self.gs.ap[:, kc, s:s + 1], self.modv.ap[:, kc, s:s + 1], ALU.mult, ALU.add, R=[self.gs, self.modv])
        if "hT" in self.debug:
            dbg = self.scratch("hT", [D, NT])
            dt_ = self._wraw_B
            for tt in range(18):
                self.cp("dve", dt_, dt_.ap, hb[0 if tt < 2 else 1 + (tt - 2) // 4], hR.ap[:, :, tt * 128:(tt + 1) * 128])
                self.fw.dma(dbg.rearrange("(kc p) t -> p kc t", p=128)[:, :, tt * 128:(tt + 1) * 128], dt_.ap, R=[dt_], q="act")
        wraw = self._wraw_B
        wr = [self.tile([128, 16, 128], F32R) for _ in range(2)]
        ostb = [self.tile([128, n]) for (t0, n, s) in BLKS]
        Wv = w_in[l].rearrange("(kc p) n -> p kc n", p=128)
        VT = self.VTOK.rearrange("(tt p) n -> p tt n", p=128)
        tok_chunks = set(tok_chunks)
        pi = 0
        for j in range(48):
            w = wr[j % 2]
            self.ld(wraw, wraw.ap, Wv[:, :, j * 128:(j + 1) * 128])
            self.cp("pool", w, w.ap, wraw, wraw.ap)
            if j in tok_chunks:
                jj = j - min(tok_chunks)
                for g in range(5):
                    ps = self.PS[2 + pi % 4]; pi += 1
                    tts = list(range(g * 4, min(g * 4 + 4, 18)))
                    o = ostb[g if len(tts) == 4 and g > 0 else (1 if g == 0 else 0)]
                    for ii, tt in enumerate(tts):
                        bi = 0 if tt < 2 else 1 + (tt - 2) // 4
                        for kc in range(16):
                            self.mm(ps, ps.ap[:, ii * 128:(ii + 1) * 128], hb[bi], hR.ap[:, kc, tt * 128:(tt + 1) * 128],
                                    w, w.ap[:, kc, :], start=(kc == 0), stop=(kc == 15))
                    nn = len(tts) * 128
                    self.cp("act" if g % 2 else "dve", o, o.ap[:, 0:nn], ps, ps.ap[:, 0:nn])
                    self.st(VT[:, tts[0]:tts[-1] + 1, jj * 128:(jj + 1) * 128], o, o.ap[:, 0:nn].rearrange("p (a b) -> p a b", b=128))
            else:
                for bi, (t0, n, s) in enumerate(BLKS):
                    ps = self.PS[2 + pi % 4]; pi += 1
                    o = ostb[bi]
                    for kc in range(16):
                        self.mm(ps, ps.ap[:, 0:n], w, w.ap[:, kc, :], hb[bi], hR.ap[:, kc, t0:t0 + n], start=(kc == 0), stop=(kc == 15))
                    self.cp("act" if bi % 2 else "dve", o, o.ap, ps, ps.ap[:, 0:n])
                    self.st(self.PROJT[j * 128:(j + 1) * 128, t0:t0 + n], o, o.ap)
        if gla is not None:
            self.gla_gate(gla, hb, hR, wraw, wr, ostb, sq, tm)

    def gla_gate(self, gla, hb, hR, wraw, wr, ostb, sq, tm):
        j, wa1, wa2, baT = gla
        nba = self.tile([128, 8])
        wflat = wraw.ap.rearrange("p a b -> p (a b)")
        w2 = T(wr[1].ap.rearrange("p a b -> p (a b)")[:, 0:1024]); w2.b = wr[1].b
        w1 = wr[0]
        lr = sq[0]
        k = 0
        for d in range(2):
            self.op("pool", lambda g: g.memset(wraw.ap, 0.0), W=[wraw])
            self.ld(wraw, wraw.ap[:, :, 0:16], wa1[j, d].rearrange("(kc p) r -> p kc r", p=128))
            self.cp("pool", w1, w1.ap, wraw, wraw.ap)
            self.op("pool", lambda g: g.memset(wraw.ap, 0.0), W=[wraw])
            self.ld(wraw, wflat[0:16, 0:1024], wa2[j, d])
            self.cp("pool", w2, w2.ap, wraw, wflat[:, 0:1024])
            self.ld(nba, nba.ap, baT[j, d])
            self.ts("dve", nba, nba.ap, nba, nba.ap, -1.0, None, ALU.mult)
            for bi, (t0, n, s) in enumerate(BLKS):
                ps = self.PS[bi % 2]
                for kc in range(16):
                    self.mm(ps, ps.ap[:, 0:n], w1, w1.ap[:, kc, :], hb[bi], hR.ap[:, kc, t0:t0 + n], start=(kc == 0), stop=(kc == 15))
                self.cp("dve", lr, lr.ap[:, 0:n], ps, ps.ap[:, 0:n])
                for c in range(8):
                    o = ostb[1 + k % 4]
                    t_ = tm[k % 2]
                    ps2 = self.PS[2 + k % 4]
                    k += 1
                    self.mm(ps2, ps2.ap[:, 0:n], w2, w2.ap[:, c * 128:(c + 1) * 128], lr, lr.ap[:, 0:n])
                    self.act(t_, t_.ap[:, 0:n], ps2, ps2.ap[:, 0:n], AF.Exp, bias=nba.ap[:, c:c + 1], scale=-1.0, R=[nba])
                    self.act(t_, t_.ap[:, 0:n], t_, t_.ap[:, 0:n], AF.Ln, bias=1.0)
                    self.ts("dve", o, o.ap[:, 0:n], t_, t_.ap[:, 0:n], -1.0 / 16.0, None, ALU.mult)
                    self.st(self.LAT[d, c * 128:(c + 1) * 128, t0:t0 + n], o, o.ap[:, 0:n])

    def phase_outproj(self, l, w_out):
        self.new_phase()
        mR = self.tile([128, 16, NT], F32R)
        mb = [T(mR.ap) for _ in BLKS]
        xc = [self.tile([128, 512]) for _ in range(3)]
        Mv = self.MRGT.rearrange("(kc p) t -> p kc t", p=128)
        xi = 0
        for bi, (t0, n, s) in enumerate(BLKS):
            for kc in range(16):
                x = xc[xi % 3]; xi += 1
                self.ld(x, x.ap[:, 0:n], Mv[:, kc, t0:t0 + n])
                self.cp(self.alt(), mb[bi], mR.ap[:, kc, t0:t0 + n], x, x.ap[:, 0:n])
        wraw = self.tile([128, 16, 128])
        wr = [self.tile([128, 16, 128], F32R) for _ in range(2)]
        xo = [self.tile([128, n]) for (t0, n, s) in BLKS]
        Wv = w_out[l].rearrange("(kc p) n -> p kc n", p=128)
        pi = 0
        for j in range(16):
            w = wr[j % 2]
            self.ld(wraw, wraw.ap, Wv[:, :, j * 128:(j + 1) * 128])
            self.cp("pool", w, w.ap, wraw, wraw.ap)
            for bi, (t0, n, s) in enumerate(BLKS):
                x = xo[bi]
                self.ld(x, x.ap, self.XT[j * 128:(j + 1) * 128, t0:t0 + n])
                ps = self.PS[pi % 4]; pi += 1
                for kc in range(16):
                    self.mm(ps, ps.ap[:, 0:n], w, w.ap[:, kc, :], mb[bi], mR.ap[:, kc, t0:t0 + n], start=(kc == 0), stop=(kc == 15))
                self.stt("dve", x, x.ap, ps, ps.ap[:, 0:n], self.modv.ap[:, 32 + j, s:s + 1], x, x.ap,
                         ALU.mult, ALU.add, R=[self.modv])
                self.st(self.XT[j * 128:(j + 1) * 128, t0:t0 + n], x, x.ap)

    def sincos(self, ang, n, sin_out=None, cos_out=None, scr=None):
        u, ki, rd = scr
        for out, shift in ((sin_out, 0.0), (cos_out, PI / 2)):
            if out is None:
                continue
            self.ts("dve", u, u.ap[:, 0:n], ang, ang.ap[:, 0:n], 1.0 / (2 * PI), shift / (2 * PI), ALU.mult, ALU.add)
            self.cp("dve", ki, ki.ap[:, 0:n], u, u.ap[:, 0:n])
            self.cp("dve", u, u.ap[:, 0:n], ki, ki.ap[:, 0:n])
            self.stt("dve", rd, rd.ap[:, 0:n], u, u.ap[:, 0:n], -2 * PI, ang, ang.ap[:, 0:n], ALU.mult, ALU.add)
            self.ts("dve", rd, rd.ap[:, 0:n], rd, rd.ap[:, 0:n], shift, 3.1415925, ALU.add, ALU.min)
            self.ts("dve", rd, rd.ap[:, 0:n], rd, rd.ap[:, 0:n], -3.1415925, None, ALU.max)
            self.act(out, out.ap[:, 0:n], rd, rd.ap[:, 0:n], AF.Sin)

    def phase_s5(self, j, s5p, s5B, s5C, s5d, s5tau, wglu):
        self.new_phase()
        W = 8 * TC
        tau = self.tile([128, 4, TC])
        self.ld(tau, tau.ap, s5tau.rearrange("k p t -> p k t"))
        dsk = self.tile([128, 8])
        self.ld(dsk, dsk.ap, s5d[j])
        t1, t2, t3, t4 = [self.tile([128, W]) for _ in range(4)]
        ki = self.tile([128, W], I32)
        scr = (t1, ki, t2)
        ang = t3
        PR = []
        for d in range(2):
            prm = self.tile([128, 3, 32])
            self.ld(prm, prm.ap, s5p[j, d])
            names = "dt r th sn cs fre fim nfre c1 s1 t1 t2 den".split()
            c = {k: self.tile([128, 32]) for k in names}
            are, aim, ldt = prm.ap[:, 0, :], prm.ap[:, 1, :], prm.ap[:, 2, :]
            self.act(c["dt"], c["dt"].ap, prm, ldt, AF.Exp)
            self.tt("dve", c["t1"], c["t1"].ap, prm, are, c["dt"], c["dt"].ap, ALU.mult)
            self.act(c["r"], c["r"].ap, c["t1"], c["t1"].ap, AF.Exp)
            self.tt("dve", c["th"], c["th"].ap, prm, aim, c["dt"], c["dt"].ap, ALU.mult)
            self.sincos(c["th"], 32, c["sn"], c["cs"], scr)
            self.tt("dve", c["t1"], c["t1"].ap, c["r"], c["r"].ap, c["cs"], c["cs"].ap, ALU.mult)
            self.ts("dve", c["t1"], c["t1"].ap, c["t1"], c["t1"].ap, -1.0, None, ALU.add)
            self.tt("dve", c["t2"], c["t2"].ap, c["r"], c["r"].ap, c["sn"], c["sn"].ap, ALU.mult)
            self.tt("dve", c["den"], c["den"].ap, prm, are, prm, are, ALU.mult)
            self.tt("dve", c["fre"], c["fre"].ap, prm, aim, prm, aim, ALU.mult)
            self.tt("dve", c["den"], c["den"].ap, c["den"], c["den"].ap, c["fre"], c["fre"].ap, ALU.add)
            self.op("dve", lambda g, t=c["den"]: g.reciprocal(t.ap, t.ap), [c["den"]], [c["den"]])
            self.tt("dve", c["fre"], c["fre"].ap, c["t1"], c["t1"].ap, prm, are, ALU.mult)
            self.tt("dve", c["fim"], c["fim"].ap, c["t2"], c["t2"].ap, prm, aim, ALU.mult)
            self.tt("dve", c["fre"], c["fre"].ap, c["fre"], c["fre"].ap, c["fim"], c["fim"].ap, ALU.add)
            self.tt("dve", c["fre"], c["fre"].ap, c["fre"], c["fre"].ap, c["den"], c["den"].ap, ALU.mult)
            self.tt("dve", c["fim"], c["fim"].ap, c["t2"], c["t2"].ap, prm, are, ALU.mult)
            self.tt("dve", c["nfre"], c["nfre"].ap, c["t1"], c["t1"].ap, prm, aim, ALU.mult)
            self.tt("dve", c["fim"], c["fim"].ap, c["fim"], c["fim"].ap, c["nfre"], c["nfre"].ap, ALU.subtract)
            self.tt("dve", c["fim"], c["fim"].ap, c["fim"], c["fim"].ap, c["den"], c["den"].ap, ALU.mult)
            self.ts("dve", c["nfre"], c["nfre"].ap, c["fre"], c["fre"].ap, -1.0, None, ALU.mult)
            self.ts("dve", c["t1"], c["t1"].ap, c["th"], c["th"].ap, float(TC), None, ALU.mult)
            self.sincos(c["t1"], 32, c["s1"], c["c1"], scr)
            self.tt("dve", c["c1"], c["c1"].ap, c["c1"], c["c1"].ap, c["r"], c["r"].ap, ALU.mult)
            self.tt("dve", c["s1"], c["s1"].ap, c["s1"], c["s1"].ap, c["r"], c["r"].ap, ALU.mult)
            PR.append(c)
        sn, cs, nsn, wr, wi, Rm = [self.tile([128, W]) for _ in range(6)]
        bre, bim = self.tile([128, W]), self.tile([128, W])
        ncs = self.tile([128, W])
        Xre = [self.tile([128, W]) for _ in range(2)]
        Xim = [self.tile([128, W]) for _ in range(2)]
        gre = [self.tile([128, W]) for _ in range(2)]
        gim = [self.tile([128, W]) for _ in range(2)]
        uu = [[self.tile([128, W], F32R) for _ in range(4)] for _ in range(2)]
        cr = [self.tile([128, 8]) for _ in range(4)]
        ustg = self.tile([128, NT])
        ur = self.tile([128, 2, NT], F32R)
        yt = self.tile([128, 2, NT])
        BCraw = self.tile([128, 16, 128])
        Br = self.tile([128, 16, 128], F32R)
        Cr = self.tile([128, 16, 128], F32R)
        PRb, PIb = [self.PS[0], self.PS[1]], [self.PS[2], self.PS[3]]
        PYs = [self.PS[4], self.PS[5]]
        pr_ap, pi_ap = self.psum[:, 0:W], self.psum[:, 2 * 512:2 * 512 + W]
        fseq = list(range(18))
        bseq = [1, 0] + list(range(17, 1, -1))
        v3 = lambda t, k: t.ap.rearrange("p (s t) -> p s t", t=TC)[:, :, k]
        for sg in range(4):
            for o in range(2):
                oc = 2 * sg + o
                self.ld(ustg, ustg.ap, self.PROJT[oc * 128:(oc + 1) * 128, :])
                self.cp("dve", ur, ur.ap[:, o, :], ustg, ustg.ap)
            for d in range(2):
                c = PR[d]
                for src, dstR in ((s5B, Br), (s5C, Cr)):
                    for ri in range(2):
                        self.ld(BCraw, BCraw.ap[:, ri::2, :], src[j, d, ri, sg * 8:(sg + 1) * 8].rearrange("s k m -> k s m"))
                    self.cp("pool", dstR, dstR.ap, BCraw, BCraw.ap)
                tauD = tau.ap[:, d, :]
                maskD = tau.ap[:, 2 + d, :]
                for i in range(8):
                    st = sg * 8 + i
                    sl = slice(i * TC, (i + 1) * TC)
                    self.ts("dve", ang, ang.ap[:, sl], tau, tauD, c["th"].ap[:, st:st + 1], None, ALU.mult, R=[c["th"]])
                    self.act(Rm, Rm.ap[:, sl], tau, maskD, AF.Copy, scale=c["r"].ap[:, st:st + 1], R=[c["r"]])
                self.sincos(ang, W, sn, cs, scr)
                self.act(nsn, nsn.ap, sn, sn.ap, AF.Copy, scale=-1.0)
                self.act(ncs, ncs.ap, cs, cs.ap, AF.Copy, scale=-1.0)
                for i in range(8):
                    st = sg * 8 + i
                    sl = slice(i * TC, (i + 1) * TC)
                    self.act(wr, wr.ap[:, sl], cs, cs.ap[:, sl], AF.Copy, scale=c["fre"].ap[:, st:st + 1], R=[c["fre"]])
                    self.act(wi, wi.ap[:, sl], cs, cs.ap[:, sl], AF.Copy, scale=c["fim"].ap[:, st:st + 1], R=[c["fim"]])
                    self.stt("dve", wr, wr.ap[:, sl], sn, sn.ap[:, sl], c["fim"].ap[:, st:st + 1], wr, wr.ap[:, sl], ALU.mult, ALU.add, R=[c["fim"]])
                    self.stt("dve", wi, wi.ap[:, sl], sn, sn.ap[:, sl], c["nfre"].ap[:, st:st + 1], wi, wi.ap[:, sl], ALU.mult, ALU.add, R=[c["nfre"]])
                first = 0 if d == 0 else TC - 1
                last = TC - 1 if d == 0 else 0
                seq = fseq if d == 0 else bseq
                c1 = c["c1"].ap[:, sg * 8:(sg + 1) * 8]
                s1 = c["s1"].ap[:, sg * 8:(sg + 1) * 8]

                def s1_pe(qi):
                    t0 = seq[qi] * TC
                    for i in range(8):
                        for ri, (pb, pap) in enumerate(((PRb, pr_ap), (PIb, pi_ap))):
                            self.mm(pb[i // 4], pap[:, i * TC:(i + 1) * TC], Br, Br.ap[:, 2 * i + ri, :], ur, ur.ap[:, i // 4, t0:t0 + TC])
                    self.fw.op("act", lambda g: g.copy(bre.ap, pr_ap), PRb, [bre])
                    self.fw.op("act", lambda g: g.copy(bim.ap, pi_ap), PIb, [bim])

                def s1_dve(qi):
                    xr, xi_ = Xre[qi % 2], Xim[qi % 2]
                    self.tt("dve", t1, t1.ap, wr, wr.ap, bre, bre.ap, ALU.mult)
                    self.tt("dve", t2, t2.ap, wi, wi.ap, bim, bim.ap, ALU.mult)
                    self.tt("dve", t3, t3.ap, wr, wr.ap, bim, bim.ap, ALU.mult)
                    self.tt("dve", t4, t4.ap, wi, wi.ap, bre, bre.ap, ALU.mult)
                    self.tt("dve", xr, xr.ap, t1, t1.ap, t2, t2.ap, ALU.subtract)
                    self.tt("dve", xi_, xi_.ap, t3, t3.ap, t4, t4.ap, ALU.add)

                def s2_dve(qi):
                    xr, xi_ = Xre[qi % 2], Xim[qi % 2]
                    g_re, g_im = gre[qi % 2], gim[qi % 2]
                    p_re, p_im = gre[(qi + 1) % 2], gim[(qi + 1) % 2]
                    u1, u2, u3, u4 = uu[qi % 2]
                    if qi > 0:
                        a, b_, e_, f_ = cr
                        self.tt("dve", a, a.ap, p_re, v3(p_re, last), c["c1"], c1, ALU.mult)
                        self.tt("dve", b_, b_.ap, p_im, v3(p_im, last), c["s1"], s1, ALU.mult)
                        self.tt("dve", e_, e_.ap, p_re, v3(p_re, last), c["s1"], s1, ALU.mult)
                        self.tt("dve", f_, f_.ap, p_im, v3(p_im, last), c["c1"], c1, ALU.mult)
                        self.tt("dve", a, a.ap, a, a.ap, b_, b_.ap, ALU.subtract)
                        self.tt("dve", e_, e_.ap, e_, e_.ap, f_, f_.ap, ALU.add)
                        self.tt("dve", xr, v3(xr, first), xr, v3(xr, first), a, a.ap, ALU.add)
                        self.tt("dve", xi_, v3(xi_, first), xi_, v3(xi_, first), e_, e_.ap, ALU.add)
                    rv = (lambda ap: ap) if d == 0 else (lambda ap: ap[:, ::-1])
                    self.op("dve", lambda g: g.tensor_tensor_scan(rv(g_re.ap), rv(Rm.ap), rv(xr.ap), 0.0, ALU.mult, ALU.add), [Rm, xr], [g_re])
                    self.op("dve", lambda g: g.tensor_tensor_scan(rv(g_im.ap), rv(Rm.ap), rv(xi_.ap), 0.0, ALU.mult, ALU.add), [Rm, xi_], [g_im])
                    self.tt("dve", u1, u1.ap, cs, cs.ap, g_re, g_re.ap, ALU.mult)
                    self.tt("dve", u2, u2.ap, nsn, nsn.ap, g_im, g_im.ap, ALU.mult)
                    self.tt("dve", u3, u3.ap, nsn, nsn.ap, g_re, g_re.ap, ALU.mult)
                    self.tt("dve", u4, u4.ap, ncs, ncs.ap, g_im, g_im.ap, ALU.mult)

                def s2_pe(qi):
                    t0 = seq[qi] * TC
                    u1, u2, u3, u4 = uu[qi % 2]
                    PY = PYs[qi % 2]
                    for o in range(2):
                        k = 0
                        for i in range(o * 4, o * 4 + 4):
                            for ri, ht in ((0, u1), (0, u2), (1, u3), (1, u4)):
                                self.mm(PY, PY.ap[:, o * TC:(o + 1) * TC], Cr, Cr.ap[:, 2 * i + ri, :], ht, ht.ap[:, i * TC:(i + 1) * TC],
                                        start=(k == 0), stop=(k == 15))
                                k += 1
                    py3 = PY.ap[:, 0:2 * TC].rearrange("p (o t) -> p o t", t=TC)
                    if d == 0:
                        self.cp("act", yt, yt.ap[:, :, t0:t0 + TC], PY, py3)
                    else:
                        self.tt("dve", yt, yt.ap[:, :, t0:t0 + TC], yt, yt.ap[:, :, t0:t0 + TC], PY, py3, ALU.add)

                s1_pe(0)
                s1_dve(0)
                for qi in range(18):
                    if qi + 1 < 18:
                        s1_pe(qi + 1)
                    s2_dve(qi)
                    s2_pe(qi)
                    if qi + 1 < 18:
                        s1_dve(qi + 1)
            for o in range(2):
                oc = 2 * sg + o
                self.ld(ustg, ustg.ap, self.PROJT[oc * 128:(oc + 1) * 128, :])
                self.stt("dve", yt, yt.ap[:, o, :], ustg, ustg.ap, dsk.ap[:, oc:oc + 1], yt, yt.ap[:, o, :], ALU.mult, ALU.add, R=[dsk])
                self.act(ustg, ustg.ap, yt, yt.ap[:, o, :], AF.Gelu_apprx_tanh)
                self.st(self.GT[oc * 128:(oc + 1) * 128, :], ustg, ustg.ap)
        self.new_phase()
        gR = self.tile([128, 8, NT], F32R)
        wgR = self.tile([128, 8, 1024], F32R)
        stg = [self.tile([128, 1024]) for _ in range(2)]
        Gv = self.GT.rearrange("(kc p) t -> p kc t", p=128)
        Wg = wglu[j].rearrange("(kc p) n -> p kc n", p=128)
        for kc in range(8):
            x = stg[kc % 2]
            self.ld(x, x.ap, Wg[:, kc, :])
            self.cp(self.alt(), wgR, wgR.ap[:, kc, :], x, x.ap)
        k = 0
        for kc in range(8):
            for (t0, n, s) in BLKS:
                x = stg[k % 2]; k += 1
                self.ld(x, x.ap[:, 0:n], Gv[:, kc, t0:t0 + n])
                self.cp(self.alt(), gR, gR.ap[:, kc, t0:t0 + n], x, x.ap[:, 0:n])
        gF = [self.tile([128, 512]) for _ in range(2)]
        zs = [self.tile([128, 512]) for _ in range(2)]
        tg = [self.tile([128, 512]) for _ in range(2)]
        ob = [self.tile([128, 512]) for _ in range(2)]
        k = 0
        for ncz in range(8):
            for (t0, n, s) in BLKS:
                ps = self.PS[k % 4]
                g_, z_, t_, o_ = gF[k % 2], zs[k % 2], tg[k % 2], ob[k % 2]
                k += 1
                self.ld(g_, g_.ap[:, 0:n], self.GT[ncz * 128:(ncz + 1) * 128, t0:t0 + n])
                self.ld(z_, z_.ap[:, 0:n], self.PROJT[1024 + ncz * 128:1024 + (ncz + 1) * 128, t0:t0 + n])
                for kc in range(8):
                    self.mm(ps, ps.ap[:, 0:n], wgR, wgR.ap[:, kc, ncz * 128:(ncz + 1) * 128], gR, gR.ap[:, kc, t0:t0 + n], start=(kc == 0), stop=(kc == 7))
                self.act(t_, t_.ap[:, 0:n], ps, ps.ap[:, 0:n], AF.Sigmoid)
                self.tt("dve", t_, t_.ap[:, 0:n], t_, t_.ap[:, 0:n], g_, g_.ap[:, 0:n], ALU.mult)
                self.act(z_, z_.ap[:, 0:n], z_, z_.ap[:, 0:n], AF.Silu)
                self.tt("dve", o_, o_.ap[:, 0:n], t_, t_.ap[:, 0:n], z_, z_.ap[:, 0:n], ALU.mult)
                self.st(self.MRGT[ncz * 128:(ncz + 1) * 128, t0:t0 + n], o_, o_.ap[:, 0:n])

    def phase_da(self, j, daq, lamT, rope, lam_init):
        self.new_phase()
        dq = self.tile([128, 3])
        self.ld(dq, dq.ap, daq[j])
        sgc = self.tile([128, 1])
        self.ts("dve", sgc, sgc.ap, dq, dq.ap[:, 2:3], 1.0 - lam_init, None, ALU.mult)
        lt = self.tile([64, 4])
        self.ld(lt, lt.ap, lamT[j])
        pr = self.tile([64, 2])
        self.tt("dve", pr, pr.ap[:, 0:1], lt, lt.ap[:, 0:1], lt, lt.ap[:, 1:2], ALU.mult)
        self.tt("dve", pr, pr.ap[:, 1:2], lt, lt.ap[:, 2:3], lt, lt.ap[:, 3:4], ALU.mult)
        p6, p7 = self.PS[6], self.PS[7]
        self.mm(p6, p6.ap[:, 0:2], self.cst, self.cst.ap[0:64, 1, :], pr, pr.ap)
        le = self.tile([128, 2])
        self.act(le, le.ap, p6, p6.ap[:, 0:2], AF.Exp)
        nlam = self.tile([128, 1])
        self.tt("dve", nlam, nlam.ap, le, le.ap[:, 1:2], le, le.ap[:, 0:1], ALU.subtract)
        self.ts("dve", nlam, nlam.ap, nlam, nlam.ap, -lam_init, None, ALU.add)
        cosT = self.tile([128, 2048]); sinT = self.tile([128, 2048])
        self.ld(cosT, cosT.ap, rope[0]); self.ld(sinT, sinT.ap, rope[1])
        q0 = [self.tile([128, NT], F32R) for _ in range(2)]
        q1 = [self.tile([128, NT], F32R) for _ in range(2)]
        kr = [self.tile([128, NT], F32R) for _ in range(2)]
        vr = [self.tile([128, 18, 128], F32R) for _ in range(2)]
        zd = [self.tile([128, NT]) for _ in range(2)]
        mo = [self.tile([128, NT]) for _ in range(2)]
        for q in q0 + q1:
            self.ts("dve", q, q.ap[:, 0:2048], cosT, cosT.ap, 0.0, None, ALU.mult)
            self.ts("dve", q, q.ap[:, 2048:NT], cosT, cosT.ap[:, 0:NT - 2048], 0.0, None, ALU.mult)
        qraw = self.tile([128, NT]); kraw = self.tile([128, NT])
        vraw = self.tile([128, 18, 128])
        sq = [self.tile([128, 512], F32R) for _ in range(2)]
        rstd = [self.tile([128, 512]) for _ in range(2)]
        qg = [self.tile([128, 512], F32R) for _ in range(2)]
        t1 = [self.tile([128, 512]) for _ in range(2)]
        t2 = [self.tile([128, 512]) for _ in range(2)]
        pT = [self.tile([128, 512], F32R) for _ in range(3)]
        rr = [self.tile([128, 512]) for _ in range(2)]
        am = [[self.tile([128, 512]) for _ in range(2)] for _ in range(2)]
        rs2 = self.tile([128, 512])
        VT = self.VTOK.rearrange("(tt p) n -> p tt n", p=128)
        qblks = [(0, 256, [0, 1])] + [(256 + 512 * i, 512, list(range(18))) for i in range(4)]
        st_ = {"cnt": 0, "u": 0, "g": 0}

        ots = [self.tile([128, 512]) for _ in range(2)]

        def load_head(hd):
            b_ = hd % 2
            self.ld(qraw, qraw.ap, self.PROJT[2048 + hd * 128:2048 + (hd + 1) * 128, :])
            self.ld(kraw, kraw.ap, self.PROJT[3072 + hd * 128:3072 + (hd + 1) * 128, :])
            self.ld(zd[b_], zd[b_].ap, self.PROJT[5120 + hd * 128:5120 + (hd + 1) * 128, :])
            self.ld(vraw, vraw.ap, VT[:, :, hd * 128:(hd + 1) * 128])
            self.cp("pool", vr[b_], vr[b_].ap, vraw, vraw.ap)

        def silu_head(hd):
            b_ = hd % 2
            self.act(zd[b_], zd[b_].ap, zd[b_], zd[b_].ap, AF.Silu)

        def prep_unit(hd, which, bi):
            b_ = hd % 2
            raw = qraw if which == 0 else kraw
            t0, n, s = BLKS[bi]
            u = st_["u"]; st_["u"] += 1
            q_, rs_, qg_, t1_, t2_ = sq[u % 2], rstd[u % 2], qg[u % 2], t1[u % 2], t2[u % 2]
            self.tt("dve", q_, q_.ap[:, 0:n], raw, raw.ap[:, t0:t0 + n], raw, raw.ap[:, t0:t0 + n], ALU.mult)
            self.mm(p6, p6.ap[:, 0:n], self.blk64r, self.blk64r.ap, q_, q_.ap[:, 0:n])
            self.rsqrt_from(rs_, rs_.ap[:, 0:n], p6, p6.ap[:, 0:n], 1.0 / 64)
            self.stt("dve", qg_, qg_.ap[:, 0:n], raw, raw.ap[:, t0:t0 + n], dq.ap[:, which:which + 1], rs_, rs_.ap[:, 0:n],
                     ALU.mult, ALU.mult, R=[dq])
            qgf = qg_.ap.bitcast(F32)
            if s == 0:
                r0_ = t0 - NCTX
                self.mm(p7, p7.ap[:, 0:n], self.protr, self.protr.ap, qg_, qg_.ap[:, 0:n])
                self.tt("dve", t1_, t1_.ap[:, 0:n], qg_, qgf[:, 0:n], cosT, cosT.ap[:, r0_:r0_ + n], ALU.mult)
                self.tt("dve", t2_, t2_.ap[:, 0:n], p7, p7.ap[:, 0:n], sinT, sinT.ap[:, r0_:r0_ + n], ALU.mult)
                if which == 0:
                    self.tt("dve", q0[b_], q0[b_].ap[0:64, t0:t0 + n], t1_, t1_.ap[0:64, 0:n], t2_, t2_.ap[0:64, 0:n], ALU.add)
                    self.tt("dve", q1[b_], q1[b_].ap[64:128, t0:t0 + n], t1_, t1_.ap[64:128, 0:n], t2_, t2_.ap[64:128, 0:n], ALU.add)
                else:
                    self.tt("dve", kr[b_], kr[b_].ap[:, t0:t0 + n], t1_, t1_.ap[:, 0:n], t2_, t2_.ap[:, 0:n], ALU.add)
            else:
                if which == 0:
                    self.cp("dve", q0[b_], q0[b_].ap[0:64, t0:t0 + n], qg_, qgf[0:64, 0:n])
                    self.cp("dve", q1[b_], q1[b_].ap[64:128, t0:t0 + n], qg_, qgf[64:128, 0:n])
                else:
                    self.cp("dve", kr[b_], kr[b_].ap[:, t0:t0 + n], qg_, qgf[:, 0:n])

        def make_tail(hd, t0, n, a0, a1, qbi, last):
            b_ = hd % 2
            ot = ots[qbi % 2]

            def tail():
                self.stt("dve", ot, ot.ap[:, 0:n], a1, a1.ap[:, 0:n], nlam.ap[:, 0:1], a0, a0.ap[:, 0:n], ALU.mult, ALU.add, R=[nlam])
                q_ = sq[0]
                self.tt("dve", q_, q_.ap[:, 0:n], ot, ot.ap[:, 0:n], ot, ot.ap[:, 0:n], ALU.mult)
                self.mm(p6, p6.ap[:, 0:n], self.onesr, self.onesr.ap, q_, q_.ap[:, 0:n])
                self.rsqrt_from(rs2, rs2.ap[:, 0:n], p6, p6.ap[:, 0:n], 1.0 / 128)
                self.stt("dve", ot, ot.ap[:, 0:n], ot, ot.ap[:, 0:n], sgc.ap[:, 0:1], rs2, rs2.ap[:, 0:n], ALU.mult, ALU.mult, R=[sgc])
                self.tt("dve", mo[b_], mo[b_].ap[:, t0:t0 + n], ot, ot.ap[:, 0:n], zd[b_], zd[b_].ap[:, t0:t0 + n], ALU.mult)
                if last:
                    self.st(self.MRGT[1024 + hd * 128:1024 + (hd + 1) * 128, :], mo[b_], mo[b_].ap)
            return tail

        units = [(w, bi) for w in range(2) for bi in range(5)]
        load_head(0)
        silu_head(0)
        for w, bi in units:
            prep_unit(0, w, bi)
        pending = None
        for hd in range(8):
            b_ = hd % 2
            if hd + 1 < 8:
                load_head(hd + 1)
            ui = 0
            for qbi, (t0, n, keys) in enumerate(qblks):
                for m in range(2):
                    qm = q0[b_] if m == 0 else q1[b_]
                    g = st_["g"]; st_["g"] += 1
                    Om, RSm = self.PS[2 + 2 * (g % 2)], self.PS[3 + 2 * (g % 2)]
                    nk = len(keys)
                    base = st_["cnt"]; st_["cnt"] += nk

                    def smm(ki):
                        sp = self.PS[(base + ki) % 2]
                        kt = keys[ki]
                        self.mm(sp, sp.ap[:, 0:n], kr[b_], kr[b_].ap[:, kt * 128:(kt + 1) * 128], qm, qm.ap[:, t0:t0 + n])
                    smm(0)
                    for ki, kt in enumerate(keys):
                        sp = self.PS[(base + ki) % 2]
                        p_ = pT[(base + ki) % 3]
                        if ki + 1 < nk:
                            smm(ki + 1)
                        self.act(p_, p_.ap[:, 0:n], sp, sp.ap[:, 0:n], AF.Exp, scale=0.125)
                        self.mm(Om, Om.ap[:, 0:n], vr[b_], vr[b_].ap[:, kt, :], p_, p_.ap[:, 0:n], start=(ki == 0), stop=(ki == nk - 1))
                        self.mm(RSm, RSm.ap[:, 0:n], self.onesr, self.onesr.ap, p_, p_.ap[:, 0:n], start=(ki == 0), stop=(ki == nk - 1))
                    r_, a_ = rr[m], am[(g // 2) % 2][m]
                    self.op("dve", lambda g_, r_=r_, RSm=RSm, n=n: g_.reciprocal(r_.ap[:, 0:n], RSm.ap[:, 0:n]), [RSm], [r_])
                    self.tt("dve", a_, a_.ap[:, 0:n], Om, Om.ap[:, 0:n], r_, r_.ap[:, 0:n], ALU.mult)
                    if m == 0 and pending is not None:
                        pending(); pending = None
                    if hd + 1 < 8 and ui < len(units):
                        if ui == 1:
                            silu_head(hd + 1)
                        prep_unit(hd + 1, *units[ui]); ui += 1
                a0, a1 = am[((st_["g"] - 1) // 2) % 2]
                pending = make_tail(hd, t0, n, a0, a1, qbi, qbi == len(qblks) - 1)
            if pending is not None:
                pending(); pending = None
            while hd + 1 < 8 and ui < len(units):
                prep_unit(hd + 1, *units[ui]); ui += 1

    def phase_gla(self, j, glag, gmask):
        self.new_phase()
        CH = 128
        NCH = NT // CH
        scale = 256.0 ** -0.5
        gg = self.tile([128, 4])
        self.ld(gg, gg.ap, glag[j])
        gm = [self.tile([128, NT]) for _ in range(2)]
        for d in range(2):
            self.ld(gm[d], gm[d].ap, gmask[d])
        qraw = self.tile([128, 2, NT]); kraw = self.tile([128, 2, NT])
        OT = self.tile([128, 4, NT])
        la = self.tile([128, NT]); bt = self.tile([128, NT])
        qd = self.tile([128, 2, NT], F32R); kd = self.tile([128, 2, NT], F32R)
        ebend = self.tile([128, 2, NCH])
        S = self.tile([128, 2, 512]); Sr = self.tile([128, 2, 512], F32R)
        vraw = [self.tile([CH, 512]) for _ in range(2)]
        vr = [self.tile([CH, 512], F32R) for _ in range(2)]
        scTs = [self.tile([CH, CH], F32R) for _ in range(3)]
        kcts = [self.tile([128, 2, CH]) for _ in range(3)]
        kendTs = [self.tile([CH, 256], F32R) for _ in range(3)]
        sq = [self.tile([128, 512], F32R) for _ in range(2)]
        rstd = self.tile([128, 512])
        zt = [self.tile([128, 512]) for _ in range(2)]
        yt = [self.tile([128, 512]) for _ in range(2)]
        ob = [self.tile([128, 512]) for _ in range(2)]
        psSs, psT, psO, psU, psN = [self.PS[0], self.PS[7]], self.PS[1], [self.PS[2], self.PS[3]], [self.PS[4], self.PS[5]], self.PS[6]
        nctx = NCTX // CH
        fseq = list(range(NCH))
        bseq = list(range(nctx - 1, -1, -1)) + list(range(NCH - 1, nctx - 1, -1))
        vi = 0
        for hd in range(4):
            self.ld(qraw, qraw.ap, self.PROJT[hd * 256:(hd + 1) * 256, :].rearrange("(c p) t -> p c t", p=128))
            self.ld(kraw, kraw.ap, self.PROJT[1024 + hd * 256:1024 + (hd + 1) * 256, :].rearrange("(c p) t -> p c t", p=128))
            for d in range(2):
                last = CH - 1 if d == 0 else 0
                for dkc in range(2):
                    r0 = hd * 256 + dkc * 128
                    self.ld(la, la.ap, self.LAT[d, r0:r0 + 128, :])
                    if d == 0:
                        self.op("dve", lambda g: g.tensor_tensor_scan(bt.ap, gm[0].ap, la.ap, 0.0, ALU.mult, ALU.add), [gm[0], la], [bt])
                    else:
                        self.op("dve", lambda g: g.tensor_tensor_scan(bt.ap[:, ::-1], gm[1].ap[:, ::-1], la.ap[:, ::-1], 0.0, ALU.mult, ALU.add), [gm[1], la], [bt])
                    self.act(la, la.ap, bt, bt.ap, AF.Exp)
                    self.stt("dve", qd, qd.ap[:, dkc, :], qraw, qraw.ap[:, dkc, :], scale, la, la.ap, ALU.mult, ALU.mult)
                    self.cp("dve", ebend, ebend.ap[:, dkc, :], la, la.ap[:, last::CH])
                    self.act(bt, bt.ap, bt, bt.ap, AF.Exp, scale=-1.0)
                    self.tt("dve", kd, kd.ap[:, dkc, :], kraw, kraw.ap[:, dkc, :], bt, bt.ap, ALU.mult)
                self.op("pool", lambda g: g.memset(S.ap, 0.0), W=[S])
                self.ts("dve", Sr, Sr.ap, S, S.ap, 0.0, None, ALU.mult)
                tri = self.cst.ap[0:CH, 4 + d, 0:CH]
                kdf, qdf = kd.ap.bitcast(F32), qd.ap.bitcast(F32)
                for qi, c in enumerate(fseq if d == 0 else bseq):
                    t0 = c * CH
                    va, v_ = vraw[vi % 2], vr[vi % 2]
                    vi += 1
                    self.ld(va, va.ap, self.VTOK[t0:t0 + CH, hd * 512:(hd + 1) * 512])
                    self.cp("act", v_, v_.ap, va, va.ap)
                    scT, kct, kendT, psS = scTs[qi % 3], kcts[qi % 3], kendTs[qi % 3], psSs[qi % 2]
                    tof = (qi % 2) * 256
                    for dkc in range(2):
                        self.mm(psS, psS.ap[0:CH, 0:CH], kd, kd.ap[:, dkc, t0:t0 + CH], qd, qd.ap[:, dkc, t0:t0 + CH], start=(dkc == 0), stop=(dkc == 1))
                    self.tt("dve", scT, scT.ap, psS, psS.ap[0:CH, 0:CH], self.cst, tri, ALU.mult)
                    for dkc in range(2):
                        self.act(kct, kct.ap[:, dkc, :], kd, kdf[:, dkc, t0:t0 + CH], AF.Copy, scale=ebend.ap[:, dkc, c:c + 1], R=[ebend])
                        self.tp(psT, psT.ap[0:CH, tof + dkc * 128:tof + (dkc + 1) * 128], kct, kct.ap[:, dkc, :], self.ident)
                    self.cp("act", kendT, kendT.ap, psT, psT.ap[0:CH, tof:tof + 256])
                    po = psO[qi % 2]
                    for dvc in range(4):
                        oap = po.ap[:, dvc * CH:(dvc + 1) * CH]
                        self.mm(po, oap, v_, v_.ap[:, dvc * 128:(dvc + 1) * 128], scT, scT.ap, start=True, stop=False)
                        self.mm(po, oap, Sr, Sr.ap[:, 0, dvc * 128:(dvc + 1) * 128], qd, qd.ap[:, 0, t0:t0 + CH], start=False, stop=False)
                        self.mm(po, oap, Sr, Sr.ap[:, 1, dvc * 128:(dvc + 1) * 128], qd, qd.ap[:, 1, t0:t0 + CH], start=False, stop=True)
                    po3 = po.ap[:, 0:4 * CH].rearrange("p (a b) -> p a b", b=CH)
                    if d == 0:
                        self.cp("act", OT, OT.ap[:, :, t0:t0 + CH], po, po3)
                    else:
                        self.tt("dve", OT, OT.ap[:, :, t0:t0 + CH], OT, OT.ap[:, :, t0:t0 + CH], po, po3, ALU.add)
                    for dkc in range(2):
                        pu = psU[dkc]
                        self.mm(pu, pu.ap, kendT, kendT.ap[:, dkc * 128:(dkc + 1) * 128], v_, v_.ap)
                        self.stt("dve", S, S.ap[:, dkc, :], S, S.ap[:, dkc, :], ebend.ap[:, dkc, c:c + 1], pu, pu.ap, ALU.mult, ALU.add, R=[ebend])
                        self.cp("dve", Sr, Sr.ap[:, dkc, :], S, S.ap[:, dkc, :])
            k = 0
            for (t0, n, s) in BLKS:
                for dvc in range(4):
                    q_ = sq[dvc % 2]
                    self.act(q_, q_.ap[:, 0:n], OT, OT.ap[:, dvc, t0:t0 + n], AF.Square)
                    self.mm(psN, psN.ap[:, 0:n], self.onesr, self.onesr.ap, q_, q_.ap[:, 0:n], start=(dvc == 0), stop=(dvc == 3))
                self.rsqrt_from(rstd, rstd.ap[:, 0:n], psN, psN.ap[:, 0:n], 1.0 / 512)
                for dvc in range(4):
                    z_, y_, o_ = zt[k % 2], yt[k % 2], ob[k % 2]
                    k += 1
                    zr = 4096 + hd * 512 + dvc * 128
                    self.ld(z_, z_.ap[:, 0:n], self.PROJT[zr:zr + 128, t0:t0 + n])
                    self.stt("dve", y_, y_.ap[:, 0:n], OT, OT.ap[:, dvc, t0:t0 + n], gg.ap[:, dvc:dvc + 1], rstd, rstd.ap[:, 0:n], ALU.mult, ALU.mult, R=[gg])
                    self.act(z_, z_.ap[:, 0:n], z_, z_.ap[:, 0:n], AF.Silu)
                    self.tt("dve", o_, o_.ap[:, 0:n], y_, y_.ap[:, 0:n], z_, z_.ap[:, 0:n], ALU.mult)
                    mr = hd * 512 + dvc * 128
                    self.st(self.MRGT[mr:mr + 128, t0:t0 + n], o_, o_.ap[:, 0:n])


def _consts():
    c = np.zeros((6, 128, 128), np.float32)
    c[0] = np.eye(128)
    c[1] = 1.0
    c[2, :64, :64] = 1.0
    c[2, 64:, 64:] = 1.0
    for m in range(2):
        o = m * 64
        for i in range(16):
            c[3, o + 16 + i, o + i] = -1.0
            c[3, o + i, o + 16 + i] = 1.0
            c[3, o + 48 + i, o + 32 + i] = -1.0
            c[3, o + 32 + i, o + 48 + i] = 1.0
    s = np.arange(128)
    c[4] = (s[:, None] <= s[None, :])
    c[5] = (s[:, None] >= s[None, :])
    return c


def _rope_tables():
    GRID_W = 64
    L = 2048
    row = np.repeat(np.arange(L // GRID_W, dtype=np.float32), GRID_W)
    col = np.tile(np.arange(GRID_W, dtype=np.float32), L // GRID_W)
    n_freq = 16
    inv_freq = (np.float32(10000.0) ** (-np.arange(n_freq, dtype=np.float32) / np.float32(n_freq))).astype(np.float32)
    ang_r = row[:, None] * inv_freq
    ang_c = col[:, None] * inv_freq
    ang = np.concatenate([ang_r, ang_r, ang_c, ang_c], axis=-1).astype(np.float32)
    cos, sin = np.cos(ang).astype(np.float32), np.sin(ang).astype(np.float32)
    t = np.zeros((2, 128, L), np.float32)
    t[0] = np.concatenate([cos.T, cos.T], axis=0)
    t[1] = np.concatenate([sin.T, sin.T], axis=0)
    return t


def _prep_shared(inp):
    f = lambda a: np.ascontiguousarray(a, dtype=np.float32)
    sh = {}
    sh["ada_w"] = f(inp["ada_w"])
    sh["adabT"] = f(inp["ada_b"].reshape(DEPTH, 48, 128).transpose(0, 2, 1))
    sh["ngT"] = f(inp["norm_g"].reshape(DEPTH, 16, 128).transpose(0, 2, 1))
    sh["w_in"] = f(np.stack([inp["ev_w_in"][0], inp["od_w_in"][0], inp["ev_w_in"][1], inp["od_w_in"][1]]))
    sh["w_out"] = f(np.stack([inp["ev_w_out"][0], inp["od_w_out"][0], inp["ev_w_out"][1], inp["od_w_out"][1]]))
    sh["consts"] = _consts()
    p = np.zeros((2, 2, 128, 3, 32), np.float32)
    for k, name in enumerate(["s5_a_re", "s5_a_im"]):
        a = inp[name].reshape(2, 2, 32, 2, 64)
        p[:, :, :, k, :] = a.transpose(0, 1, 3, 4, 2).reshape(2, 2, 128, 32)
    ldt = np.broadcast_to(inp["s5_log_dt"].reshape(2, 2, 32, 2, 1), (2, 2, 32, 2, 64))
    p[:, :, :, 2, :] = ldt.transpose(0, 1, 3, 4, 2).reshape(2, 2, 128, 32)
    sh["s5p"] = p
    Bm = np.zeros((2, 2, 2, 32, 128, 128), np.float32)
    Cm = np.zeros((2, 2, 2, 32, 128, 128), np.float32)
    for ri, (bn, cn) in enumerate([("s5_b_re", "s5_c_re"), ("s5_b_im", "s5_c_im")]):
        b = inp[bn]
        c = inp[cn]
        for st in range(32):
            for gi in range(2):
                g = 2 * st + gi
                g8 = g % 8
                Bm[:, :, ri, st, g8 * 16:(g8 + 1) * 16, gi * 64:(gi + 1) * 64] = b[:, :, g].transpose(0, 1, 3, 2)
                Cm[:, :, ri, st, gi * 64:(gi + 1) * 64, g8 * 16:(g8 + 1) * 16] = c[:, :, g].transpose(0, 1, 3, 2)
    sh["s5B"], sh["s5C"] = Bm, Cm
    sh["s5d"] = f(inp["s5_d"].reshape(2, 8, 128).transpose(0, 2, 1))
    tau = np.zeros((4, 128, TC), np.float32)
    tau[0] = np.arange(TC)[None, :]
    tau[1] = (TC - 1 - np.arange(TC))[None, :]
    tau[2] = 1.0; tau[2, :, 0] = 0.0
    tau[3] = 1.0; tau[3, :, TC - 1] = 0.0
    sh["s5tau"] = tau
    sh["wglu"] = f(inp["s5_w_glu"])
    dq = np.zeros((2, 128, 3), np.float32)
    dq[:, :, 0] = np.tile(inp["da_qn_g"], (1, 2))
    dq[:, :, 1] = np.tile(inp["da_kn_g"], (1, 2))
    dq[:, :, 2] = inp["da_subln_g"]
    sh["daq"] = dq
    sh["lamT"] = f(inp["da_lam"].transpose(0, 2, 1))
    sh["rope"] = _rope_tables()
    sh["wa1"] = f(inp["gla_wa1"])
    sh["wa2"] = f(inp["gla_wa2"])
    sh["baT"] = f(inp["gla_ba"].reshape(2, 2, 8, 128).transpose(0, 1, 3, 2))
    sh["glag"] = f(inp["gla_norm_g"].reshape(2, 4, 128).transpose(0, 2, 1))
    gm = np.ones((2, 128, NT), np.float32)
    gm[0, :, 0::128] = 0.0
    gm[1, :, 127::128] = 0.0
    sh["gmask"] = gm
    return sh


def _prep_core(inp, b):
    d = {}
    d["xin"] = np.ascontiguousarray(np.concatenate([inp["ctx"][b], inp["x"][b]], axis=0), dtype=np.float32)
    cT = np.zeros((128, 16, 2), np.float32)
    cT[:, :, 0] = inp["c"][b].reshape(16, 128).T
    cT[:, :, 1] = inp["c_ctx"].reshape(16, 128).T
    d["cT"] = cT
    return d


_NC_CACHE = {}


def kernel(**inputs):
    inp = {k: np.asarray(v) for k, v in inputs.items()}
    if "full" not in _NC_CACHE:
        _NC_CACHE["full"] = KB().build()
    nc = _NC_CACHE["full"]
    sh = _prep_shared(inp)
    in_maps = []
    for core in range(8):
        m = dict(sh)
        m.update(_prep_core(inp, core % 4))
        in_maps.append(m)
    res = run_bass_kernel_spmd(nc, in_maps, core_ids=list(range(8)))
    out = np.stack([np.asarray(res.results[b]["out"]) for b in range(4)], axis=0)
    return out.astype(np.float32)
```

```python
import math
import numpy as np
import concourse.bass as bass
import concourse.mybir as mybir
from concourse.bass_utils import run_bass_kernel_spmd

F32 = mybir.dt.float32
F32R = mybir.dt.float32r
I32 = mybir.dt.int32
AF = mybir.ActivationFunctionType
ALU = mybir.AluOpType

D = 2048
NT = 2304
NCTX = 256
DEPTH = 4
EPS = 1e-6
PI = math.pi
BLKS = [(0, 256, 1)] + [(256 + 512 * i, 512, 0) for i in range(4)]
TC = 128


class Buf:
    __slots__ = ("w", "r")

    def __init__(self):
        self.w = None
        self.r = {}


class T:
    __slots__ = ("ap", "b")

    def __init__(self, ap):
        self.ap = ap
        self.b = Buf()

    def __getitem__(self, k):
        return self.ap[k]


class FW:
    NDMA = 24
    SEM_ROLL = 30000

    def __init__(self, nc):
        self.nc = nc
        self.engs = {"pe": nc.tensor, "act": nc.scalar, "dve": nc.vector, "pool": nc.gpsimd, "sp": nc.sync}
        self.ops = {e: [] for e in self.engs}
        self.sems, self.cnt, self.cur, self.owner = {}, {}, {}, {}
        self.seen = {e: {} for e in self.engs}
        self.nsem = 0
        for e in self.engs:
            self._new_csem(e)
        self.dma_keys = []
        for i in range(self.NDMA):
            k = f"dma{i}"
            self.sems[k] = nc.alloc_semaphore(k)
            self.cnt[k] = 0
            self.dma_keys.append(k)
        self.dma_rr = 0
        self.out_tickets = []
        self.n_ins = 0

    def _new_csem(self, e):
        k = f"c_{e}_{self.nsem}"
        self.nsem += 1
        self.sems[k] = self.nc.alloc_semaphore(k)
        self.cnt[k] = 0
        self.cur[e] = k
        self.owner[k] = e

    def _waits(self, e, reads, writes, extra=()):
        need = {}

        def add(k, v):
            if need.get(k, 0) < v:
                need[k] = v
        for t in reads:
            if t.b.w is not None:
                add(*t.b.w)
        for t in writes:
            if t.b.w is not None:
                add(*t.b.w)
            for k, v in t.b.r.items():
                add(k, v)
        for k, v in extra:
            add(k, v)
        out = []
        seen = self.seen[e]
        for k, v in need.items():
            if seen.get(k, 0) >= v:
                continue
            if self.owner.get(k) == e and e == "pe":
                continue
            seen[k] = v
            out.append((k, v))
        return out

    def _mark(self, t, reads, writes):
        k, v = t
        for x in reads:
            if x.b.r.get(k, 0) < v:
                x.b.r[k] = v
        for x in writes:
            x.b.w = t
            x.b.r = {}

    def op(self, e, fn, R=(), W=()):
        waits = self._waits(e, R, W)
        k = self.cur[e]
        if self.cnt[k] >= self.SEM_ROLL:
            self._new_csem(e)
            k = self.cur[e]
        self.cnt[k] += 1
        t = (k, self.cnt[k])
        self.ops[e].append((waits, fn, (k, 1)))
        self._mark(t, R, W)
        self.n_ins += 1
        return t

    def dma(self, out, in_, R=(), W=(), q="sp", is_output=False, **kw):
        k = self.dma_keys[self.dma_rr % self.NDMA]
        self.dma_rr += 1
        extra = [(k, self.cnt[k])] if self.cnt[k] > 0 else []
        waits = self._waits(q, R, W, extra)
        self.cnt[k] += 16
        t = (k, self.cnt[k])

        def fn(eng, out=out, in_=in_, kw=kw):
            return eng.dma_start(out=out, in_=in_, **kw)
        self.ops[q].append((waits, fn, (k, 16)))
        self._mark(t, R, W)
        if is_output:
            self.out_tickets.append(t)
        self.n_ins += 1
        return t

    def barrier(self):
        allv = [(k, v) for k, v in self.cnt.items() if v > 0]
        for e in self.engs:
            seen = self.seen[e]
            waits = []
            for k, v in allv:
                if self.owner.get(k) == e and e == "pe":
                    continue
                if seen.get(k, 0) < v:
                    seen[k] = v
                    waits.append((k, v))
            if waits:
                self.ops[e].append((waits, None, None))

    def finish(self):
        self.barrier()
        nc, sems = self.nc, self.sems
        with nc.Block() as block:
            def mk(e):
                def body(eng):
                    for waits, fn, inc in self.ops[e]:
                        for k, v in waits:
                            eng.wait_ge(sems[k], v)
                        if fn is not None:
                            fn(eng).then_inc(sems[inc[0]], inc[1])
                return body
            block.sync(mk("sp"))
            block.scalar(mk("act"))
            block.vector(mk("dve"))
            block.gpsimd(mk("pool"))
            block.tensor(mk("pe"))


class KB:
    ARENA_WORDS = 52000

    def __init__(self, nlayers=DEPTH, debug=()):
        self.nlayers = nlayers
        self.debug = set(debug)
        nc = self.nc = bass.Bass("TRN2", target_bir_lowering=False)
        self.fw = FW(nc)
        arena = nc.alloc_sbuf_tensor("arena", [128, self.ARENA_WORDS], F32)
        self.arena_addr = int(nc.lookup_mloc(arena).addr)
        self.ntile = 0
        self.base = 0
        self.off = 0
        self.psum = nc.alloc_psum_tensor("psum", [128, 4096], F32).ap()
        self.PS = [T(self.psum[:, i * 512:(i + 1) * 512]) for i in range(8)]
        self.din = {}
        self.rr = 0

    def inp(self, name, shape):
        ap = self.nc.dram_tensor(name, list(shape), F32, kind="ExternalInput").ap()
        self.din[name] = ap
        return ap

    def scratch(self, name, shape):
        kind = "ExternalOutput" if name in self.debug else "Internal"
        return self.nc.dram_tensor(name, list(shape), F32, kind=kind).ap()

    def tile(self, shape, dt=F32):
        n = int(np.prod(shape[1:]))
        assert self.off + n <= self.ARENA_WORDS, ("SBUF overflow", self.off, n)
        self.ntile += 1
        h = self.nc.alloc_sbuf_tensor_at(f"t{self.ntile}", [int(x) for x in shape], dt, offset=self.arena_addr + 4 * self.off)
        self.off += (n + 7) // 8 * 8
        return T(h.ap())

    def new_phase(self):
        self.fw.barrier()
        self.off = self.base

    def op(self, e, fn, R=(), W=()):
        return self.fw.op(e, fn, R, W)

    def mm(self, out, o_ap, lt, l_ap, rt, r_ap, start=True, stop=True):
        self.fw.op("pe", lambda e: e.matmul(o_ap, lhsT=l_ap, rhs=r_ap, start=start, stop=stop), [lt, rt], [out])

    def tp(self, out, o_ap, it, i_ap, ident):
        self.fw.op("pe", lambda e: e.transpose(o_ap, i_ap, ident.ap[0:i_ap.shape[0], 0:i_ap.shape[0]]), [it, ident], [out])

    def tt(self, e, out, o_ap, a, a_ap, b, b_ap, op):
        self.fw.op(e, lambda g: g.tensor_tensor(o_ap, a_ap, b_ap, op), [a, b], [out])

    def ts(self, e, out, o_ap, a, a_ap, s1, s2, op0, op1=None, R=()):
        if op1 is None:
            self.fw.op(e, lambda g: g.tensor_scalar(o_ap, a_ap, s1, None, op0), [a] + list(R), [out])
        else:
            self.fw.op(e, lambda g: g.tensor_scalar(o_ap, a_ap, s1, s2, op0, op1), [a] + list(R), [out])

    def stt(self, e, out, o_ap, a, a_ap, sc, b, b_ap, op0, op1, R=()):
        e = "dve"
        self.fw.op(e, lambda g: g.scalar_tensor_tensor(o_ap, a_ap, sc, b_ap, op0, op1), [a, b] + list(R), [out])

    def act(self, out, o_ap, a, a_ap, func, bias=None, scale=None, R=()):
        kw = {}
        if bias is not None:
            kw["bias"] = bias
        if scale is not None:
            kw["scale"] = scale
        self.fw.op("act", lambda g: g.activation(o_ap, a_ap, func, **kw), [a] + list(R), [out])

    def cp(self, e, out, o_ap, a, a_ap):
        if e == "act":
            self.fw.op("act", lambda g: g.copy(o_ap, a_ap), [a], [out])
        else:
            self.fw.op(e, lambda g: g.tensor_copy(o_ap, a_ap), [a], [out])

    def alt(self, engines=("dve", "act")):
        self.rr += 1
        return engines[self.rr % len(engines)]

    def ld(self, out, o_ap, src, q="sp"):
        self.fw.dma(o_ap, src, W=[out], q=q)

    def st(self, dst, it, i_ap, q="act", is_output=False):
        self.fw.dma(dst, i_ap, R=[it], q=q, is_output=is_output)

    def rsqrt_from(self, out, o_ap, src, s_ap, inv_n):
        self.act(out, o_ap, src, s_ap, AF.Ln, bias=self.epsT.ap[:, 0:1], scale=inv_n, R=[self.epsT])
        self.act(out, o_ap, out, o_ap, AF.Exp, scale=-0.5)

    def build(self):
        nc = self.nc
        L = self.nlayers
        xin = self.inp("xin", [NT, D])
        cT_d = self.inp("cT", [128, 16, 2])
        ada_w = self.inp("ada_w", [DEPTH, D, 3 * D])
        adabT = self.inp("adabT", [DEPTH, 128, 48])
        ngT = self.inp("ngT", [DEPTH, 128, 16])
        w_in = self.inp("w_in", [DEPTH, D, 3 * D])
        w_out = self.inp("w_out", [DEPTH, D, D])
        consts = self.inp("consts", [6, 128, 128])
        s5p = self.inp("s5p", [2, 2, 128, 3, 32])
        s5B = self.inp("s5B", [2, 2, 2, 32, 128, 128])
        s5C = self.inp("s5C", [2, 2, 2, 32, 128, 128])
        s5d = self.inp("s5d", [2, 128, 8])
        s5tau = self.inp("s5tau", [4, 128, TC])
        wglu = self.inp("wglu", [2, 1024, 1024])
        daq = self.inp("daq", [2, 128, 3])
        lamT = self.inp("lamT", [2, 64, 4])
        rope = self.inp("rope", [2, 128, 2048])
        wa1 = self.inp("wa1", [2, 2, D, 16])
        wa2 = self.inp("wa2", [2, 2, 16, 1024])
        baT = self.inp("baT", [2, 2, 128, 8])
        glag = self.inp("glag", [2, 128, 4])
        gmask = self.inp("gmask", [2, 128, NT])
        out_d = nc.dram_tensor("out", [2048, D], F32, kind="ExternalOutput").ap()

        self.XT = self.scratch("XT", [D, NT])
        self.PROJT = self.scratch("PROJT", [3 * D, NT])
        self.VTOK = self.scratch("VTOK", [NT, D])
        self.MRGT = self.scratch("MRGT", [D, NT])
        self.GT = self.scratch("GT", [1024, NT])
        self.LAT = self.scratch("LAT", [2, 1024, NT])

        self.cst = self.tile([128, 6, 128])
        self.ld(self.cst, self.cst.ap, consts.rearrange("c p n -> p c n"))
        self.ident = T(self.cst.ap[:, 0, :]); self.ident.b = self.cst.b
        self.onesr = self.tile([128, 128], F32R)
        self.blk64r = self.tile([128, 128], F32R)
        self.protr = self.tile([128, 128], F32R)
        self.cp("dve", self.onesr, self.onesr.ap, self.cst, self.cst.ap[:, 1, :])
        self.cp("dve", self.blk64r, self.blk64r.ap, self.cst, self.cst.ap[:, 2, :])
        self.cp("dve", self.protr, self.protr.ap, self.cst, self.cst.ap[:, 3, :])
        self.epsT = self.tile([128, 1])
        self.op("dve", lambda g: g.memset(self.epsT.ap, EPS), W=[self.epsT])
        self.modv = self.tile([128, 48, 2])
        self.gs = self.tile([128, 16, 2])
        self.cT = self.tile([128, 16, 2])
        self.sc = self.tile([128, 16, 2])
        self.ld(self.cT, self.cT.ap, cT_d)
        self.act(self.sc, self.sc.ap, self.cT, self.cT.ap, AF.Silu)
        self.base = self.off

        self.phase_in_transpose(xin)
        for l in range(L):
            j = l // 2
            self.phase_mod(l, ada_w, adabT, ngT)
            if l % 2 == 0:
                self.phase_inproj(l, w_in, tok_chunks=range(32, 40))
                if "stop_proj" in self.debug:
                    break
                self.phase_s5(j, s5p, s5B, s5C, s5d, s5tau, wglu)
                if "stop_s5" in self.debug:
                    break
                lam_init = 0.8 - 0.6 * math.exp(-0.3 * l)
                self.phase_da(j, daq, lamT, rope, lam_init)
                if "stop_da" in self.debug:
                    break
            else:
                self.phase_inproj(l, w_in, tok_chunks=range(16, 32), gla=(j, wa1, wa2, baT))
                self.phase_gla(j, glag, gmask)
                if "stop_gla" in self.debug:
                    break
            self.phase_outproj(l, w_out)
        self.phase_out_transpose(out_d)
        self.fw.finish()
        return nc

    def phase_in_transpose(self, xin):
        self.new_phase()
        xt = [self.tile([128, D]) for _ in range(2)]
        stg = [self.tile([128, 16, 128]) for _ in range(2)]
        XTv = self.XT.rearrange("(kc p) t -> p kc t", p=128)
        for tt in range(18):
            x = xt[tt % 2]
            s = stg[tt % 2]
            self.ld(x, x.ap, xin[tt * 128:(tt + 1) * 128, :])
            for g4 in range(4):
                ps = self.PS[g4 % 4]
                for i in range(4):
                    kc = g4 * 4 + i
                    self.tp(ps, ps.ap[:, i * 128:(i + 1) * 128], x, x.ap[:, kc * 128:(kc + 1) * 128], self.ident)
                e = "act" if g4 % 2 else "dve"
                self.cp(e, s, s.ap[:, g4 * 4:(g4 + 1) * 4, :], ps, ps.ap.rearrange("p (a b) -> p a b", b=128))
            self.st(XTv[:, :, tt * 128:(tt + 1) * 128], s, s.ap)

    def phase_out_transpose(self, out_d):
        self.new_phase()
        xt = [self.tile([128, 16, 128]) for _ in range(2)]
        stg = [self.tile([128, D]) for _ in range(2)]
        XTv = self.XT.rearrange("(kc p) t -> p kc t", p=128)
        for tt in range(16):
            x = xt[tt % 2]
            s = stg[tt % 2]
            t0 = NCTX + tt * 128
            self.ld(x, x.ap, XTv[:, :, t0:t0 + 128])
            for g4 in range(4):
                ps = self.PS[g4 % 4]
                for i in range(4):
                    kc = g4 * 4 + i
                    self.tp(ps, ps.ap[:, i * 128:(i + 1) * 128], x, x.ap[:, kc, :], self.ident)
                e = "act" if g4 % 2 else "dve"
                self.cp(e, s, s.ap[:, g4 * 512:(g4 + 1) * 512], ps, ps.ap)
            self.st(out_d[tt * 128:(tt + 1) * 128, :], s, s.ap, is_output=True)

    def phase_mod(self, l, ada_w, adabT, ngT):
        self.new_phase()
        wb = [self.tile([128, 16, 512]) for _ in range(2)]
        adab = self.tile([128, 48])
        ng = self.tile([128, 16])
        self.ld(adab, adab.ap, adabT[l])
        self.ld(ng, ng.ap, ngT[l])
        Wv = ada_w[l].rearrange("(kc p) n -> p kc n", p=128)
        pm = self.PS[0]
        for cc in range(12):
            w = wb[cc % 2]
            self.ld(w, w.ap, Wv[:, :, cc * 512:(cc + 1) * 512])
            for jj in range(4):
                j = cc * 4 + jj
                for kc in range(16):
                    self.mm(pm, pm.ap[:, 2 * j:2 * j + 2], w, w.ap[:, kc, jj * 128:(jj + 1) * 128],
                            self.sc, self.sc.ap[:, kc, :], start=(kc == 0), stop=(kc == 15))
        pm3 = pm.ap[:, 0:96].rearrange("p (j s) -> p j s", s=2)
        for s in range(2):
            self.tt("dve", self.modv, self.modv.ap[:, :, s], pm, pm3[:, :, s], adab, adab.ap, ALU.add)
        for s in range(2):
            self.stt("dve", self.gs, self.gs.ap[:, :, s], self.modv, self.modv.ap[:, 16:32, s], 1.0, ng, ng.ap, ALU.add, ALU.mult)

    def phase_inproj(self, l, w_in, tok_chunks=(), gla=None):
        self.new_phase()
        hR = self.tile([128, 16, NT], F32R)
        hb = [T(hR.ap) for _ in BLKS]
        self._wraw_B = self.tile([128, 16, 128])
        xc = [self.tile([128, 512]) for _ in range(3)]
        tm = [self.tile([128, 512]) for _ in range(2)]
        sq = [self.tile([128, 512], F32R) for _ in range(2)]
        rstd = self.tile([128, 512])
        XTv = self.XT.rearrange("(kc p) t -> p kc t", p=128)
        xi = 0
        for bi, (t0, n, s) in enumerate(BLKS):
            h = hb[bi]
            ps = self.PS[bi % 2]
            for kc in range(16):
                x = xc[xi % 3]; xi += 1
                self.ld(x, x.ap[:, 0:n], XTv[:, kc, t0:t0 + n])
                q = sq[kc % 2]
                self.act(q, q.ap[:, 0:n], x, x.ap[:, 0:n], AF.Square)
                self.mm(ps, ps.ap[:, 0:n], self.onesr, self.onesr.ap, q, q.ap[:, 0:n], start=(kc == 0), stop=(kc == 15))
            self.rsqrt_from(rstd, rstd.ap[:, 0:n], ps, ps.ap[:, 0:n], 1.0 / D)
            for kc in range(16):
                x = xc[xi % 3]; xi += 1
                self.ld(x, x.ap[:, 0:n], XTv[:, kc, t0:t0 + n])
                t = tm[kc % 2]
                self.tt("dve", t, t.ap[:, 0:n], x, x.ap[:, 0:n], rstd, rstd.ap[:, 0:n], ALU.mult)
                self.ts("dve", h, hR.ap[:, kc, t0:t0 + n], t, t.ap[:, 0:n],
                        self.gs.ap[:, kc, s:s + 1], self.modv.ap[:, kc, s:s + 1], ALU.mult, ALU.add, R=[self.gs, self.modv])
        if "hT" in self.debug:
            dbg = self.scratch("hT", [D, NT])
            dt_ = self._wraw_B
            for tt in range(18):
                self.cp("dve", dt_, dt_.ap, hb[0 if tt < 2 else 1 + (tt - 2) // 4], hR.ap[:, :, tt * 128:(tt + 1) * 128])
                self.fw.dma(dbg.rearrange("(kc p) t -> p kc t", p=128)[:, :, tt * 128:(tt + 1) * 128], dt_.ap, R=[dt_], q="act")
        wraw = self._wraw_B
        wr = [self.tile([128, 16, 128], F32R) for _ in range(2)]
        ostb = [self.tile([128, n]) for (t0, n, s) in BLKS]
        Wv = w_in[l].rearrange("(kc p) n -> p kc n", p=128)
        VT = self.VTOK.rearrange("(tt p) n -> p tt n", p=128)
        tok_chunks = set(tok_chunks)
        pi = 0
        for j in range(48):
            w = wr[j % 2]
            self.ld(wraw, wraw.ap, Wv[:, :, j * 128:(j + 1) * 128])
            self.cp("pool", w, w.ap, wraw, wraw.ap)
            if j in tok_chunks:
                jj = j - min(tok_chunks)
                for g in range(5):
                    ps = self.PS[2 + pi % 4]; pi += 1
                    tts = list(range(g * 4, min(g * 4 + 4, 18)))
                    o = ostb[g if len(tts) == 4 and g > 0 else (1 if g == 0 else 0)]
                    for ii, tt in enumerate(tts):
                        bi = 0 if tt < 2 else 1 + (tt - 2) // 4
                        for kc in range(16):
                            self.mm(ps, ps.ap[:, ii * 128:(ii + 1) * 128], hb[bi], hR.ap[:, kc, tt * 128:(tt + 1) * 128],
                                    w, w.ap[:, kc, :], start=(kc == 0), stop=(kc == 15))
                    nn = len(tts) * 128
                    self.cp("act" if g % 2 else "dve", o, o.ap[:, 0:nn], ps, ps.ap[:, 0:nn])
                    self.st(VT[:, tts[0]:tts[-1] + 1, jj * 128:(jj + 1) * 128], o, o.ap[:, 0:nn].rearrange("p (a b) -> p a b", b=128))
            else:
                for bi, (t0, n, s) in enumerate(BLKS):
                    ps = self.PS[2 + pi % 4]; pi += 1
                    o = ostb[bi]
                    for kc in range(16):
                        self.mm(ps, ps.ap[:, 0:n], w, w.ap[:, kc, :], hb[bi], hR.ap[:, kc, t0:t0 + n], start=(kc == 0), stop=(kc == 15))
                    self.cp("act" if bi % 2 else "dve", o, o.ap, ps, ps.ap[:, 0:n])
                    self.st(self.PROJT[j * 128:(j + 1) * 128, t0:t0 + n], o, o.ap)
        if gla is not None:
            self.gla_gate(gla, hb, hR, wraw, wr, ostb, sq, tm)

    def gla_gate(self, gla, hb, hR, wraw, wr, ostb, sq, tm):
        j, wa1, wa2, baT = gla
        nba = self.tile([128, 8])
        wflat = wraw.ap.rearrange("p a b -> p (a b)")
        w2 = T(wr[1].ap.rearrange("p a b -> p (a b)")[:, 0:1024]); w2.b = wr[1].b
        w1 = wr[0]
        lr = sq[0]
        k = 0
        for d in range(2):
            self.op("pool", lambda g: g.memset(wraw.ap, 0.0), W=[wraw])
            self.ld(wraw, wraw.ap[:, :, 0:16], wa1[j, d].rearrange("(kc p) r -> p kc r", p=128))
            self.cp("pool", w1, w1.ap, wraw, wraw.ap)
            self.op("pool", lambda g: g.memset(wraw.ap, 0.0), W=[wraw])
            self.ld(wraw, wflat[0:16, 0:1024], wa2[j, d])
            self.cp("pool", w2, w2.ap, wraw, wflat[:, 0:1024])
            self.ld(nba, nba.ap, baT[j, d])
            self.ts("dve", nba, nba.ap, nba, nba.ap, -1.0, None, ALU.mult)
            for bi, (t0, n, s) in enumerate(BLKS):
                ps = self.PS[bi % 2]
                for kc in range(16):
                    self.mm(ps, ps.ap[:, 0:n], w1, w1.ap[:, kc, :], hb[bi], hR.ap[:, kc, t0:t0 + n], start=(kc == 0), stop=(kc == 15))
                self.cp("dve", lr, lr.ap[:, 0:n], ps, ps.ap[:, 0:n])
                for c in range(8):
                    o = ostb[1 + k % 4]
                    t_ = tm[k % 2]
                    ps2 = self.PS[2 + k % 4]
                    k += 1
                    self.mm(ps2, ps2.ap[:, 0:n], w2, w2.ap[:, c * 128:(c + 1) * 128], lr, lr.ap[:, 0:n])
                    self.act(t_, t_.ap[:, 0:n], ps2, ps2.ap[:, 0:n], AF.Exp, bias=nba.ap[:, c:c + 1], scale=-1.0, R=[nba])
                    self.act(t_, t_.ap[:, 0:n], t_, t_.ap[:, 0:n], AF.Ln, bias=1.0)
                    self.ts("dve", o, o.ap[:, 0:n], t_, t_.ap[:, 0:n], -1.0 / 16.0, None, ALU.mult)
                    self.st(self.LAT[d, c * 128:(c + 1) * 128, t0:t0 + n], o, o.ap[:, 0:n])

    def phase_outproj(self, l, w_out):
        self.new_phase()
        mR = self.tile([128, 16, NT], F32R)
        mb = [T(mR.ap) for _ in BLKS]
        xc = [self.tile([128, 512]) for _ in range(3)]
        Mv = self.MRGT.rearrange("(kc p) t -> p kc t", p=128)
        xi = 0
        for bi, (t0, n, s) in enumerate(BLKS):
            for kc in range(16):
                x = xc[xi % 3]; xi += 1
                self.ld(x, x.ap[:, 0:n], Mv[:, kc, t0:t0 + n])
                self.cp(self.alt(), mb[bi], mR.ap[:, kc, t0:t0 + n], x, x.ap[:, 0:n])
        wraw = self.tile([128, 16, 128])
        wr = [self.tile([128, 16, 128], F32R) for _ in range(2)]
        xo = [self.tile([128, n]) for (t0, n, s) in BLKS]
        Wv = w_out[l].rearrange("(kc p) n -> p kc n", p=128)
        pi = 0
        for j in range(16):
            w = wr[j % 2]
            self.ld(wraw, wraw.ap, Wv[:, :, j * 128:(j + 1) * 128])
            self.cp("pool", w, w.ap, wraw, wraw.ap)
            for bi, (t0, n, s) in enumerate(BLKS):
                x = xo[bi]
                self.ld(x, x.ap, self.XT[j * 128:(j + 1) * 128, t0:t0 + n])
                ps = self.PS[pi % 4]; pi += 1
                for kc in range(16):
                    self.mm(ps, ps.ap[:, 0:n], w, w.ap[:, kc, :], mb[bi], mR.ap[:, kc, t0:t0 + n], start=(kc == 0), stop=(kc == 15))
                self.stt("dve", x, x.ap, ps, ps.ap[:, 0:n], self.modv.ap[:, 32 + j, s:s + 1], x, x.ap,
                         ALU.mult, ALU.add, R=[self.modv])
                self.st(self.XT[j * 128:(j + 1) * 128, t0:t0 + n], x, x.ap)

    def sincos(self, ang, n, sin_out=None, cos_out=None, scr=None):
        u, ki, rd = scr
        for out, shift in ((sin_out, 0.0), (cos_out, PI / 2)):
            if out is None:
                continue
            self.ts("dve", u, u.ap[:, 0:n], ang, ang.ap[:, 0:n], 1.0 / (2 * PI), shift / (2 * PI), ALU.mult, ALU.add)
            self.cp("dve", ki, ki.ap[:, 0:n], u, u.ap[:, 0:n])
            self.cp("dve", u, u.ap[:, 0:n], ki, ki.ap[:, 0:n])
            self.stt("dve", rd, rd.ap[:, 0:n], u, u.ap[:, 0:n], -2 * PI, ang, ang.ap[:, 0:n], ALU.mult, ALU.add)
            self.ts("dve", rd, rd.ap[:, 0:n], rd, rd.ap[:, 0:n], shift, 3.1415925, ALU.add, ALU.min)
            self.ts("dve", rd, rd.ap[:, 0:n], rd, rd.ap[:, 0:n], -3.1415925, None, ALU.max)
            self.act(out, out.ap[:, 0:n], rd, rd.ap[:, 0:n], AF.Sin)

    def phase_s5(self, j, s5p, s5B, s5C, s5d, s5tau, wglu):
        self.new_phase()
        W = 8 * TC
        tau = self.tile([128, 4, TC])
        self.ld(tau, tau.ap, s5tau.rearrange("k p t -> p k t"))
        dsk = self.tile([128, 8])
        self.ld(dsk, dsk.ap, s5d[j])
        t1, t2, t3, t4 = [self.tile([128, W]) for _ in range(4)]
        ki = self.tile([128, W], I32)
        scr = (t1, ki, t2)
        ang = t3
        PR = []
        for d in range(2):
            prm = self.tile([128, 3, 32])
            self.ld(prm, prm.ap, s5p[j, d])
            names = "dt r th sn cs fre fim nfre c1 s1 t1 t2 den".split()
            c = {k: self.tile([128, 32]) for k in names}
            are, aim, ldt = prm.ap[:, 0, :], prm.ap[:, 1, :], prm.ap[:, 2, :]
            self.act(c["dt"], c["dt"].ap, prm, ldt, AF.Exp)
            self.tt("dve", c["t1"], c["t1"].ap, prm, are, c["dt"], c["dt"].ap, ALU.mult)
            self.act(c["r"], c["r"].ap, c["t1"], c["t1"].ap, AF.Exp)
            self.tt("dve", c["th"], c["th"].ap, prm, aim, c["dt"], c["dt"].ap, ALU.mult)
            self.sincos(c["th"], 32, c["sn"], c["cs"], scr)
            self.tt("dve", c["t1"], c["t1"].ap, c["r"], c["r"].ap, c["cs"], c["cs"].ap, ALU.mult)
            self.ts("dve", c["t1"], c["t1"].ap, c["t1"], c["t1"].ap, -1.0, None, ALU.add)
            self.tt("dve", c["t2"], c["t2"].ap, c["r"], c["r"].ap, c["sn"], c["sn"].ap, ALU.mult)
            self.tt("dve", c["den"], c["den"].ap, prm, are, prm, are, ALU.mult)
            self.tt("dve", c["fre"], c["fre"].ap, prm, aim, prm, aim, ALU.mult)
            self.tt("dve", c["den"], c["den"].ap, c["den"], c["den"].ap, c["fre"], c["fre"].ap, ALU.add)
            self.op("dve", lambda g, t=c["den"]: g.reciprocal(t.ap, t.ap), [c["den"]], [c["den"]])
            self.tt("dve", c["fre"], c["fre"].ap, c["t1"], c["t1"].ap, prm, are, ALU.mult)
            self.tt("dve", c["fim"], c["fim"].ap, c["t2"], c["t2"].ap, prm, aim, ALU.mult)
            self.tt("dve", c["fre"], c["fre"].ap, c["fre"], c["fre"].ap, c["fim"], c["fim"].ap, ALU.add)
            self.tt("dve", c["fre"], c["fre"].ap, c["fre"], c["fre"].ap, c["den"], c["den"].ap, ALU.mult)
            self.tt("dve", c["fim"], c["fim"].ap, c["t2"], c["t2"].ap, prm, are, ALU.mult)
            self.tt("dve", c["nfre"], c["nfre"].ap, c["t1"], c["t1"].ap, prm, aim, ALU.mult)
            self.tt("dve", c["fim"], c["fim"].ap, c["fim"], c["fim"].ap, c["nfre"], c["nfre"].ap, ALU.subtract)
            self.tt("dve", c["fim"], c["fim"].ap, c["fim"], c["fim"].ap, c["den"], c["den"].ap, ALU.mult)
            self.ts("dve", c["nfre"], c["nfre"].ap, c["fre"], c["fre"].ap, -1.0, None, ALU.mult)
            self.ts("dve", c["t1"], c["t1"].ap, c["th"], c["th"].ap, float(TC), None, ALU.mult)
            self.sincos(c["t1"], 32, c["s1"], c["c1"], scr)
            self.tt("dve", c["c1"], c["c1"].ap, c["c1"], c["c1"].ap, c["r"], c["r"].ap, ALU.mult)
            self.tt("dve", c["s1"], c["s1"].ap, c["s1"], c["s1"].ap, c["r"], c["r"].ap, ALU.mult)
            PR.append(c)
        sn, cs, nsn, wr, wi, Rm = [self.tile([128, W]) for _ in range(6)]
        bre, bim = self.tile([128, W]), self.tile([128, W])
        ncs = self.tile([128, W])
        Xre = [self.tile([128, W]) for _ in range(2)]
        Xim = [self.tile([128, W]) for _ in range(2)]
        gre = [self.tile([128, W]) for _ in range(2)]
        gim = [self.tile([128, W]) for _ in range(2)]
        uu = [[self.tile([128, W], F32R) for _ in range(4)] for _ in range(2)]
        cr = [self.tile([128, 8]) for _ in range(4)]
        ustg = self.tile([128, NT])
        ur = self.tile([128, 2, NT], F32R)
        yt = self.tile([128, 2, NT])
        BCraw = self.tile([128, 16, 128])
        Br = self.tile([128, 16, 128], F32R)
        Cr = self.tile([128, 16, 128], F32R)
        PRb, PIb = [self.PS[0], self.PS[1]], [self.PS[2], self.PS[3]]
        PYs = [self.PS[4], self.PS[5]]
        pr_ap, pi_ap = self.psum[:, 0:W], self.psum[:, 2 * 512:2 * 512 + W]
        fseq = list(range(18))
        bseq = [1, 0] + list(range(17, 1, -1))
        v3 = lambda t, k: t.ap.rearrange("p (s t) -> p s t", t=TC)[:, :, k]
        for sg in range(4):
            for o in range(2):
                oc = 2 * sg + o
                self.ld(ustg, ustg.ap, self.PROJT[oc * 128:(oc + 1) * 128, :])
                self.cp("dve", ur, ur.ap[:, o, :], ustg, ustg.ap)
            for d in range(2):
                c = PR[d]
                for src, dstR in ((s5B, Br), (s5C, Cr)):
                    for ri in range(2):
                        self.ld(BCraw, BCraw.ap[:, ri::2, :], src[j, d, ri, sg * 8:(sg + 1) * 8].rearrange("s k m -> k s m"))
                    self.cp("pool", dstR, dstR.ap, BCraw, BCraw.ap)
                tauD = tau.ap[:, d, :]
                maskD = tau.ap[:, 2 + d, :]
                for i in range(8):
                    st = sg * 8 + i
                    sl = slice(i * TC, (i + 1) * TC)
                    self.ts("dve", ang, ang.ap[:, sl], tau, tauD, c["th"].ap[:, st:st + 1], None, ALU.mult, R=[c["th"]])
                    self.act(Rm, Rm.ap[:, sl], tau, maskD, AF.Copy, scale=c["r"].ap[:, st:st + 1], R=[c["r"]])
                self.sincos(ang, W, sn, cs, scr)
                self.act(nsn, nsn.ap, sn, sn.ap, AF.Copy, scale=-1.0)
                self.act(ncs, ncs.ap, cs, cs.ap, AF.Copy, scale=-1.0)
                for i in range(8):
                    st = sg * 8 + i
                    sl = slice(i * TC, (i + 1) * TC)
                    self.act(wr, wr.ap[:, sl], cs, cs.ap[:, sl], AF.Copy, scale=c["fre"].ap[:, st:st + 1], R=[c["fre"]])
                    self.act(wi, wi.ap[:, sl], cs, cs.ap[:, sl], AF.Copy, scale=c["fim"].ap[:, st:st + 1], R=[c["fim"]])
                    self.stt("dve", wr, wr.ap[:, sl], sn, sn.ap[:, sl], c["fim"].ap[:, st:st + 1], wr, wr.ap[:, sl], ALU.mult, ALU.add, R=[c["fim"]])
                    self.stt("dve", wi, wi.ap[:, sl], sn, sn.ap[:, sl], c["nfre"].ap[:, st:st + 1], wi, wi.ap[:, sl], ALU.mult, ALU.add, R=[c["nfre"]])
                first = 0 if d == 0 else TC - 1
                last = TC - 1 if d == 0 else 0
                seq = fseq if d == 0 else bseq
                c1 = c["c1"].ap[:, sg * 8:(sg + 1) * 8]
                s1 = c["s1"].ap[:, sg * 8:(sg + 1) * 8]

                def s1_pe(qi):
                    t0 = seq[qi] * TC
                    for i in range(8):
                        for ri, (pb, pap) in enumerate(((PRb, pr_ap), (PIb, pi_ap))):
                            self.mm(pb[i // 4], pap[:, i * TC:(i + 1) * TC], Br, Br.ap[:, 2 * i + ri, :], ur, ur.ap[:, i // 4, t0:t0 + TC])
                    self.fw.op("act", lambda g: g.copy(bre.ap, pr_ap), PRb, [bre])
                    self.fw.op("act", lambda g: g.copy(bim.ap, pi_ap), PIb, [bim])

                def s1_dve(qi):
                    xr, xi_ = Xre[qi % 2], Xim[qi % 2]
                    self.tt("dve", t1, t1.ap, wr, wr.ap, bre, bre.ap, ALU.mult)
                    self.tt("dve", t2, t2.ap, wi, wi.ap, bim, bim.ap, ALU.mult)
                    self.tt("dve", t3, t3.ap, wr, wr.ap, bim, bim.ap, ALU.mult)
                    self.tt("dve", t4, t4.ap, wi, wi.ap, bre, bre.ap, ALU.mult)
                    self.tt("dve", xr, xr.ap, t1, t1.ap, t2, t2.ap, ALU.subtract)
                    self.tt("dve", xi_, xi_.ap, t3, t3.ap, t4, t4.ap, ALU.add)

                def s2_dve(qi):
                    xr, xi_ = Xre[qi % 2], Xim[qi % 2]
                    g_re, g_im = gre[qi % 2], gim[qi % 2]
                    p_re, p_im = gre[(qi + 1) % 2], gim[(qi + 1) % 2]
                    u1, u2, u3, u4 = uu[qi % 2]
                    if qi > 0:
                        a, b_, e_, f_ = cr
                        self.tt("dve", a, a.ap, p_re, v3(p_re, last), c["c1"], c1, ALU.mult)
                        self.tt("dve", b_, b_.ap, p_im, v3(p_im, last), c["s1"], s1, ALU.mult)
                        self.tt("dve", e_, e_.ap, p_re, v3(p_re, last), c["s1"], s1, ALU.mult)
                        self.tt("dve", f_, f_.ap, p_im, v3(p_im, last), c["c1"], c1, ALU.mult)
                        self.tt("dve", a, a.ap, a, a.ap, b_, b_.ap, ALU.subtract)
                        self.tt("dve", e_, e_.ap, e_, e_.ap, f_, f_.ap, ALU.add)
                        self.tt("dve", xr, v3(xr, first), xr, v3(xr, first), a, a.ap, ALU.add)
                        self.tt("dve", xi_, v3(xi_, first), xi_, v3(xi_, first), e_, e_.ap, ALU.add)
                    rv = (lambda ap: ap) if d == 0 else (lambda ap: ap[:, ::-1])
                    self.op("dve", lambda g: g.tensor_tensor_scan(rv(g_re.ap), rv(Rm.ap), rv(xr.ap), 0.0, ALU.mult, ALU.add), [Rm, xr], [g_re])
                    self.op("dve", lambda g: g.tensor_tensor_scan(rv(g_im.ap), rv(Rm.ap), rv(xi_.ap), 0.0, ALU.mult, ALU.add), [Rm, xi_], [g_im])
                    self.tt("dve", u1, u1.ap, cs, cs.ap, g_re, g_re.ap, ALU.mult)
                    self.tt("dve", u2, u2.ap, nsn, nsn.ap, g_im, g_im.ap, ALU.mult)
                    self.tt("dve", u3, u3.ap, nsn, nsn.ap, g_re, g_re.ap, ALU.mult)
                    self.tt("dve", u4, u4.ap, ncs, ncs.ap, g_im, g_im.ap, ALU.mult)

                def s2_pe(qi):
                    t0 = seq[qi] * TC
                    u1, u2, u3, u4 = uu[qi % 2]
                    PY = PYs[qi % 2]
                    for o in range(2):
                        k = 0
                        for i in range(o * 4, o * 4 + 4):
                            for ri, ht in ((0, u1), (0, u2), (1, u3), (1, u4)):
                                self.mm(PY, PY.ap[:, o * TC:(o + 1) * TC], Cr, Cr.ap[:, 2 * i + ri, :], ht, ht.ap[:, i * TC:(i + 1) * TC],
                                        start=(k == 0), stop=(k == 15))
                                k += 1
                    py3 = PY.ap[:, 0:2 * TC].rearrange("p (o t) -> p o t", t=TC)
                    if d == 0:
                        self.cp("act", yt, yt.ap[:, :, t0:t0 + TC], PY, py3)
                    else:
                        self.tt("dve", yt, yt.ap[:, :, t0:t0 + TC], yt, yt.ap[:, :, t0:t0 + TC], PY, py3, ALU.add)

                s1_pe(0)
                s1_dve(0)
                for qi in range(18):
                    if qi + 1 < 18:
                        s1_pe(qi + 1)
                    s2_dve(qi)
                    s2_pe(qi)
                    if qi + 1 < 18:
                        s1_dve(qi + 1)
            for o in range(2):
                oc = 2 * sg + o
                self.ld(ustg, ustg.ap, self.PROJT[oc * 128:(oc + 1) * 128, :])
                self.stt("dve", yt, yt.ap[:, o, :], ustg, ustg.ap, dsk.ap[:, oc:oc + 1], yt, yt.ap[:, o, :], ALU.mult, ALU.add, R=[dsk])
                self.act(ustg, ustg.ap, yt, yt.ap[:, o, :], AF.Gelu_apprx_tanh)
                self.st(self.GT[oc * 128:(oc + 1) * 128, :], ustg, ustg.ap)
        self.new_phase()
        gR = self.tile([128, 8, NT], F32R)
        wgR = self.tile([128, 8, 1024], F32R)
        stg = [self.tile([128, 1024]) for _ in range(2)]
        Gv = self.GT.rearrange("(kc p) t -> p kc t", p=128)
        Wg = wglu[j].rearrange("(kc p) n -> p kc n", p=128)
        for kc in range(8):
            x = stg[kc % 2]
            self.ld(x, x.ap, Wg[:, kc, :])
            self.cp(self.alt(), wgR, wgR.ap[:, kc, :], x, x.ap)
        k = 0
        for kc in range(8):
            for (t0, n, s) in BLKS:
                x = stg[k % 2]; k += 1
                self.ld(x, x.ap[:, 0:n], Gv[:, kc, t0:t0 + n])
                self.cp(self.alt(), gR, gR.ap[:, kc, t0:t0 + n], x, x.ap[:, 0:n])
        gF = [self.tile([128, 512]) for _ in range(2)]
        zs = [self.tile([128, 512]) for _ in range(2)]
        tg = [self.tile([128, 512]) for _ in range(2)]
        ob = [self.tile([128, 512]) for _ in range(2)]
        k = 0
        for ncz in range(8):
            for (t0, n, s) in BLKS:
                ps = self.PS[k % 4]
                g_, z_, t_, o_ = gF[k % 2], zs[k % 2], tg[k % 2], ob[k % 2]
                k += 1
                self.ld(g_, g_.ap[:, 0:n], self.GT[ncz * 128:(ncz + 1) * 128, t0:t0 + n])
                self.ld(z_, z_.ap[:, 0:n], self.PROJT[1024 + ncz * 128:1024 + (ncz + 1) * 128, t0:t0 + n])
                for kc in range(8):
                    self.mm(ps, ps.ap[:, 0:n], wgR, wgR.ap[:, kc, ncz * 128:(ncz + 1) * 128], gR, gR.ap[:, kc, t0:t0 + n], start=(kc == 0), stop=(kc == 7))
                self.act(t_, t_.ap[:, 0:n], ps, ps.ap[:, 0:n], AF.Sigmoid)
                self.tt("dve", t_, t_.ap[:, 0:n], t_, t_.ap[:, 0:n], g_, g_.ap[:, 0:n], ALU.mult)
                self.act(z_, z_.ap[:, 0:n], z_, z_.ap[:, 0:n], AF.Silu)
                self.tt("dve", o_, o_.ap[:, 0:n], t_, t_.ap[:, 0:n], z_, z_.ap[:, 0:n], ALU.mult)
                self.st(self.MRGT[ncz * 128:(ncz + 1) * 128, t0:t0 + n], o_, o_.ap[:, 0:n])

    def phase_da(self, j, daq, lamT, rope, lam_init):
        self.new_phase()
        dq = self.tile([128, 3])
        self.ld(dq, dq.ap, daq[j])
        sgc = self.tile([128, 1])
        self.ts("dve", sgc, sgc.ap, dq, dq.ap[:, 2:3], 1.0 - lam_init, None, ALU.mult)
        lt = self.tile([64, 4])
        self.ld(lt, lt.ap, lamT[j])
        pr = self.tile([64, 2])
        self.tt("dve", pr, pr.ap[:, 0:1], lt, lt.ap[:, 0:1], lt, lt.ap[:, 1:2], ALU.mult)
        self.tt("dve", pr, pr.ap[:, 1:2], lt, lt.ap[:, 2:3], lt, lt.ap[:, 3:4], ALU.mult)
        p6, p7 = self.PS[6], self.PS[7]
        self.mm(p6, p6.ap[:, 0:2], self.cst, self.cst.ap[0:64, 1, :], pr, pr.ap)
        le = self.tile([128, 2])
        self.act(le, le.ap, p6, p6.ap[:, 0:2], AF.Exp)
        nlam = self.tile([128, 1])
        self.tt("dve", nlam, nlam.ap, le, le.ap[:, 1:2], le, le.ap[:, 0:1], ALU.subtract)
        self.ts("dve", nlam, nlam.ap, nlam, nlam.ap, -lam_init, None, ALU.add)
        cosT = self.tile([128, 2048]); sinT = self.tile([128, 2048])
        self.ld(cosT, cosT.ap, rope[0]); self.ld(sinT, sinT.ap, rope[1])
        q0 = [self.tile([128, NT], F32R) for _ in range(2)]
        q1 = [self.tile([128, NT], F32R) for _ in range(2)]
        kr = [self.tile([128, NT], F32R) for _ in range(2)]
        vr = [self.tile([128, 18, 128], F32R) for _ in range(2)]
        zd = [self.tile([128, NT]) for _ in range(2)]
        mo = [self.tile([128, NT]) for _ in range(2)]
        for q in q0 + q1:
            self.ts("dve", q, q.ap[:, 0:2048], cosT, cosT.ap, 0.0, None, ALU.mult)
            self.ts("dve", q, q.ap[:, 2048:NT], cosT, cosT.ap[:, 0:NT - 2048], 0.0, None, ALU.mult)
        qraw = self.tile([128, NT]); kraw = self.tile([128, NT])
        vraw = self.tile([128, 18, 128])
        sq = [self.tile([128, 512], F32R) for _ in range(2)]
        rstd = [self.tile([128, 512]) for _ in range(2)]
        qg = [self.tile([128, 512], F32R) for _ in range(2)]
        t1 = [self.tile([128, 512]) for _ in range(2)]
        t2 = [self.tile([128, 512]) for _ in range(2)]
        pT = [self.tile([128, 512], F32R) for _ in range(3)]
        rr = [self.tile([128, 512]) for _ in range(2)]
        am = [[self.tile([128, 512]) for _ in range(2)] for _ in range(2)]
        rs2 = self.tile([128, 512])
        VT = self.VTOK.rearrange("(tt p) n -> p tt n", p=128)
        qblks = [(0, 256, [0, 1])] + [(256 + 512 * i, 512, list(range(18))) for i in range(4)]
        st_ = {"cnt": 0, "u": 0, "g": 0}

        ots = [self.tile([128, 512]) for _ in range(2)]

        def load_head(hd):
            b_ = hd % 2
            self.ld(qraw, qraw.ap, self.PROJT[2048 + hd * 128:2048 + (hd + 1) * 128, :])
            self.ld(kraw, kraw.ap, self.PROJT[3072 + hd * 128:3072 + (hd + 1) * 128, :])
            self.ld(zd[b_], zd[b_].ap, self.PROJT[5120 + hd * 128:5120 + (hd + 1) * 128, :])
            self.ld(vraw, vraw.ap, VT[:, :, hd * 128:(hd + 1) * 128])
            self.cp("pool", vr[b_], vr[b_].ap, vraw, vraw.ap)

        def silu_head(hd):
            b_ = hd % 2
            self.act(zd[b_], zd[b_].ap, zd[b_], zd[b_].ap, AF.Silu)

        from collections import deque
        bg = deque()

        def prep_stages(hd, which, bi):
            b_ = hd % 2
            raw = qraw if which == 0 else kraw
            t0, n, s = BLKS[bi]
            u = st_["u"]; st_["u"] += 1
            q_, rs_, qg_, t1_, t2_ = sq[u % 2], rstd[u % 2], qg[u % 2], t1[u % 2], t2[u % 2]
            qgf = qg_.ap.bitcast(F32)
            r0_ = t0 - NCTX

            def s0():
                self.tt("dve", q_, q_.ap[:, 0:n], raw, raw.ap[:, t0:t0 + n], raw, raw.ap[:, t0:t0 + n], ALU.mult)

            def s1():
                self.mm(p6, p6.ap[:, 0:n], self.blk64r, self.blk64r.ap, q_, q_.ap[:, 0:n])

            def s2():
                self.rsqrt_from(rs_, rs_.ap[:, 0:n], p6, p6.ap[:, 0:n], 1.0 / 64)

            def s3():
                self.stt("dve", qg_, qg_.ap[:, 0:n], raw, raw.ap[:, t0:t0 + n], dq.ap[:, which:which + 1], rs_, rs_.ap[:, 0:n],
                         ALU.mult, ALU.mult, R=[dq])
                if s != 0:
                    if which == 0:
                        self.cp("dve", q0[b_], q0[b_].ap[0:64, t0:t0 + n], qg_, qgf[0:64, 0:n])
                        self.cp("dve", q1[b_], q1[b_].ap[64:128, t0:t0 + n], qg_, qgf[64:128, 0:n])
                    else:
                        self.cp("dve", kr[b_], kr[b_].ap[:, t0:t0 + n], qg_, qgf[:, 0:n])

            def s4():
                self.mm(p7, p7.ap[:, 0:n], self.protr, self.protr.ap, qg_, qg_.ap[:, 0:n])
                self.tt("dve", t1_, t1_.ap[:, 0:n], qg_, qgf[:, 0:n], cosT, cosT.ap[:, r0_:r0_ + n], ALU.mult)

            def s5():
                self.tt("dve", t2_, t2_.ap[:, 0:n], p7, p7.ap[:, 0:n], sinT, sinT.ap[:, r0_:r0_ + n], ALU.mult)
                if which == 0:
                    self.tt("dve", q0[b_], q0[b_].ap[0:64, t0:t0 + n], t1_, t1_.ap[0:64, 0:n], t2_, t2_.ap[0:64, 0:n], ALU.add)
                    self.tt("dve", q1[b_], q1[b_].ap[64:128, t0:t0 + n], t1_, t1_.ap[64:128, 0:n], t2_, t2_.ap[64:128, 0:n], ALU.add)
                else:
                    self.tt("dve", kr[b_], kr[b_].ap[:, t0:t0 + n], t1_, t1_.ap[:, 0:n], t2_, t2_.ap[:, 0:n], ALU.add)
            return [s0, s1, s2, s3] + ([s4, s5] if s == 0 else [])

        sqt = self.tile([128, 512], F32R)

        def tail_stages(hd, t0, n, a0, a1, qbi, last):
            b_ = hd % 2
            ot = ots[qbi % 2]

            def t0_():
                self.stt("dve", ot, ot.ap[:, 0:n], a1, a1.ap[:, 0:n], nlam.ap[:, 0:1], a0, a0.ap[:, 0:n], ALU.mult, ALU.add, R=[nlam])
                self.tt("dve", sqt, sqt.ap[:, 0:n], ot, ot.ap[:, 0:n], ot, ot.ap[:, 0:n], ALU.mult)

            def t1_():
                self.mm(p6, p6.ap[:, 0:n], self.onesr, self.onesr.ap, sqt, sqt.ap[:, 0:n])

            def t2_():
                self.rsqrt_from(rs2, rs2.ap[:, 0:n], p6, p6.ap[:, 0:n], 1.0 / 128)

            def t3_():
                self.stt("dve", ot, ot.ap[:, 0:n], ot, ot.ap[:, 0:n], sgc.ap[:, 0:1], rs2, rs2.ap[:, 0:n], ALU.mult, ALU.mult, R=[sgc])
                self.tt("dve", mo[b_], mo[b_].ap[:, t0:t0 + n], ot, ot.ap[:, 0:n], zd[b_], zd[b_].ap[:, t0:t0 + n], ALU.mult)
                if last:
                    self.st(self.MRGT[1024 + hd * 128:1024 + (hd + 1) * 128, :], mo[b_], mo[b_].ap)
            return [t0_, t1_, t2_, t3_]

        def flush():
            while bg:
                bg.popleft()()

        units = [(w, bi) for w in range(2) for bi in range(5)]
        load_head(0)
        silu_head(0)
        for w, bi in units:
            for f in prep_stages(0, w, bi):
                f()
        for hd in range(8):
            b_ = hd % 2
            if hd + 1 < 8:
                load_head(hd + 1)
            ui = 0
            for qbi, (t0, n, keys) in enumerate(qblks):
                for m in range(2):
                    qm = q0[b_] if m == 0 else q1[b_]
                    g = st_["g"]; st_["g"] += 1
                    Om, RSm = self.PS[2 + 2 * (g % 2)], self.PS[3 + 2 * (g % 2)]
                    nk = len(keys)
                    base = st_["cnt"]; st_["cnt"] += nk

                    def smm(ki):
                        sp = self.PS[(base + ki) % 2]
                        kt = keys[ki]
                        self.mm(sp, sp.ap[:, 0:n], kr[b_], kr[b_].ap[:, kt * 128:(kt + 1) * 128], qm, qm.ap[:, t0:t0 + n])
                    smm(0)
                    for ki, kt in enumerate(keys):
                        sp = self.PS[(base + ki) % 2]
                        p_ = pT[(base + ki) % 3]
                        if ki + 1 < nk:
                            smm(ki + 1)
                        self.act(p_, p_.ap[:, 0:n], sp, sp.ap[:, 0:n], AF.Exp, scale=0.125)
                        self.mm(Om, Om.ap[:, 0:n], vr[b_], vr[b_].ap[:, kt, :], p_, p_.ap[:, 0:n], start=(ki == 0), stop=(ki == nk - 1))
                        self.mm(RSm, RSm.ap[:, 0:n], self.onesr, self.onesr.ap, p_, p_.ap[:, 0:n], start=(ki == 0), stop=(ki == nk - 1))
                        if bg:
                            bg.popleft()()
                    r_, a_ = rr[m], am[(g // 2) % 2][m]
                    self.op("dve", lambda g_, r_=r_, RSm=RSm, n=n: g_.reciprocal(r_.ap[:, 0:n], RSm.ap[:, 0:n]), [RSm], [r_])
                    self.tt("dve", a_, a_.ap[:, 0:n], Om, Om.ap[:, 0:n], r_, r_.ap[:, 0:n], ALU.mult)
                    if hd + 1 < 8 and ui < len(units):
                        if ui == 1:
                            bg.append(lambda hd=hd: silu_head(hd + 1))
                        bg.extend(prep_stages(hd + 1, *units[ui])); ui += 1
                a0, a1 = am[((st_["g"] - 1) // 2) % 2]
                bg.extend(tail_stages(hd, t0, n, a0, a1, qbi, qbi == len(qblks) - 1))
            flush()

    def phase_gla(self, j, glag, gmask):
        self.new_phase()
        CH = 128
        NCH = NT // CH
        scale = 256.0 ** -0.5
        gg = self.tile([128, 4])
        self.ld(gg, gg.ap, glag[j])
        gm = [self.tile([128, NT]) for _ in range(2)]
        for d in range(2):
            self.ld(gm[d], gm[d].ap, gmask[d])
        qraw = self.tile([128, 2, NT]); kraw = self.tile([128, 2, NT])
        OT = self.tile([128, 4, NT])
        la = self.tile([128, NT]); bt = self.tile([128, NT])
        qd = self.tile([128, 2, NT], F32R); kd = self.tile([128, 2, NT], F32R)
        ebend = self.tile([128, 2, NCH])
        S = self.tile([128, 2, 512]); Sr = self.tile([128, 2, 512], F32R)
        vraw = [self.tile([CH, 512]) for _ in range(2)]
        vr = [self.tile([CH, 512], F32R) for _ in range(2)]
        scTs = [self.tile([CH, CH], F32R) for _ in range(3)]
        kcts = [self.tile([128, 2, CH]) for _ in range(3)]
        kendTs = [self.tile([CH, 256], F32R) for _ in range(3)]
        sq = [self.tile([128, 512], F32R) for _ in range(2)]
        rstd = self.tile([128, 512])
        zt = [self.tile([128, 512]) for _ in range(2)]
        yt = [self.tile([128, 512]) for _ in range(2)]
        ob = [self.tile([128, 512]) for _ in range(2)]
        psSs, psT, psO, psU, psN = [self.PS[0], self.PS[7]], self.PS[1], [self.PS[2], self.PS[3]], [self.PS[4], self.PS[5]], self.PS[6]
        nctx = NCTX // CH
        fseq = list(range(NCH))
        bseq = list(range(nctx - 1, -1, -1)) + list(range(NCH - 1, nctx - 1, -1))
        vi = 0
        for hd in range(4):
            self.ld(qraw, qraw.ap, self.PROJT[hd * 256:(hd + 1) * 256, :].rearrange("(c p) t -> p c t", p=128))
            self.ld(kraw, kraw.ap, self.PROJT[1024 + hd * 256:1024 + (hd + 1) * 256, :].rearrange("(c p) t -> p c t", p=128))
            for d in range(2):
                last = CH - 1 if d == 0 else 0
                for dkc in range(2):
                    r0 = hd * 256 + dkc * 128
                    self.ld(la, la.ap, self.LAT[d, r0:r0 + 128, :])
                    if d == 0:
                        self.op("dve", lambda g: g.tensor_tensor_scan(bt.ap, gm[0].ap, la.ap, 0.0, ALU.mult, ALU.add), [gm[0], la], [bt])
                    else:
                        self.op("dve", lambda g: g.tensor_tensor_scan(bt.ap[:, ::-1], gm[1].ap[:, ::-1], la.ap[:, ::-1], 0.0, ALU.mult, ALU.add), [gm[1], la], [bt])
                    self.act(la, la.ap, bt, bt.ap, AF.Exp)
                    self.stt("dve", qd, qd.ap[:, dkc, :], qraw, qraw.ap[:, dkc, :], scale, la, la.ap, ALU.mult, ALU.mult)
                    self.cp("dve", ebend, ebend.ap[:, dkc, :], la, la.ap[:, last::CH])
                    self.act(bt, bt.ap, bt, bt.ap, AF.Exp, scale=-1.0)
                    self.tt("dve", kd, kd.ap[:, dkc, :], kraw, kraw.ap[:, dkc, :], bt, bt.ap, ALU.mult)
                self.op("pool", lambda g: g.memset(S.ap, 0.0), W=[S])
                self.ts("dve", Sr, Sr.ap, S, S.ap, 0.0, None, ALU.mult)
                tri = self.cst.ap[0:CH, 4 + d, 0:CH]
                kdf, qdf = kd.ap.bitcast(F32), qd.ap.bitcast(F32)
                for qi, c in enumerate(fseq if d == 0 else bseq):
                    t0 = c * CH
                    va, v_ = vraw[vi % 2], vr[vi % 2]
                    vi += 1
                    self.ld(va, va.ap, self.VTOK[t0:t0 + CH, hd * 512:(hd + 1) * 512])
                    self.cp("act", v_, v_.ap, va, va.ap)
                    scT, kct, kendT, psS = scTs[qi % 3], kcts[qi % 3], kendTs[qi % 3], psSs[qi % 2]
                    tof = (qi % 2) * 256
                    for dkc in range(2):
                        self.mm(psS, psS.ap[0:CH, 0:CH], kd, kd.ap[:, dkc, t0:t0 + CH], qd, qd.ap[:, dkc, t0:t0 + CH], start=(dkc == 0), stop=(dkc == 1))
                    self.tt("dve", scT, scT.ap, psS, psS.ap[0:CH, 0:CH], self.cst, tri, ALU.mult)
                    for dkc in range(2):
                        self.act(kct, kct.ap[:, dkc, :], kd, kdf[:, dkc, t0:t0 + CH], AF.Copy, scale=ebend.ap[:, dkc, c:c + 1], R=[ebend])
                        self.tp(psT, psT.ap[0:CH, tof + dkc * 128:tof + (dkc + 1) * 128], kct, kct.ap[:, dkc, :], self.ident)
                    self.cp("act", kendT, kendT.ap, psT, psT.ap[0:CH, tof:tof + 256])
                    po = psO[qi % 2]
                    for dvc in range(4):
                        oap = po.ap[:, dvc * CH:(dvc + 1) * CH]
                        self.mm(po, oap, v_, v_.ap[:, dvc * 128:(dvc + 1) * 128], scT, scT.ap, start=True, stop=False)
                        self.mm(po, oap, Sr, Sr.ap[:, 0, dvc * 128:(dvc + 1) * 128], qd, qd.ap[:, 0, t0:t0 + CH], start=False, stop=False)
                        self.mm(po, oap, Sr, Sr.ap[:, 1, dvc * 128:(dvc + 1) * 128], qd, qd.ap[:, 1, t0:t0 + CH], start=False, stop=True)
                    po3 = po.ap[:, 0:4 * CH].rearrange("p (a b) -> p a b", b=CH)
                    if d == 0:
                        self.cp("act", OT, OT.ap[:, :, t0:t0 + CH], po, po3)
                    else:
                        self.tt("dve", OT, OT.ap[:, :, t0:t0 + CH], OT, OT.ap[:, :, t0:t0 + CH], po, po3, ALU.add)
                    for dkc in range(2):
                        pu = psU[dkc]
                        self.mm(pu, pu.ap, kendT, kendT.ap[:, dkc * 128:(dkc + 1) * 128], v_, v_.ap)
                        self.stt("dve", S, S.ap[:, dkc, :], S, S.ap[:, dkc, :], ebend.ap[:, dkc, c:c + 1], pu, pu.ap, ALU.mult, ALU.add, R=[ebend])
                        self.cp("dve", Sr, Sr.ap[:, dkc, :], S, S.ap[:, dkc, :])
            k = 0
            for (t0, n, s) in BLKS:
                for dvc in range(4):
                    q_ = sq[dvc % 2]
                    self.act(q_, q_.ap[:, 0:n], OT, OT.ap[:, dvc, t0:t0 + n], AF.Square)
                    self.mm(psN, psN.ap[:, 0:n], self.onesr, self.onesr.ap, q_, q_.ap[:, 0:n], start=(dvc == 0), stop=(dvc == 3))
                self.rsqrt_from(rstd, rstd.ap[:, 0:n], psN, psN.ap[:, 0:n], 1.0 / 512)
                for dvc in range(4):
                    z_, y_, o_ = zt[k % 2], yt[k % 2], ob[k % 2]
                    k += 1
                    zr = 4096 + hd * 512 + dvc * 128
                    self.ld(z_, z_.ap[:, 0:n], self.PROJT[zr:zr + 128, t0:t0 + n])
                    self.stt("dve", y_, y_.ap[:, 0:n], OT, OT.ap[:, dvc, t0:t0 + n], gg.ap[:, dvc:dvc + 1], rstd, rstd.ap[:, 0:n], ALU.mult, ALU.mult, R=[gg])
                    self.act(z_, z_.ap[:, 0:n], z_, z_.ap[:, 0:n], AF.Silu)
                    self.tt("dve", o_, o_.ap[:, 0:n], y_, y_.ap[:, 0:n], z_, z_.ap[:, 0:n], ALU.mult)
                    mr = hd * 512 + dvc * 128
                    self.st(self.MRGT[mr:mr + 128, t0:t0 + n], o_, o_.ap[:, 0:n])


def _consts():
    c = np.zeros((6, 128, 128), np.float32)
    c[0] = np.eye(128)
    c[1] = 1.0
    c[2, :64, :64] = 1.0
    c[2, 64:, 64:] = 1.0
    for m in range(2):
        o = m * 64
        for i in range(16):
            c[3, o + 16 + i, o + i] = -1.0
            c[3, o + i, o + 16 + i] = 1.0
            c[3, o + 48 + i, o + 32 + i] = -1.0
            c[3, o + 32 + i, o + 48 + i] = 1.0
    s = np.arange(128)
    c[4] = (s[:, None] <= s[None, :])
    c[5] = (s[:, None] >= s[None, :])
    return c


def _rope_tables():
    GRID_W = 64
    L = 2048
    row = np.repeat(np.arange(L // GRID_W, dtype=np.float32), GRID_W)
    col = np.tile(np.arange(GRID_W, dtype=np.float32), L // GRID_W)
    n_freq = 16
    inv_freq = (np.float32(10000.0) ** (-np.arange(n_freq, dtype=np.float32) / np.float32(n_freq))).astype(np.float32)
    ang_r = row[:, None] * inv_freq
    ang_c = col[:, None] * inv_freq
    ang = np.concatenate([ang_r, ang_r, ang_c, ang_c], axis=-1).astype(np.float32)
    cos, sin = np.cos(ang).astype(np.float32), np.sin(ang).astype(np.float32)
    t = np.zeros((2, 128, L), np.float32)
    t[0] = np.concatenate([cos.T, cos.T], axis=0)
    t[1] = np.concatenate([sin.T, sin.T], axis=0)
    return t


def _prep_shared(inp):
    f = lambda a: np.ascontiguousarray(a, dtype=np.float32)
    sh = {}
    sh["ada_w"] = f(inp["ada_w"])
    sh["adabT"] = f(inp["ada_b"].reshape(DEPTH, 48, 128).transpose(0, 2, 1))
    sh["ngT"] = f(inp["norm_g"].reshape(DEPTH, 16, 128).transpose(0, 2, 1))
    sh["w_in"] = f(np.stack([inp["ev_w_in"][0], inp["od_w_in"][0], inp["ev_w_in"][1], inp["od_w_in"][1]]))
    sh["w_out"] = f(np.stack([inp["ev_w_out"][0], inp["od_w_out"][0], inp["ev_w_out"][1], inp["od_w_out"][1]]))
    sh["consts"] = _consts()
    p = np.zeros((2, 2, 128, 3, 32), np.float32)
    for k, name in enumerate(["s5_a_re", "s5_a_im"]):
        a = inp[name].reshape(2, 2, 32, 2, 64)
        p[:, :, :, k, :] = a.transpose(0, 1, 3, 4, 2).reshape(2, 2, 128, 32)
    ldt = np.broadcast_to(inp["s5_log_dt"].reshape(2, 2, 32, 2, 1), (2, 2, 32, 2, 64))
    p[:, :, :, 2, :] = ldt.transpose(0, 1, 3, 4, 2).reshape(2, 2, 128, 32)
    sh["s5p"] = p
    Bm = np.zeros((2, 2, 2, 32, 128, 128), np.float32)
    Cm = np.zeros((2, 2, 2, 32, 128, 128), np.float32)
    for ri, (bn, cn) in enumerate([("s5_b_re", "s5_c_re"), ("s5_b_im", "s5_c_im")]):
        b = inp[bn]
        c = inp[cn]
        for st in range(32):
            for gi in range(2):
                g = 2 * st + gi
                g8 = g % 8
                Bm[:, :, ri, st, g8 * 16:(g8 + 1) * 16, gi * 64:(gi + 1) * 64] = b[:, :, g].transpose(0, 1, 3, 2)
                Cm[:, :, ri, st, gi * 64:(gi + 1) * 64, g8 * 16:(g8 + 1) * 16] = c[:, :, g].transpose(0, 1, 3, 2)
    sh["s5B"], sh["s5C"] = Bm, Cm
    sh["s5d"] = f(inp["s5_d"].reshape(2, 8, 128).transpose(0, 2, 1))
    tau = np.zeros((4, 128, TC), np.float32)
    tau[0] = np.arange(TC)[None, :]
    tau[1] = (TC - 1 - np.arange(TC))[None, :]
    tau[2] = 1.0; tau[2, :, 0] = 0.0
    tau[3] = 1.0; tau[3, :, TC - 1] = 0.0
    sh["s5tau"] = tau
    sh["wglu"] = f(inp["s5_w_glu"])
    dq = np.zeros((2, 128, 3), np.float32)
    dq[:, :, 0] = np.tile(inp["da_qn_g"], (1, 2))
    dq[:, :, 1] = np.tile(inp["da_kn_g"], (1, 2))
    dq[:, :, 2] = inp["da_subln_g"]
    sh["daq"] = dq
    sh["lamT"] = f(inp["da_lam"].transpose(0, 2, 1))
    sh["rope"] = _rope_tables()
    sh["wa1"] = f(inp["gla_wa1"])
    sh["wa2"] = f(inp["gla_wa2"])
    sh["baT"] = f(inp["gla_ba"].reshape(2, 2, 8, 128).transpose(0, 1, 3, 2))
    sh["glag"] = f(inp["gla_norm_g"].reshape(2, 4, 128).transpose(0, 2, 1))
    gm = np.ones((2, 128, NT), np.float32)
    gm[0, :, 0::128] = 0.0
    gm[1, :, 127::128] = 0.0
    sh["gmask"] = gm
    return sh


def _prep_core(inp, b):
    d = {}
    d["xin"] = np.ascontiguousarray(np.concatenate([inp["ctx"][b], inp["x"][b]], axis=0), dtype=np.float32)
    cT = np.zeros((128, 16, 2), np.float32)
    cT[:, :, 0] = inp["c"][b].reshape(16, 128).T
    cT[:, :, 1] = inp["c_ctx"].reshape(16, 128).T
    d["cT"] = cT
    return d


_NC_CACHE = {}


def kernel(**inputs):
    inp = {k: np.asarray(v) for k, v in inputs.items()}
    if "full" not in _NC_CACHE:
        _NC_CACHE["full"] = KB().build()
    nc = _NC_CACHE["full"]
    sh = _prep_shared(inp)
    in_maps = []
    for core in range(8):
        m = dict(sh)
        m.update(_prep_core(inp, core % 4))
        in_maps.append(m)
    res = run_bass_kernel_spmd(nc, in_maps, core_ids=list(range(8)))
    out = np.stack([np.asarray(res.results[b]["out"]) for b in range(4)], axis=0)
    return out.astype(np.float32)
```

```python
import math
import numpy as np
import concourse.bass as bass
import concourse.mybir as mybir
from concourse.bass_utils import run_bass_kernel_spmd

F32 = mybir.dt.float32
F32R = mybir.dt.float32r
I32 = mybir.dt.int32
AF = mybir.ActivationFunctionType
ALU = mybir.AluOpType

D = 2048
NT = 2304
NCTX = 256
DEPTH = 4
EPS = 1e-6
PI = math.pi
BLKS = [(0, 256, 1)] + [(256 + 512 * i, 512, 0) for i in range(4)]
TC = 128


class Buf:
    __slots__ = ("w", "r")

    def __init__(self):
        self.w = None
        self.r = {}


class T:
    __slots__ = ("ap", "b")

    def __init__(self, ap):
        self.ap = ap
        self.b = Buf()

    def __getitem__(self, k):
        return self.ap[k]


class FW:
    NDMA = 24
    SEM_ROLL = 30000

    def __init__(self, nc):
        self.nc = nc
        self.engs = {"pe": nc.tensor, "act": nc.scalar, "dve": nc.vector, "pool": nc.gpsimd, "sp": nc.sync}
        self.ops = {e: [] for e in self.engs}
        self.sems, self.cnt, self.cur, self.owner = {}, {}, {}, {}
        self.seen = {e: {} for e in self.engs}
        self.nsem = 0
        for e in self.engs:
            self._new_csem(e)
        self.dma_keys = []
        for i in range(self.NDMA):
            k = f"dma{i}"
            self.sems[k] = nc.alloc_semaphore(k)
            self.cnt[k] = 0
            self.dma_keys.append(k)
        self.dma_rr = 0
        self.out_tickets = []
        self.n_ins = 0

    def _new_csem(self, e):
        k = f"c_{e}_{self.nsem}"
        self.nsem += 1
        self.sems[k] = self.nc.alloc_semaphore(k)
        self.cnt[k] = 0
        self.cur[e] = k
        self.owner[k] = e

    def _waits(self, e, reads, writes, extra=()):
        need = {}

        def add(k, v):
            if need.get(k, 0) < v:
                need[k] = v
        for t in reads:
            if t.b.w is not None:
                add(*t.b.w)
        for t in writes:
            if t.b.w is not None:
                add(*t.b.w)
            for k, v in t.b.r.items():
                add(k, v)
        for k, v in extra:
            add(k, v)
        out = []
        seen = self.seen[e]
        for k, v in need.items():
            if seen.get(k, 0) >= v:
                continue
            if self.owner.get(k) == e and e == "pe":
                continue
            seen[k] = v
            out.append((k, v))
        return out

    def _mark(self, t, reads, writes):
        k, v = t
        for x in reads:
            if x.b.r.get(k, 0) < v:
                x.b.r[k] = v
        for x in writes:
            x.b.w = t
            x.b.r = {}

    def op(self, e, fn, R=(), W=()):
        waits = self._waits(e, R, W)
        k = self.cur[e]
        if self.cnt[k] >= self.SEM_ROLL:
            self._new_csem(e)
            k = self.cur[e]
        self.cnt[k] += 1
        t = (k, self.cnt[k])
        self.ops[e].append((waits, fn, (k, 1)))
        self._mark(t, R, W)
        self.n_ins += 1
        return t

    def dma(self, out, in_, R=(), W=(), q="sp", is_output=False, **kw):
        k = self.dma_keys[self.dma_rr % self.NDMA]
        self.dma_rr += 1
        extra = [(k, self.cnt[k])] if self.cnt[k] > 0 else []
        waits = self._waits(q, R, W, extra)
        self.cnt[k] += 16
        t = (k, self.cnt[k])

        def fn(eng, out=out, in_=in_, kw=kw):
            return eng.dma_start(out=out, in_=in_, **kw)
        self.ops[q].append((waits, fn, (k, 16)))
        self._mark(t, R, W)
        if is_output:
            self.out_tickets.append(t)
        self.n_ins += 1
        return t

    def barrier(self):
        allv = [(k, v) for k, v in self.cnt.items() if v > 0]
        for e in self.engs:
            seen = self.seen[e]
            waits = []
            for k, v in allv:
                if self.owner.get(k) == e and e == "pe":
                    continue
                if seen.get(k, 0) < v:
                    seen[k] = v
                    waits.append((k, v))
            if waits:
                self.ops[e].append((waits, None, None))

    def finish(self):
        self.barrier()
        nc, sems = self.nc, self.sems
        with nc.Block() as block:
            def mk(e):
                def body(eng):
                    for waits, fn, inc in self.ops[e]:
                        for k, v in waits:
                            eng.wait_ge(sems[k], v)
                        if fn is not None:
                            fn(eng).then_inc(sems[inc[0]], inc[1])
                return body
            block.sync(mk("sp"))
            block.scalar(mk("act"))
            block.vector(mk("dve"))
            block.gpsimd(mk("pool"))
            block.tensor(mk("pe"))


class KB:
    ARENA_WORDS = 52000

    def __init__(self, nlayers=DEPTH, debug=()):
        self.nlayers = nlayers
        self.debug = set(debug)
        nc = self.nc = bass.Bass("TRN2", target_bir_lowering=False)
        self.fw = FW(nc)
        arena = nc.alloc_sbuf_tensor("arena", [128, self.ARENA_WORDS], F32)
        self.arena_addr = int(nc.lookup_mloc(arena).addr)
        self.ntile = 0
        self.base = 0
        self.off = 0
        self.psum = nc.alloc_psum_tensor("psum", [128, 4096], F32).ap()
        self.PS = [T(self.psum[:, i * 512:(i + 1) * 512]) for i in range(8)]
        self.din = {}
        self.rr = 0

    def inp(self, name, shape):
        ap = self.nc.dram_tensor(name, list(shape), F32, kind="ExternalInput").ap()
        self.din[name] = ap
        return ap

    def scratch(self, name, shape):
        kind = "ExternalOutput" if name in self.debug else "Internal"
        return self.nc.dram_tensor(name, list(shape), F32, kind=kind).ap()

    def tile(self, shape, dt=F32):
        n = int(np.prod(shape[1:]))
        assert self.off + n <= self.ARENA_WORDS, ("SBUF overflow", self.off, n)
        self.ntile += 1
        h = self.nc.alloc_sbuf_tensor_at(f"t{self.ntile}", [int(x) for x in shape], dt, offset=self.arena_addr + 4 * self.off)
        self.off += (n + 7) // 8 * 8
        return T(h.ap())

    def new_phase(self):
        self.fw.barrier()
        self.off = self.base

    def op(self, e, fn, R=(), W=()):
        return self.fw.op(e, fn, R, W)

    def mm(self, out, o_ap, lt, l_ap, rt, r_ap, start=True, stop=True):
        self.fw.op("pe", lambda e: e.matmul(o_ap, lhsT=l_ap, rhs=r_ap, start=start, stop=stop), [lt, rt], [out])

    def tp(self, out, o_ap, it, i_ap, ident):
        self.fw.op("pe", lambda e: e.transpose(o_ap, i_ap, ident.ap[0:i_ap.shape[0], 0:i_ap.shape[0]]), [it, ident], [out])

    def tt(self, e, out, o_ap, a, a_ap, b, b_ap, op):
        self.fw.op(e, lambda g: g.tensor_tensor(o_ap, a_ap, b_ap, op), [a, b], [out])

    def ts(self, e, out, o_ap, a, a_ap, s1, s2, op0, op1=None, R=()):
        if op1 is None:
            self.fw.op(e, lambda g: g.tensor_scalar(o_ap, a_ap, s1, None, op0), [a] + list(R), [out])
        else:
            self.fw.op(e, lambda g: g.tensor_scalar(o_ap, a_ap, s1, s2, op0, op1), [a] + list(R), [out])

    def stt(self, e, out, o_ap, a, a_ap, sc, b, b_ap, op0, op1, R=()):
        e = "dve"
        self.fw.op(e, lambda g: g.scalar_tensor_tensor(o_ap, a_ap, sc, b_ap, op0, op1), [a, b] + list(R), [out])

    def act(self, out, o_ap, a, a_ap, func, bias=None, scale=None, R=()):
        kw = {}
        if bias is not None:
            kw["bias"] = bias
        if scale is not None:
            kw["scale"] = scale
        self.fw.op("act", lambda g: g.activation(o_ap, a_ap, func, **kw), [a] + list(R), [out])

    def cp(self, e, out, o_ap, a, a_ap):
        if e == "act":
            self.fw.op("act", lambda g: g.copy(o_ap, a_ap), [a], [out])
        else:
            self.fw.op(e, lambda g: g.tensor_copy(o_ap, a_ap), [a], [out])

    def alt(self, engines=("dve", "act")):
        self.rr += 1
        return engines[self.rr % len(engines)]

    def ld(self, out, o_ap, src, q="sp"):
        self.fw.dma(o_ap, src, W=[out], q=q)

    def st(self, dst, it, i_ap, q="act", is_output=False):
        self.fw.dma(dst, i_ap, R=[it], q=q, is_output=is_output)

    def rsqrt_from(self, out, o_ap, src, s_ap, inv_n):
        self.act(out, o_ap, src, s_ap, AF.Ln, bias=self.epsT.ap[:, 0:1], scale=inv_n, R=[self.epsT])
        self.act(out, o_ap, out, o_ap, AF.Exp, scale=-0.5)

    def build(self):
        nc = self.nc
        L = self.nlayers
        xin = self.inp("xin", [NT, D])
        cT_d = self.inp("cT", [128, 16, 2])
        ada_w = self.inp("ada_w", [DEPTH, D, 3 * D])
        adabT = self.inp("adabT", [DEPTH, 128, 48])
        ngT = self.inp("ngT", [DEPTH, 128, 16])
        w_in = self.inp("w_in", [DEPTH, D, 3 * D])
        w_out = self.inp("w_out", [DEPTH, D, D])
        consts = self.inp("consts", [6, 128, 128])
        s5p = self.inp("s5p", [2, 2, 128, 3, 32])
        s5B = self.inp("s5B", [2, 2, 2, 32, 128, 128])
        s5C = self.inp("s5C", [2, 2, 2, 32, 128, 128])
        s5d = self.inp("s5d", [2, 128, 8])
        s5tau = self.inp("s5tau", [4, 128, TC])
        wglu = self.inp("wglu", [2, 1024, 1024])
        daq = self.inp("daq", [2, 128, 3])
        lamT = self.inp("lamT", [2, 64, 4])
        rope = self.inp("rope", [2, 128, 2048])
        wa1 = self.inp("wa1", [2, 2, D, 16])
        wa2 = self.inp("wa2", [2, 2, 16, 1024])
        baT = self.inp("baT", [2, 2, 128, 8])
        glag = self.inp("glag", [2, 128, 4])
        gmask = self.inp("gmask", [2, 128, NT])
        out_d = nc.dram_tensor("out", [2048, D], F32, kind="ExternalOutput").ap()

        self.XT = self.scratch("XT", [D, NT])
        self.PROJT = self.scratch("PROJT", [3 * D, NT])
        self.VTOK = self.scratch("VTOK", [NT, D])
        self.MRGT = self.scratch("MRGT", [D, NT])
        self.GT = self.scratch("GT", [1024, NT])
        self.LAT = self.scratch("LAT", [2, 1024, NT])

        self.cst = self.tile([128, 6, 128])
        self.ld(self.cst, self.cst.ap, consts.rearrange("c p n -> p c n"))
        self.ident = T(self.cst.ap[:, 0, :]); self.ident.b = self.cst.b
        self.onesr = self.tile([128, 128], F32R)
        self.blk64r = self.tile([128, 128], F32R)
        self.protr = self.tile([128, 128], F32R)
        self.cp("dve", self.onesr, self.onesr.ap, self.cst, self.cst.ap[:, 1, :])
        self.cp("dve", self.blk64r, self.blk64r.ap, self.cst, self.cst.ap[:, 2, :])
        self.cp("dve", self.protr, self.protr.ap, self.cst, self.cst.ap[:, 3, :])
        self.epsT = self.tile([128, 1])
        self.op("dve", lambda g: g.memset(self.epsT.ap, EPS), W=[self.epsT])
        self.modv = self.tile([128, 48, 2])
        self.gs = self.tile([128, 16, 2])
        self.cT = self.tile([128, 16, 2])
        self.sc = self.tile([128, 16, 2])
        self.ld(self.cT, self.cT.ap, cT_d)
        self.act(self.sc, self.sc.ap, self.cT, self.cT.ap, AF.Silu)
        self.base = self.off

        self.phase_in_transpose(xin)
        for l in range(L):
            j = l // 2
            self.phase_mod(l, ada_w, adabT, ngT)
            if l % 2 == 0:
                self.phase_inproj(l, w_in, tok_chunks=range(32, 40))
                if "stop_proj" in self.debug:
                    break
                self.phase_s5(j, s5p, s5B, s5C, s5d, s5tau, wglu)
                if "stop_s5" in self.debug:
                    break
                lam_init = 0.8 - 0.6 * math.exp(-0.3 * l)
                self.phase_da(j, daq, lamT, rope, lam_init)
                if "stop_da" in self.debug:
                    break
            else:
                self.phase_inproj(l, w_in, tok_chunks=range(16, 32), gla=(j, wa1, wa2, baT))
                self.phase_gla(j, glag, gmask)
                if "stop_gla" in self.debug:
                    break
            self.phase_outproj(l, w_out)
        self.phase_out_transpose(out_d)
        self.fw.finish()
        return nc

    def phase_in_transpose(self, xin):
        self.new_phase()
        xt = [self.tile([128, D]) for _ in range(2)]
        stg = [self.tile([128, 16, 128]) for _ in range(2)]
        XTv = self.XT.rearrange("(kc p) t -> p kc t", p=128)
        for tt in range(18):
            x = xt[tt % 2]
            s = stg[tt % 2]
            self.ld(x, x.ap, xin[tt * 128:(tt + 1) * 128, :])
            for g4 in range(4):
                ps = self.PS[g4 % 4]
                for i in range(4):
                    kc = g4 * 4 + i
                    self.tp(ps, ps.ap[:, i * 128:(i + 1) * 128], x, x.ap[:, kc * 128:(kc + 1) * 128], self.ident)
                e = "act" if g4 % 2 else "dve"
                self.cp(e, s, s.ap[:, g4 * 4:(g4 + 1) * 4, :], ps, ps.ap.rearrange("p (a b) -> p a b", b=128))
            self.st(XTv[:, :, tt * 128:(tt + 1) * 128], s, s.ap)

    def phase_out_transpose(self, out_d):
        self.new_phase()
        xt = [self.tile([128, 16, 128]) for _ in range(2)]
        stg = [self.tile([128, D]) for _ in range(2)]
        XTv = self.XT.rearrange("(kc p) t -> p kc t", p=128)
        for tt in range(16):
            x = xt[tt % 2]
            s = stg[tt % 2]
            t0 = NCTX + tt * 128
            self.ld(x, x.ap, XTv[:, :, t0:t0 + 128])
            for g4 in range(4):
                ps = self.PS[g4 % 4]
                for i in range(4):
                    kc = g4 * 4 + i
                    self.tp(ps, ps.ap[:, i * 128:(i + 1) * 128], x, x.ap[:, kc, :], self.ident)
                e = "act" if g4 % 2 else "dve"
                self.cp(e, s, s.ap[:, g4 * 512:(g4 + 1) * 512], ps, ps.ap)
            self.st(out_d[tt * 128:(tt + 1) * 128, :], s, s.ap, is_output=True)

    def phase_mod(self, l, ada_w, adabT, ngT):
        self.new_phase()
        wb = [self.tile([128, 16, 512]) for _ in range(2)]
        adab = self.tile([128, 48])
        ng = self.tile([128, 16])
        self.ld(adab, adab.ap, adabT[l])
        self.ld(ng, ng.ap, ngT[l])
        Wv = ada_w[l].rearrange("(kc p) n -> p kc n", p=128)
        pm = self.PS[0]
        for cc in range(12):
            w = wb[cc % 2]
            self.ld(w, w.ap, Wv[:, :, cc * 512:(cc + 1) * 512])
            for jj in range(4):
                j = cc * 4 + jj
                for kc in range(16):
                    self.mm(pm, pm.ap[:, 2 * j:2 * j + 2], w, w.ap[:, kc, jj * 128:(jj + 1) * 128],
                            self.sc, self.sc.ap[:, kc, :], start=(kc == 0), stop=(kc == 15))
        pm3 = pm.ap[:, 0:96].rearrange("p (j s) -> p j s", s=2)
        for s in range(2):
            self.tt("dve", self.modv, self.modv.ap[:, :, s], pm, pm3[:, :, s], adab, adab.ap, ALU.add)
        for s in range(2):
            self.stt("dve", self.gs, self.gs.ap[:, :, s], self.modv, self.modv.ap[:, 16:32, s], 1.0, ng, ng.ap, ALU.add, ALU.mult)

    def phase_inproj(self, l, w_in, tok_chunks=(), gla=None):
        self.new_phase()
        hR = self.tile([128, 16, NT], F32R)
        hb = [T(hR.ap) for _ in BLKS]
        self._wraw_B = self.tile([128, 16, 128])
        xc = [self.tile([128, 512]) for _ in range(3)]
        tm = [self.tile([128, 512]) for _ in range(2)]
        sq = [self.tile([128, 512], F32R) for _ in range(2)]
        rstd = self.tile([128, 512])
        XTv = self.XT.rearrange("(kc p) t -> p kc t", p=128)
        xi = 0
        for bi, (t0, n, s) in enumerate(BLKS):
            h = hb[bi]
            ps = self.PS[bi % 2]
            for kc in range(16):
                x = xc[xi % 3]; xi += 1
                self.ld(x, x.ap[:, 0:n], XTv[:, kc, t0:t0 + n])
                q = sq[kc % 2]
                self.act(q, q.ap[:, 0:n], x, x.ap[:, 0:n], AF.Square)
                self.mm(ps, ps.ap[:, 0:n], self.onesr, self.onesr.ap, q, q.ap[:, 0:n], start=(kc == 0), stop=(kc == 15))
            self.rsqrt_from(rstd, rstd.ap[:, 0:n], ps, ps.ap[:, 0:n], 1.0 / D)
            for kc in range(16):
                x = xc[xi % 3]; xi += 1
                self.ld(x, x.ap[:, 0:n], XTv[:, kc, t0:t0 + n])
                t = tm[kc % 2]
                self.tt("dve", t, t.ap[:, 0:n], x, x.ap[:, 0:n], rstd, rstd.ap[:, 0:n], ALU.mult)
                self.ts("dve", h, hR.ap[:, kc, t0:t0 + n], t, t.ap[:, 0:n],
                        self.gs.ap[:, kc, s:s + 1], self.modv.ap[:, kc, s:s + 1], ALU.mult, ALU.add, R=[self.gs, self.modv])
        if "hT" in self.debug:
            dbg = self.scratch("hT", [D, NT])
            dt_ = self._wraw_B
            for tt in range(18):
                self.cp("dve", dt_, dt_.ap, hb[0 if tt < 2 else 1 + (tt - 2) // 4], hR.ap[:, :, tt * 128:(tt + 1) * 128])
                self.fw.dma(dbg.rearrange("(kc p) t -> p kc t", p=128)[:, :, tt * 128:(tt + 1) * 128], dt_.ap, R=[dt_], q="act")
        wraw = self._wraw_B
        wr = [self.tile([128, 16, 128], F32R) for _ in range(2)]
        ostb = [self.tile([128, n]) for (t0, n, s) in BLKS]
        Wv = w_in[l].rearrange("(kc p) n -> p kc n", p=128)
        VT = self.VTOK.rearrange("(tt p) n -> p tt n", p=128)
        tok_chunks = set(tok_chunks)
        pi = 0
        for j in range(48):
            w = wr[j % 2]
            self.ld(wraw, wraw.ap, Wv[:, :, j * 128:(j + 1) * 128])
            self.cp("pool", w, w.ap, wraw, wraw.ap)
            if j in tok_chunks:
                jj = j - min(tok_chunks)
                for g in range(5):
                    ps = self.PS[2 + pi % 4]; pi += 1
                    tts = list(range(g * 4, min(g * 4 + 4, 18)))
                    o = ostb[g if len(tts) == 4 and g > 0 else (1 if g == 0 else 0)]
                    for ii, tt in enumerate(tts):
                        bi = 0 if tt < 2 else 1 + (tt - 2) // 4
                        for kc in range(16):
                            self.mm(ps, ps.ap[:, ii * 128:(ii + 1) * 128], hb[bi], hR.ap[:, kc, tt * 128:(tt + 1) * 128],
                                    w, w.ap[:, kc, :], start=(kc == 0), stop=(kc == 15))
                    nn = len(tts) * 128
                    self.cp("act" if g % 2 else "dve", o, o.ap[:, 0:nn], ps, ps.ap[:, 0:nn])
                    self.st(VT[:, tts[0]:tts[-1] + 1, jj * 128:(jj + 1) * 128], o, o.ap[:, 0:nn].rearrange("p (a b) -> p a b", b=128))
            else:
                for bi, (t0, n, s) in enumerate(BLKS):
                    ps = self.PS[2 + pi % 4]; pi += 1
                    o = ostb[bi]
                    for kc in range(16):
                        self.mm(ps, ps.ap[:, 0:n], w, w.ap[:, kc, :], hb[bi], hR.ap[:, kc, t0:t0 + n], start=(kc == 0), stop=(kc == 15))
                    self.cp("act" if bi % 2 else "dve", o, o.ap, ps, ps.ap[:, 0:n])
                    self.st(self.PROJT[j * 128:(j + 1) * 128, t0:t0 + n], o, o.ap)
        if gla is not None:
            self.gla_gate(gla, hb, hR, wraw, wr, ostb, sq, tm)

    def gla_gate(self, gla, hb, hR, wraw, wr, ostb, sq, tm):
        j, wa1, wa2, baT = gla
        nba = self.tile([128, 8])
        wflat = wraw.ap.rearrange("p a b -> p (a b)")
        w2 = T(wr[1].ap.rearrange("p a b -> p (a b)")[:, 0:1024]); w2.b = wr[1].b
        w1 = wr[0]
        lr = sq[0]
        k = 0
        for d in range(2):
            self.op("pool", lambda g: g.memset(wraw.ap, 0.0), W=[wraw])
            self.ld(wraw, wraw.ap[:, :, 0:16], wa1[j, d].rearrange("(kc p) r -> p kc r", p=128))
            self.cp("pool", w1, w1.ap, wraw, wraw.ap)
            self.op("pool", lambda g: g.memset(wraw.ap, 0.0), W=[wraw])
            self.ld(wraw, wflat[0:16, 0:1024], wa2[j, d])
            self.cp("pool", w2, w2.ap, wraw, wflat[:, 0:1024])
            self.ld(nba, nba.ap, baT[j, d])
            self.ts("dve", nba, nba.ap, nba, nba.ap, -1.0, None, ALU.mult)
            for bi, (t0, n, s) in enumerate(BLKS):
                ps = self.PS[bi % 2]
                for kc in range(16):
                    self.mm(ps, ps.ap[:, 0:n], w1, w1.ap[:, kc, :], hb[bi], hR.ap[:, kc, t0:t0 + n], start=(kc == 0), stop=(kc == 15))
                self.cp("dve", lr, lr.ap[:, 0:n], ps, ps.ap[:, 0:n])
                for c in range(8):
                    o = ostb[1 + k % 4]
                    t_ = tm[k % 2]
                    ps2 = self.PS[2 + k % 4]
                    k += 1
                    self.mm(ps2, ps2.ap[:, 0:n], w2, w2.ap[:, c * 128:(c + 1) * 128], lr, lr.ap[:, 0:n])
                    self.act(t_, t_.ap[:, 0:n], ps2, ps2.ap[:, 0:n], AF.Exp, bias=nba.ap[:, c:c + 1], scale=-1.0, R=[nba])
                    self.act(t_, t_.ap[:, 0:n], t_, t_.ap[:, 0:n], AF.Ln, bias=1.0)
                    self.ts("dve", o, o.ap[:, 0:n], t_, t_.ap[:, 0:n], -1.0 / 16.0, None, ALU.mult)
                    self.st(self.LAT[d, c * 128:(c + 1) * 128, t0:t0 + n], o, o.ap[:, 0:n])

    def phase_outproj(self, l, w_out):
        self.new_phase()
        mR = self.tile([128, 16, NT], F32R)
        mb = [T(mR.ap) for _ in BLKS]
        xc = [self.tile([128, 512]) for _ in range(3)]
        Mv = self.MRGT.rearrange("(kc p) t -> p kc t", p=128)
        xi = 0
        for bi, (t0, n, s) in enumerate(BLKS):
            for kc in range(16):
                x = xc[xi % 3]; xi += 1
                self.ld(x, x.ap[:, 0:n], Mv[:, kc, t0:t0 + n])
                self.cp(self.alt(), mb[bi], mR.ap[:, kc, t0:t0 + n], x, x.ap[:, 0:n])
        wraw = self.tile([128, 16, 128])
        wr = [self.tile([128, 16, 128], F32R) for _ in range(2)]
        xo = [self.tile([128, n]) for (t0, n, s) in BLKS]
        Wv = w_out[l].rearrange("(kc p) n -> p kc n", p=128)
        pi = 0
        for j in range(16):
            w = wr[j % 2]
            self.ld(wraw, wraw.ap, Wv[:, :, j * 128:(j + 1) * 128])
            self.cp("pool", w, w.ap, wraw, wraw.ap)
            for bi, (t0, n, s) in enumerate(BLKS):
                x = xo[bi]
                self.ld(x, x.ap, self.XT[j * 128:(j + 1) * 128, t0:t0 + n])
                ps = self.PS[pi % 4]; pi += 1
                for kc in range(16):
                    self.mm(ps, ps.ap[:, 0:n], w, w.ap[:, kc, :], mb[bi], mR.ap[:, kc, t0:t0 + n], start=(kc == 0), stop=(kc == 15))
                self.stt("dve", x, x.ap, ps, ps.ap[:, 0:n], self.modv.ap[:, 32 + j, s:s + 1], x, x.ap,
                         ALU.mult, ALU.add, R=[self.modv])
                self.st(self.XT[j * 128:(j + 1) * 128, t0:t0 + n], x, x.ap)

    def sincos(self, ang, n, sin_out=None, cos_out=None, scr=None):
        u, ki, rd = scr
        for out, shift in ((sin_out, 0.0), (cos_out, PI / 2)):
            if out is None:
                continue
            self.ts("dve", u, u.ap[:, 0:n], ang, ang.ap[:, 0:n], 1.0 / (2 * PI), shift / (2 * PI), ALU.mult, ALU.add)
            self.cp("dve", ki, ki.ap[:, 0:n], u, u.ap[:, 0:n])
            self.cp("dve", u, u.ap[:, 0:n], ki, ki.ap[:, 0:n])
            self.stt("dve", rd, rd.ap[:, 0:n], u, u.ap[:, 0:n], -2 * PI, ang, ang.ap[:, 0:n], ALU.mult, ALU.add)
            self.ts("dve", rd, rd.ap[:, 0:n], rd, rd.ap[:, 0:n], shift, 3.1415925, ALU.add, ALU.min)
            self.ts("dve", rd, rd.ap[:, 0:n], rd, rd.ap[:, 0:n], -3.1415925, None, ALU.max)
            self.act(out, out.ap[:, 0:n], rd, rd.ap[:, 0:n], AF.Sin)

    def phase_s5(self, j, s5p, s5B, s5C, s5d, s5tau, wglu):
        self.new_phase()
        W = 8 * TC
        tau = self.tile([128, 4, TC])
        self.ld(tau, tau.ap, s5tau.rearrange("k p t -> p k t"))
        dsk = self.tile([128, 8])
        self.ld(dsk, dsk.ap, s5d[j])
        t1, t2, t3, t4 = [self.tile([128, W]) for _ in range(4)]
        ki = self.tile([128, W], I32)
        scr = (t1, ki, t2)
        ang = t3
        PR = []
        for d in range(2):
            prm = self.tile([128, 3, 32])
            self.ld(prm, prm.ap, s5p[j, d])
            names = "dt r th sn cs fre fim nfre c1 s1 t1 t2 den".split()
            c = {k: self.tile([128, 32]) for k in names}
            are, aim, ldt = prm.ap[:, 0, :], prm.ap[:, 1, :], prm.ap[:, 2, :]
            self.act(c["dt"], c["dt"].ap, prm, ldt, AF.Exp)
            self.tt("dve", c["t1"], c["t1"].ap, prm, are, c["dt"], c["dt"].ap, ALU.mult)
            self.act(c["r"], c["r"].ap, c["t1"], c["t1"].ap, AF.Exp)
            self.tt("dve", c["th"], c["th"].ap, prm, aim, c["dt"], c["dt"].ap, ALU.mult)
            self.sincos(c["th"], 32, c["sn"], c["cs"], scr)
            self.tt("dve", c["t1"], c["t1"].ap, c["r"], c["r"].ap, c["cs"], c["cs"].ap, ALU.mult)
            self.ts("dve", c["t1"], c["t1"].ap, c["t1"], c["t1"].ap, -1.0, None, ALU.add)
            self.tt("dve", c["t2"], c["t2"].ap, c["r"], c["r"].ap, c["sn"], c["sn"].ap, ALU.mult)
            self.tt("dve", c["den"], c["den"].ap, prm, are, prm, are, ALU.mult)
            self.tt("dve", c["fre"], c["fre"].ap, prm, aim, prm, aim, ALU.mult)
            self.tt("dve", c["den"], c["den"].ap, c["den"], c["den"].ap, c["fre"], c["fre"].ap, ALU.add)
            self.op("dve", lambda g, t=c["den"]: g.reciprocal(t.ap, t.ap), [c["den"]], [c["den"]])
            self.tt("dve", c["fre"], c["fre"].ap, c["t1"], c["t1"].ap, prm, are, ALU.mult)
            self.tt("dve", c["fim"], c["fim"].ap, c["t2"], c["t2"].ap, prm, aim, ALU.mult)
            self.tt("dve", c["fre"], c["fre"].ap, c["fre"], c["fre"].ap, c["fim"], c["fim"].ap, ALU.add)
            self.tt("dve", c["fre"], c["fre"].ap, c["fre"], c["fre"].ap, c["den"], c["den"].ap, ALU.mult)
            self.tt("dve", c["fim"], c["fim"].ap, c["t2"], c["t2"].ap, prm, are, ALU.mult)
            self.tt("dve", c["nfre"], c["nfre"].ap, c["t1"], c["t1"].ap, prm, aim, ALU.mult)
            self.tt("dve", c["fim"], c["fim"].ap, c["fim"], c["fim"].ap, c["nfre"], c["nfre"].ap, ALU.subtract)
            self.tt("dve", c["fim"], c["fim"].ap, c["fim"], c["fim"].ap, c["den"], c["den"].ap, ALU.mult)
            self.ts("dve", c["nfre"], c["nfre"].ap, c["fre"], c["fre"].ap, -1.0, None, ALU.mult)
            self.ts("dve", c["t1"], c["t1"].ap, c["th"], c["th"].ap, float(TC), None, ALU.mult)
            self.sincos(c["t1"], 32, c["s1"], c["c1"], scr)
            self.tt("dve", c["c1"], c["c1"].ap, c["c1"], c["c1"].ap, c["r"], c["r"].ap, ALU.mult)
            self.tt("dve", c["s1"], c["s1"].ap, c["s1"], c["s1"].ap, c["r"], c["r"].ap, ALU.mult)
            PR.append(c)
        sn, cs, nsn, wr, wi, Rm = [self.tile([128, W]) for _ in range(6)]
        bre, bim = self.tile([128, W]), self.tile([128, W])
        ncs = self.tile([128, W])
        Xre = [self.tile([128, W]) for _ in range(2)]
        Xim = [self.tile([128, W]) for _ in range(2)]
        gre = [self.tile([128, W]) for _ in range(2)]
        gim = [self.tile([128, W]) for _ in range(2)]
        uu = [[self.tile([128, W], F32R) for _ in range(4)] for _ in range(2)]
        cr = [self.tile([128, 8]) for _ in range(4)]
        ustg = self.tile([128, NT])
        ur = self.tile([128, 2, NT], F32R)
        yt = self.tile([128, 2, NT])
        BCraw = self.tile([128, 16, 128])
        Br = self.tile([128, 16, 128], F32R)
        Cr = self.tile([128, 16, 128], F32R)
        PRb, PIb = [self.PS[0], self.PS[1]], [self.PS[2], self.PS[3]]
        PYs = [self.PS[4], self.PS[5]]
        pr_ap, pi_ap = self.psum[:, 0:W], self.psum[:, 2 * 512:2 * 512 + W]
        fseq = list(range(18))
        bseq = [1, 0] + list(range(17, 1, -1))
        v3 = lambda t, k: t.ap.rearrange("p (s t) -> p s t", t=TC)[:, :, k]
        for sg in range(4):
            for o in range(2):
                oc = 2 * sg + o
                self.ld(ustg, ustg.ap, self.PROJT[oc * 128:(oc + 1) * 128, :])
                self.cp("dve", ur, ur.ap[:, o, :], ustg, ustg.ap)
            for d in range(2):
                c = PR[d]
                for src, dstR in ((s5B, Br), (s5C, Cr)):
                    for ri in range(2):
                        self.ld(BCraw, BCraw.ap[:, ri::2, :], src[j, d, ri, sg * 8:(sg + 1) * 8].rearrange("s k m -> k s m"))
                    self.cp("pool", dstR, dstR.ap, BCraw, BCraw.ap)
                tauD = tau.ap[:, d, :]
                maskD = tau.ap[:, 2 + d, :]
                for i in range(8):
                    st = sg * 8 + i
                    sl = slice(i * TC, (i + 1) * TC)
                    self.ts("dve", ang, ang.ap[:, sl], tau, tauD, c["th"].ap[:, st:st + 1], None, ALU.mult, R=[c["th"]])
                    self.act(Rm, Rm.ap[:, sl], tau, maskD, AF.Copy, scale=c["r"].ap[:, st:st + 1], R=[c["r"]])
                self.sincos(ang, W, sn, cs, scr)
                self.act(nsn, nsn.ap, sn, sn.ap, AF.Copy, scale=-1.0)
                self.act(ncs, ncs.ap, cs, cs.ap, AF.Copy, scale=-1.0)
                for i in range(8):
                    st = sg * 8 + i
                    sl = slice(i * TC, (i + 1) * TC)
                    self.act(wr, wr.ap[:, sl], cs, cs.ap[:, sl], AF.Copy, scale=c["fre"].ap[:, st:st + 1], R=[c["fre"]])
                    self.act(wi, wi.ap[:, sl], cs, cs.ap[:, sl], AF.Copy, scale=c["fim"].ap[:, st:st + 1], R=[c["fim"]])
                    self.stt("dve", wr, wr.ap[:, sl], sn, sn.ap[:, sl], c["fim"].ap[:, st:st + 1], wr, wr.ap[:, sl], ALU.mult, ALU.add, R=[c["fim"]])
                    self.stt("dve", wi, wi.ap[:, sl], sn, sn.ap[:, sl], c["nfre"].ap[:, st:st + 1], wi, wi.ap[:, sl], ALU.mult, ALU.add, R=[c["nfre"]])
                first = 0 if d == 0 else TC - 1
                last = TC - 1 if d == 0 else 0
                seq = fseq if d == 0 else bseq
                c1 = c["c1"].ap[:, sg * 8:(sg + 1) * 8]
                s1 = c["s1"].ap[:, sg * 8:(sg + 1) * 8]

                def s1_pe(qi):
                    t0 = seq[qi] * TC
                    for i in range(8):
                        for ri, (pb, pap) in enumerate(((PRb, pr_ap), (PIb, pi_ap))):
                            self.mm(pb[i // 4], pap[:, i * TC:(i + 1) * TC], Br, Br.ap[:, 2 * i + ri, :], ur, ur.ap[:, i // 4, t0:t0 + TC])
                    self.fw.op("act", lambda g: g.copy(bre.ap, pr_ap), PRb, [bre])
                    self.fw.op("act", lambda g: g.copy(bim.ap, pi_ap), PIb, [bim])

                def s1_dve(qi):
                    xr, xi_ = Xre[qi % 2], Xim[qi % 2]
                    self.tt("dve", t1, t1.ap, wr, wr.ap, bre, bre.ap, ALU.mult)
                    self.tt("dve", t2, t2.ap, wi, wi.ap, bim, bim.ap, ALU.mult)
                    self.tt("dve", t3, t3.ap, wr, wr.ap, bim, bim.ap, ALU.mult)
                    self.tt("dve", t4, t4.ap, wi, wi.ap, bre, bre.ap, ALU.mult)
                    self.tt("dve", xr, xr.ap, t1, t1.ap, t2, t2.ap, ALU.subtract)
                    self.tt("dve", xi_, xi_.ap, t3, t3.ap, t4, t4.ap, ALU.add)

                def s2_dve(qi):
                    xr, xi_ = Xre[qi % 2], Xim[qi % 2]
                    g_re, g_im = gre[qi % 2], gim[qi % 2]
                    p_re, p_im = gre[(qi + 1) % 2], gim[(qi + 1) % 2]
                    u1, u2, u3, u4 = uu[qi % 2]
                    if qi > 0:
                        a, b_, e_, f_ = cr
                        self.tt("dve", a, a.ap, p_re, v3(p_re, last), c["c1"], c1, ALU.mult)
                        self.tt("dve", b_, b_.ap, p_im, v3(p_im, last), c["s1"], s1, ALU.mult)
                        self.tt("dve", e_, e_.ap, p_re, v3(p_re, last), c["s1"], s1, ALU.mult)
                        self.tt("dve", f_, f_.ap, p_im, v3(p_im, last), c["c1"], c1, ALU.mult)
                        self.tt("dve", a, a.ap, a, a.ap, b_, b_.ap, ALU.subtract)
                        self.tt("dve", e_, e_.ap, e_, e_.ap, f_, f_.ap, ALU.add)
                        self.tt("dve", xr, v3(xr, first), xr, v3(xr, first), a, a.ap, ALU.add)
                        self.tt("dve", xi_, v3(xi_, first), xi_, v3(xi_, first), e_, e_.ap, ALU.add)
                    rv = (lambda ap: ap) if d == 0 else (lambda ap: ap[:, ::-1])
                    self.op("dve", lambda g: g.tensor_tensor_scan(rv(g_re.ap), rv(Rm.ap), rv(xr.ap), 0.0, ALU.mult, ALU.add), [Rm, xr], [g_re])
                    self.op("dve", lambda g: g.tensor_tensor_scan(rv(g_im.ap), rv(Rm.ap), rv(xi_.ap), 0.0, ALU.mult, ALU.add), [Rm, xi_], [g_im])
                    self.tt("dve", u1, u1.ap, cs, cs.ap, g_re, g_re.ap, ALU.mult)
                    self.tt("dve", u2, u2.ap, nsn, nsn.ap, g_im, g_im.ap, ALU.mult)
                    self.tt("dve", u3, u3.ap, nsn, nsn.ap, g_re, g_re.ap, ALU.mult)
                    self.tt("dve", u4, u4.ap, ncs, ncs.ap, g_im, g_im.ap, ALU.mult)

                def s2_pe(qi):
                    t0 = seq[qi] * TC
                    u1, u2, u3, u4 = uu[qi % 2]
                    PY = PYs[qi % 2]
                    for o in range(2):
                        k = 0
                        for i in range(o * 4, o * 4 + 4):
                            for ri, ht in ((0, u1), (0, u2), (1, u3), (1, u4)):
                                self.mm(PY, PY.ap[:, o * TC:(o + 1) * TC], Cr, Cr.ap[:, 2 * i + ri, :], ht, ht.ap[:, i * TC:(i + 1) * TC],
                                        start=(k == 0), stop=(k == 15))
                                k += 1
                    py3 = PY.ap[:, 0:2 * TC].rearrange("p (o t) -> p o t", t=TC)
                    if d == 0:
                        self.cp("act", yt, yt.ap[:, :, t0:t0 + TC], PY, py3)
                    else:
                        self.tt("dve", yt, yt.ap[:, :, t0:t0 + TC], yt, yt.ap[:, :, t0:t0 + TC], PY, py3, ALU.add)

                s1_pe(0)
                s1_dve(0)
                for qi in range(18):
                    if qi + 1 < 18:
                        s1_pe(qi + 1)
                    s2_dve(qi)
                    s2_pe(qi)
                    if qi + 1 < 18:
                        s1_dve(qi + 1)
            for o in range(2):
                oc = 2 * sg + o
                self.ld(ustg, ustg.ap, self.PROJT[oc * 128:(oc + 1) * 128, :])
                self.stt("dve", yt, yt.ap[:, o, :], ustg, ustg.ap, dsk.ap[:, oc:oc + 1], yt, yt.ap[:, o, :], ALU.mult, ALU.add, R=[dsk])
                self.act(ustg, ustg.ap, yt, yt.ap[:, o, :], AF.Gelu_apprx_tanh)
                self.st(self.GT[oc * 128:(oc + 1) * 128, :], ustg, ustg.ap)
        self.new_phase()
        gR = self.tile([128, 8, NT], F32R)
        wgR = self.tile([128, 8, 1024], F32R)
        stg = [self.tile([128, 1024]) for _ in range(2)]
        Gv = self.GT.rearrange("(kc p) t -> p kc t", p=128)
        Wg = wglu[j].rearrange("(kc p) n -> p kc n", p=128)
        for kc in range(8):
            x = stg[kc % 2]
            self.ld(x, x.ap, Wg[:, kc, :])
            self.cp(self.alt(), wgR, wgR.ap[:, kc, :], x, x.ap)
        k = 0
        for kc in range(8):
            for (t0, n, s) in BLKS:
                x = stg[k % 2]; k += 1
                self.ld(x, x.ap[:, 0:n], Gv[:, kc, t0:t0 + n])
                self.cp(self.alt(), gR, gR.ap[:, kc, t0:t0 + n], x, x.ap[:, 0:n])
        gF = [self.tile([128, 512]) for _ in range(2)]
        zs = [self.tile([128, 512]) for _ in range(2)]
        tg = [self.tile([128, 512]) for _ in range(2)]
        ob = [self.tile([128, 512]) for _ in range(2)]
        k = 0
        for ncz in range(8):
            for (t0, n, s) in BLKS:
                ps = self.PS[k % 4]
                g_, z_, t_, o_ = gF[k % 2], zs[k % 2], tg[k % 2], ob[k % 2]
                k += 1
                self.ld(g_, g_.ap[:, 0:n], self.GT[ncz * 128:(ncz + 1) * 128, t0:t0 + n])
                self.ld(z_, z_.ap[:, 0:n], self.PROJT[1024 + ncz * 128:1024 + (ncz + 1) * 128, t0:t0 + n])
                for kc in range(8):
                    self.mm(ps, ps.ap[:, 0:n], wgR, wgR.ap[:, kc, ncz * 128:(ncz + 1) * 128], gR, gR.ap[:, kc, t0:t0 + n], start=(kc == 0), stop=(kc == 7))
                self.act(t_, t_.ap[:, 0:n], ps, ps.ap[:, 0:n], AF.Sigmoid)
                self.tt("dve", t_, t_.ap[:, 0:n], t_, t_.ap[:, 0:n], g_, g_.ap[:, 0:n], ALU.mult)
                self.act(z_, z_.ap[:, 0:n], z_, z_.ap[:, 0:n], AF.Silu)
                self.tt("dve", o_, o_.ap[:, 0:n], t_, t_.ap[:, 0:n], z_, z_.ap[:, 0:n], ALU.mult)
                self.st(self.MRGT[ncz * 128:(ncz + 1) * 128, t0:t0 + n], o_, o_.ap[:, 0:n])

    def phase_da(self, j, daq, lamT, rope, lam_init):
        self.new_phase()
        dq = self.tile([128, 3])
        self.ld(dq, dq.ap, daq[j])
        sgc = self.tile([128, 1])
        self.ts("dve", sgc, sgc.ap, dq, dq.ap[:, 2:3], 1.0 - lam_init, None, ALU.mult)
        lt = self.tile([64, 4])
        self.ld(lt, lt.ap, lamT[j])
        pr = self.tile([64, 2])
        self.tt("dve", pr, pr.ap[:, 0:1], lt, lt.ap[:, 0:1], lt, lt.ap[:, 1:2], ALU.mult)
        self.tt("dve", pr, pr.ap[:, 1:2], lt, lt.ap[:, 2:3], lt, lt.ap[:, 3:4], ALU.mult)
        p6, p7 = self.PS[6], self.PS[7]
        self.mm(p6, p6.ap[:, 0:2], self.cst, self.cst.ap[0:64, 1, :], pr, pr.ap)
        le = self.tile([128, 2])
        self.act(le, le.ap, p6, p6.ap[:, 0:2], AF.Exp)
        nlam = self.tile([128, 1])
        self.tt("dve", nlam, nlam.ap, le, le.ap[:, 1:2], le, le.ap[:, 0:1], ALU.subtract)
        self.ts("dve", nlam, nlam.ap, nlam, nlam.ap, -lam_init, None, ALU.add)
        cosT = self.tile([128, 2048]); sinT = self.tile([128, 2048])
        self.ld(cosT, cosT.ap, rope[0]); self.ld(sinT, sinT.ap, rope[1])
        q0 = [self.tile([128, NT], F32R) for _ in range(2)]
        q1 = [self.tile([128, NT], F32R) for _ in range(2)]
        kr = [self.tile([128, NT], F32R) for _ in range(2)]
        vr = [self.tile([128, 18, 128], F32R) for _ in range(2)]
        zd = [self.tile([128, NT]) for _ in range(2)]
        mo = [self.tile([128, NT]) for _ in range(2)]
        for q in q0 + q1:
            self.ts("dve", q, q.ap[:, 0:2048], cosT, cosT.ap, 0.0, None, ALU.mult)
            self.ts("dve", q, q.ap[:, 2048:NT], cosT, cosT.ap[:, 0:NT - 2048], 0.0, None, ALU.mult)
        qraw = self.tile([128, NT]); kraw = self.tile([128, NT])
        vraw = self.tile([128, 18, 128])
        sq = [self.tile([128, 512], F32R) for _ in range(2)]
        rstd = [self.tile([128, 512]) for _ in range(2)]
        qg = [self.tile([128, 512], F32R) for _ in range(2)]
        t1 = [self.tile([128, 512]) for _ in range(2)]
        t2 = [self.tile([128, 512]) for _ in range(2)]
        pT = [self.tile([128, 512], F32R) for _ in range(3)]
        rr = [self.tile([128, 512]) for _ in range(2)]
        am = [[self.tile([128, 512]) for _ in range(2)] for _ in range(2)]
        rs2 = self.tile([128, 512])
        VT = self.VTOK.rearrange("(tt p) n -> p tt n", p=128)
        qblks = [(0, 256, [0, 1])] + [(256 + 512 * i, 512, list(range(18))) for i in range(4)]
        st_ = {"cnt": 0, "u": 0, "g": 0}

        ots = [self.tile([128, 512]) for _ in range(2)]

        def load_head(hd):
            b_ = hd % 2
            self.ld(qraw, qraw.ap, self.PROJT[2048 + hd * 128:2048 + (hd + 1) * 128, :])
            self.ld(kraw, kraw.ap, self.PROJT[3072 + hd * 128:3072 + (hd + 1) * 128, :])
            self.ld(zd[b_], zd[b_].ap, self.PROJT[5120 + hd * 128:5120 + (hd + 1) * 128, :])
            self.ld(vraw, vraw.ap, VT[:, :, hd * 128:(hd + 1) * 128])
            self.cp("pool", vr[b_], vr[b_].ap, vraw, vraw.ap)

        def silu_head(hd):
            b_ = hd % 2
            self.act(zd[b_], zd[b_].ap, zd[b_], zd[b_].ap, AF.Silu)

        from collections import deque
        bg = deque()

        def prep_stages(hd, which, bi):
            b_ = hd % 2
            raw = qraw if which == 0 else kraw
            t0, n, s = BLKS[bi]
            u = st_["u"]; st_["u"] += 1
            q_, rs_, qg_, t1_, t2_ = sq[u % 2], rstd[u % 2], qg[u % 2], t1[u % 2], t2[u % 2]
            qgf = qg_.ap.bitcast(F32)
            r0_ = t0 - NCTX

            def s0():
                self.tt("dve", q_, q_.ap[:, 0:n], raw, raw.ap[:, t0:t0 + n], raw, raw.ap[:, t0:t0 + n], ALU.mult)

            def s1():
                self.mm(p6, p6.ap[:, 0:n], self.blk64r, self.blk64r.ap, q_, q_.ap[:, 0:n])

            def s2():
                self.rsqrt_from(rs_, rs_.ap[:, 0:n], p6, p6.ap[:, 0:n], 1.0 / 64)

            def s3():
                self.stt("dve", qg_, qg_.ap[:, 0:n], raw, raw.ap[:, t0:t0 + n], dq.ap[:, which:which + 1], rs_, rs_.ap[:, 0:n],
                         ALU.mult, ALU.mult, R=[dq])
                if s != 0:
                    if which == 0:
                        self.cp("dve", q0[b_], q0[b_].ap[0:64, t0:t0 + n], qg_, qgf[0:64, 0:n])
                        self.cp("dve", q1[b_], q1[b_].ap[64:128, t0:t0 + n], qg_, qgf[64:128, 0:n])
                    else:
                        self.cp("dve", kr[b_], kr[b_].ap[:, t0:t0 + n], qg_, qgf[:, 0:n])

            def s4():
                self.mm(p7, p7.ap[:, 0:n], self.protr, self.protr.ap, qg_, qg_.ap[:, 0:n])
                self.tt("dve", t1_, t1_.ap[:, 0:n], qg_, qgf[:, 0:n], cosT, cosT.ap[:, r0_:r0_ + n], ALU.mult)

            def s5():
                self.tt("dve", t2_, t2_.ap[:, 0:n], p7, p7.ap[:, 0:n], sinT, sinT.ap[:, r0_:r0_ + n], ALU.mult)
                if which == 0:
                    self.tt("dve", q0[b_], q0[b_].ap[0:64, t0:t0 + n], t1_, t1_.ap[0:64, 0:n], t2_, t2_.ap[0:64, 0:n], ALU.add)
                    self.tt("dve", q1[b_], q1[b_].ap[64:128, t0:t0 + n], t1_, t1_.ap[64:128, 0:n], t2_, t2_.ap[64:128, 0:n], ALU.add)
                else:
                    self.tt("dve", kr[b_], kr[b_].ap[:, t0:t0 + n], t1_, t1_.ap[:, 0:n], t2_, t2_.ap[:, 0:n], ALU.add)
            return [s0, s1, s2, s3] + ([s4, s5] if s == 0 else [])

        sqt = self.tile([128, 512], F32R)

        def tail_stages(hd, t0, n, a0, a1, qbi, last):
            b_ = hd % 2
            ot = ots[qbi % 2]

            def t0_():
                self.stt("dve", ot, ot.ap[:, 0:n], a1, a1.ap[:, 0:n], nlam.ap[:, 0:1], a0, a0.ap[:, 0:n], ALU.mult, ALU.add, R=[nlam])
                self.tt("dve", sqt, sqt.ap[:, 0:n], ot, ot.ap[:, 0:n], ot, ot.ap[:, 0:n], ALU.mult)

            def t1_():
                self.mm(p6, p6.ap[:, 0:n], self.onesr, self.onesr.ap, sqt, sqt.ap[:, 0:n])

            def t2_():
                self.rsqrt_from(rs2, rs2.ap[:, 0:n], p6, p6.ap[:, 0:n], 1.0 / 128)

            def t3_():
                self.stt("dve", ot, ot.ap[:, 0:n], ot, ot.ap[:, 0:n], sgc.ap[:, 0:1], rs2, rs2.ap[:, 0:n], ALU.mult, ALU.mult, R=[sgc])
                self.tt("dve", mo[b_], mo[b_].ap[:, t0:t0 + n], ot, ot.ap[:, 0:n], zd[b_], zd[b_].ap[:, t0:t0 + n], ALU.mult)
                if last:
                    self.st(self.MRGT[1024 + hd * 128:1024 + (hd + 1) * 128, :], mo[b_], mo[b_].ap)
            return [t0_, t1_, t2_, t3_]

        def flush():
            while bg:
                bg.popleft()()

        units = [(w, bi) for w in range(2) for bi in range(5)]
        load_head(0)
        silu_head(0)
        for w, bi in units:
            for f in prep_stages(0, w, bi):
                f()
        for hd in range(8):
            b_ = hd % 2
            if hd + 1 < 8:
                load_head(hd + 1)
            ui = 0
            for qbi, (t0, n, keys) in enumerate(qblks):
                for m in range(2):
                    qm = q0[b_] if m == 0 else q1[b_]
                    g = st_["g"]; st_["g"] += 1
                    Om, RSm = self.PS[2 + 2 * (g % 2)], self.PS[3 + 2 * (g % 2)]
                    nk = len(keys)
                    base = st_["cnt"]; st_["cnt"] += nk

                    def smm(ki):
                        sp = self.PS[(base + ki) % 2]
                        kt = keys[ki]
                        self.mm(sp, sp.ap[:, 0:n], kr[b_], kr[b_].ap[:, kt * 128:(kt + 1) * 128], qm, qm.ap[:, t0:t0 + n])
                    smm(0)
                    for ki, kt in enumerate(keys):
                        sp = self.PS[(base + ki) % 2]
                        p_ = pT[(base + ki) % 3]
                        if ki + 1 < nk:
                            smm(ki + 1)
                        self.act(p_, p_.ap[:, 0:n], sp, sp.ap[:, 0:n], AF.Exp, scale=0.125)
                        self.mm(Om, Om.ap[:, 0:n], vr[b_], vr[b_].ap[:, kt, :], p_, p_.ap[:, 0:n], start=(ki == 0), stop=(ki == nk - 1))
                        self.mm(RSm, RSm.ap[:, 0:n], self.onesr, self.onesr.ap, p_, p_.ap[:, 0:n], start=(ki == 0), stop=(ki == nk - 1))
                        if bg:
                            bg.popleft()()
                    r_, a_ = rr[m], am[(g // 2) % 2][m]
                    self.op("dve", lambda g_, r_=r_, RSm=RSm, n=n: g_.reciprocal(r_.ap[:, 0:n], RSm.ap[:, 0:n]), [RSm], [r_])
                    self.tt("dve", a_, a_.ap[:, 0:n], Om, Om.ap[:, 0:n], r_, r_.ap[:, 0:n], ALU.mult)
                    if hd + 1 < 8 and ui < len(units):
                        if ui == 1:
                            bg.append(lambda hd=hd: silu_head(hd + 1))
                        bg.extend(prep_stages(hd + 1, *units[ui])); ui += 1
                a0, a1 = am[((st_["g"] - 1) // 2) % 2]
                bg.extend(tail_stages(hd, t0, n, a0, a1, qbi, qbi == len(qblks) - 1))
            flush()

    def phase_gla(self, j, glag, gmask):
        self.new_phase()
        CH = 128
        NCH = NT // CH
        scale = 256.0 ** -0.5
        gg = self.tile([128, 4])
        self.ld(gg, gg.ap, glag[j])
        gm = [self.tile([128, NT]) for _ in range(2)]
        for d in range(2):
            self.ld(gm[d], gm[d].ap, gmask[d])
        qraw = self.tile([128, 2, NT]); kraw = self.tile([128, 2, NT])
        OT = self.tile([128, 4, NT])
        la = self.tile([128, NT]); bt = self.tile([128, NT])
        qd = self.tile([128, 2, NT], F32R); kd = self.tile([128, 2, NT], F32R)
        ebend = self.tile([128, 2, NCH])
        S = self.tile([128, 2, 512]); Srs = [self.tile([128, 2, 512], F32R) for _ in range(2)]
        vraw = [self.tile([CH, 512]) for _ in range(2)]
        vr = [self.tile([CH, 512], F32R) for _ in range(2)]
        scTs = [self.tile([CH, CH], F32R) for _ in range(3)]
        kcts = [self.tile([128, 2, CH]) for _ in range(3)]
        kendTs = [self.tile([CH, 256], F32R) for _ in range(3)]
        sq = [self.tile([128, 512], F32R) for _ in range(2)]
        rstd = self.tile([128, 512])
        zt = [self.tile([128, 512]) for _ in range(2)]
        yt = [self.tile([128, 512]) for _ in range(2)]
        ob = [self.tile([128, 512]) for _ in range(2)]
        psSs, psT, psO, psU, psN = [self.PS[0], self.PS[7]], self.PS[1], [self.PS[2], self.PS[3]], [self.PS[4], self.PS[5]], self.PS[6]
        nctx = NCTX // CH
        fseq = list(range(NCH))
        bseq = list(range(nctx - 1, -1, -1)) + list(range(NCH - 1, nctx - 1, -1))
        vi = 0
        for hd in range(4):
            self.ld(qraw, qraw.ap, self.PROJT[hd * 256:(hd + 1) * 256, :].rearrange("(c p) t -> p c t", p=128))
            self.ld(kraw, kraw.ap, self.PROJT[1024 + hd * 256:1024 + (hd + 1) * 256, :].rearrange("(c p) t -> p c t", p=128))
            for d in range(2):
                last = CH - 1 if d == 0 else 0
                for dkc in range(2):
                    r0 = hd * 256 + dkc * 128
                    self.ld(la, la.ap, self.LAT[d, r0:r0 + 128, :])
                    if d == 0:
                        self.op("dve", lambda g: g.tensor_tensor_scan(bt.ap, gm[0].ap, la.ap, 0.0, ALU.mult, ALU.add), [gm[0], la], [bt])
                    else:
                        self.op("dve", lambda g: g.tensor_tensor_scan(bt.ap[:, ::-1], gm[1].ap[:, ::-1], la.ap[:, ::-1], 0.0, ALU.mult, ALU.add), [gm[1], la], [bt])
                    self.act(la, la.ap, bt, bt.ap, AF.Exp)
                    self.stt("dve", qd, qd.ap[:, dkc, :], qraw, qraw.ap[:, dkc, :], scale, la, la.ap, ALU.mult, ALU.mult)
                    self.cp("dve", ebend, ebend.ap[:, dkc, :], la, la.ap[:, last::CH])
                    self.act(bt, bt.ap, bt, bt.ap, AF.Exp, scale=-1.0)
                    self.tt("dve", kd, kd.ap[:, dkc, :], kraw, kraw.ap[:, dkc, :], bt, bt.ap, ALU.mult)
                self.op("pool", lambda g: g.memset(S.ap, 0.0), W=[S])
                self.ts("dve", Srs[0], Srs[0].ap, S, S.ap, 0.0, None, ALU.mult)
                tri = self.cst.ap[0:CH, 4 + d, 0:CH]
                kdf, qdf = kd.ap.bitcast(F32), qd.ap.bitcast(F32)
                seq = fseq if d == 0 else bseq
                vbase = vi
                vi += len(seq)

                def pre(qi):
                    c = seq[qi]
                    t0 = c * CH
                    va, v_ = vraw[(vbase + qi) % 2], vr[(vbase + qi) % 2]
                    self.ld(va, va.ap, self.VTOK[t0:t0 + CH, hd * 512:(hd + 1) * 512])
                    self.cp("act", v_, v_.ap, va, va.ap)
                    scT, kct, kendT, psS = scTs[qi % 3], kcts[qi % 3], kendTs[qi % 3], psSs[qi % 2]
                    tof = (qi % 2) * 256
                    for dkc in range(2):
                        self.mm(psS, psS.ap[0:CH, 0:CH], kd, kd.ap[:, dkc, t0:t0 + CH], qd, qd.ap[:, dkc, t0:t0 + CH], start=(dkc == 0), stop=(dkc == 1))
                    self.tt("dve", scT, scT.ap, psS, psS.ap[0:CH, 0:CH], self.cst, tri, ALU.mult)
                    for dkc in range(2):
                        self.act(kct, kct.ap[:, dkc, :], kd, kdf[:, dkc, t0:t0 + CH], AF.Copy, scale=ebend.ap[:, dkc, c:c + 1], R=[ebend])
                        self.tp(psT, psT.ap[0:CH, tof + dkc * 128:tof + (dkc + 1) * 128], kct, kct.ap[:, dkc, :], self.ident)
                    self.cp("act", kendT, kendT.ap, psT, psT.ap[0:CH, tof:tof + 256])

                def main(qi):
                    c = seq[qi]
                    t0 = c * CH
                    v_ = vr[(vbase + qi) % 2]
                    scT, kendT = scTs[qi % 3], kendTs[qi % 3]
                    So, Sn = Srs[qi % 2], Srs[(qi + 1) % 2]
                    for dkc in range(2):
                        pu = psU[dkc]
                        self.mm(pu, pu.ap, kendT, kendT.ap[:, dkc * 128:(dkc + 1) * 128], v_, v_.ap)
                    po = psO[qi % 2]
                    for dvc in range(4):
                        oap = po.ap[:, dvc * CH:(dvc + 1) * CH]
                        self.mm(po, oap, v_, v_.ap[:, dvc * 128:(dvc + 1) * 128], scT, scT.ap, start=True, stop=False)
                        self.mm(po, oap, So, So.ap[:, 0, dvc * 128:(dvc + 1) * 128], qd, qd.ap[:, 0, t0:t0 + CH], start=False, stop=False)
                        self.mm(po, oap, So, So.ap[:, 1, dvc * 128:(dvc + 1) * 128], qd, qd.ap[:, 1, t0:t0 + CH], start=False, stop=True)
                    for dkc in range(2):
                        pu = psU[dkc]
                        self.stt("dve", S, S.ap[:, dkc, :], S, S.ap[:, dkc, :], ebend.ap[:, dkc, c:c + 1], pu, pu.ap, ALU.mult, ALU.add, R=[ebend])
                        self.cp("dve", Sn, Sn.ap[:, dkc, :], S, S.ap[:, dkc, :])
                    po3 = po.ap[:, 0:4 * CH].rearrange("p (a b) -> p a b", b=CH)
                    if d == 0:
                        self.cp("act", OT, OT.ap[:, :, t0:t0 + CH], po, po3)
                    else:
                        self.tt("dve", OT, OT.ap[:, :, t0:t0 + CH], OT, OT.ap[:, :, t0:t0 + CH], po, po3, ALU.add)

                pre(0)
                for qi in range(len(seq)):
                    if qi + 1 < len(seq):
                        pre(qi + 1)
                    main(qi)
            k = 0
            for (t0, n, s) in BLKS:
                for dvc in range(4):
                    q_ = sq[dvc % 2]
                    self.act(q_, q_.ap[:, 0:n], OT, OT.ap[:, dvc, t0:t0 + n], AF.Square)
                    self.mm(psN, psN.ap[:, 0:n], self.onesr, self.onesr.ap, q_, q_.ap[:, 0:n], start=(dvc == 0), stop=(dvc == 3))
                self.rsqrt_from(rstd, rstd.ap[:, 0:n], psN, psN.ap[:, 0:n], 1.0 / 512)
                for dvc in range(4):
                    z_, y_, o_ = zt[k % 2], yt[k % 2], ob[k % 2]
                    k += 1
                    zr = 4096 + hd * 512 + dvc * 128
                    self.ld(z_, z_.ap[:, 0:n], self.PROJT[zr:zr + 128, t0:t0 + n])
                    self.stt("dve", y_, y_.ap[:, 0:n], OT, OT.ap[:, dvc, t0:t0 + n], gg.ap[:, dvc:dvc + 1], rstd, rstd.ap[:, 0:n], ALU.mult, ALU.mult, R=[gg])
                    self.act(z_, z_.ap[:, 0:n], z_, z_.ap[:, 0:n], AF.Silu)
                    self.tt("dve", o_, o_.ap[:, 0:n], y_, y_.ap[:, 0:n], z_, z_.ap[:, 0:n], ALU.mult)
                    mr = hd * 512 + dvc * 128
                    self.st(self.MRGT[mr:mr + 128, t0:t0 + n], o_, o_.ap[:, 0:n])


def _consts():
    c = np.zeros((6, 128, 128), np.float32)
    c[0] = np.eye(128)
    c[1] = 1.0
    c[2, :64, :64] = 1.0
    c[2, 64:, 64:] = 1.0
    for m in range(2):
        o = m * 64
        for i in range(16):
            c[3, o + 16 + i, o + i] = -1.0
            c[3, o + i, o + 16 + i] = 1.0
            c[3, o + 48 + i, o + 32 + i] = -1.0
            c[3, o + 32 + i, o + 48 + i] = 1.0
    s = np.arange(128)
    c[4] = (s[:, None] <= s[None, :])
    c[5] = (s[:, None] >= s[None, :])
    return c


def _rope_tables():
    GRID_W = 64
    L = 2048
    row = np.repeat(np.arange(L // GRID_W, dtype=np.float32), GRID_W)
    col = np.tile(np.arange(GRID_W, dtype=np.float32), L // GRID_W)
    n_freq = 16
    inv_freq = (np.float32(10000.0) ** (-np.arange(n_freq, dtype=np.float32) / np.float32(n_freq))).astype(np.float32)
    ang_r = row[:, None] * inv_freq
    ang_c = col[:, None] * inv_freq
    ang = np.concatenate([ang_r, ang_r, ang_c, ang_c], axis=-1).astype(np.float32)
    cos, sin = np.cos(ang).astype(np.float32), np.sin(ang).astype(np.float32)
    t = np.zeros((2, 128, L), np.float32)
    t[0] = np.concatenate([cos.T, cos.T], axis=0)
    t[1] = np.concatenate([sin.T, sin.T], axis=0)
    return t


def _prep_shared(inp):
    f = lambda a: np.ascontiguousarray(a, dtype=np.float32)
    sh = {}
    sh["ada_w"] = f(inp["ada_w"])
    sh["adabT"] = f(inp["ada_b"].reshape(DEPTH, 48, 128).transpose(0, 2, 1))
    sh["ngT"] = f(inp["norm_g"].reshape(DEPTH, 16, 128).transpose(0, 2, 1))
    sh["w_in"] = f(np.stack([inp["ev_w_in"][0], inp["od_w_in"][0], inp["ev_w_in"][1], inp["od_w_in"][1]]))
    sh["w_out"] = f(np.stack([inp["ev_w_out"][0], inp["od_w_out"][0], inp["ev_w_out"][1], inp["od_w_out"][1]]))
    sh["consts"] = _consts()
    p = np.zeros((2, 2, 128, 3, 32), np.float32)
    for k, name in enumerate(["s5_a_re", "s5_a_im"]):
        a = inp[name].reshape(2, 2, 32, 2, 64)
        p[:, :, :, k, :] = a.transpose(0, 1, 3, 4, 2).reshape(2, 2, 128, 32)
    ldt = np.broadcast_to(inp["s5_log_dt"].reshape(2, 2, 32, 2, 1), (2, 2, 32, 2, 64))
    p[:, :, :, 2, :] = ldt.transpose(0, 1, 3, 4, 2).reshape(2, 2, 128, 32)
    sh["s5p"] = p
    Bm = np.zeros((2, 2, 2, 32, 128, 128), np.float32)
    Cm = np.zeros((2, 2, 2, 32, 128, 128), np.float32)
    for ri, (bn, cn) in enumerate([("s5_b_re", "s5_c_re"), ("s5_b_im", "s5_c_im")]):
        b = inp[bn]
        c = inp[cn]
        for st in range(32):
            for gi in range(2):
                g = 2 * st + gi
                g8 = g % 8
                Bm[:, :, ri, st, g8 * 16:(g8 + 1) * 16, gi * 64:(gi + 1) * 64] = b[:, :, g].transpose(0, 1, 3, 2)
                Cm[:, :, ri, st, gi * 64:(gi + 1) * 64, g8 * 16:(g8 + 1) * 16] = c[:, :, g].transpose(0, 1, 3, 2)
    sh["s5B"], sh["s5C"] = Bm, Cm
    sh["s5d"] = f(inp["s5_d"].reshape(2, 8, 128).transpose(0, 2, 1))
    tau = np.zeros((4, 128, TC), np.float32)
    tau[0] = np.arange(TC)[None, :]
    tau[1] = (TC - 1 - np.arange(TC))[None, :]
    tau[2] = 1.0; tau[2, :, 0] = 0.0
    tau[3] = 1.0; tau[3, :, TC - 1] = 0.0
    sh["s5tau"] = tau
    sh["wglu"] = f(inp["s5_w_glu"])
    dq = np.zeros((2, 128, 3), np.float32)
    dq[:, :, 0] = np.tile(inp["da_qn_g"], (1, 2))
    dq[:, :, 1] = np.tile(inp["da_kn_g"], (1, 2))
    dq[:, :, 2] = inp["da_subln_g"]
    sh["daq"] = dq
    sh["lamT"] = f(inp["da_lam"].transpose(0, 2, 1))
    sh["rope"] = _rope_tables()
    sh["wa1"] = f(inp["gla_wa1"])
    sh["wa2"] = f(inp["gla_wa2"])
    sh["baT"] = f(inp["gla_ba"].reshape(2, 2, 8, 128).transpose(0, 1, 3, 2))
    sh["glag"] = f(inp["gla_norm_g"].reshape(2, 4, 128).transpose(0, 2, 1))
    gm = np.ones((2, 128, NT), np.float32)
    gm[0, :, 0::128] = 0.0
    gm[1, :, 127::128] = 0.0
    sh["gmask"] = gm
    return sh


def _prep_core(inp, b):
    d = {}
    d["xin"] = np.ascontiguousarray(np.concatenate([inp["ctx"][b], inp["x"][b]], axis=0), dtype=np.float32)
    cT = np.zeros((128, 16, 2), np.float32)
    cT[:, :, 0] = inp["c"][b].reshape(16, 128).T
    cT[:, :, 1] = inp["c_ctx"].reshape(16, 128).T
    d["cT"] = cT
    return d


_NC_CACHE = {}


def kernel(**inputs):
    inp = {k: np.asarray(v) for k, v in inputs.items()}
    if "full" not in _NC_CACHE:
        _NC_CACHE["full"] = KB().build()
    nc = _NC_CACHE["full"]
    sh = _prep_shared(inp)
    in_maps = []
    for core in range(8):
        m = dict(sh)
        m.update(_prep_core(inp, core % 4))
        in_maps.append(m)
    res = run_bass_kernel_spmd(nc, in_maps, core_ids=list(range(8)))
    out = np.stack([np.asarray(res.results[b]["out"]) for b in range(4)], axis=0)
    return out.astype(np.float32)
```

```python
import math
import numpy as np
import concourse.bass as bass
import concourse.mybir as mybir
from concourse.bass_utils import run_bass_kernel_spmd

F32 = mybir.dt.float32
F32R = mybir.dt.float32r
I32 = mybir.dt.int32
AF = mybir.ActivationFunctionType
ALU = mybir.AluOpType

D = 2048
NT = 2304
NCTX = 256
DEPTH = 4
EPS = 1e-6
PI = math.pi
BLKS = [(0, 256, 1)] + [(256 + 512 * i, 512, 0) for i in range(4)]
TC = 128


class Buf:
    __slots__ = ("w", "r")

    def __init__(self):
        self.w = None
        self.r = {}


class T:
    __slots__ = ("ap", "b")

    def __init__(self, ap):
        self.ap = ap
        self.b = Buf()

    def __getitem__(self, k):
        return self.ap[k]


class FW:
    NDMA = 24
    SEM_ROLL = 30000

    def __init__(self, nc):
        self.nc = nc
        self.engs = {"pe": nc.tensor, "act": nc.scalar, "dve": nc.vector, "pool": nc.gpsimd, "sp": nc.sync}
        self.ops = {e: [] for e in self.engs}
        self.sems, self.cnt, self.cur, self.owner = {}, {}, {}, {}
        self.seen = {e: {} for e in self.engs}
        self.nsem = 0
        for e in self.engs:
            self._new_csem(e)
        self.dma_keys = []
        for i in range(self.NDMA):
            k = f"dma{i}"
            self.sems[k] = nc.alloc_semaphore(k)
            self.cnt[k] = 0
            self.dma_keys.append(k)
        self.dma_rr = 0
        self.out_tickets = []
        self.n_ins = 0

    def _new_csem(self, e):
        k = f"c_{e}_{self.nsem}"
        self.nsem += 1
        self.sems[k] = self.nc.alloc_semaphore(k)
        self.cnt[k] = 0
        self.cur[e] = k
        self.owner[k] = e

    def _waits(self, e, reads, writes, extra=()):
        need = {}

        def add(k, v):
            if need.get(k, 0) < v:
                need[k] = v
        for t in reads:
            if t.b.w is not None:
                add(*t.b.w)
        for t in writes:
            if t.b.w is not None:
                add(*t.b.w)
            for k, v in t.b.r.items():
                add(k, v)
        for k, v in extra:
            add(k, v)
        out = []
        seen = self.seen[e]
        for k, v in need.items():
            if seen.get(k, 0) >= v:
                continue
            if self.owner.get(k) == e and e == "pe":
                continue
            seen[k] = v
            out.append((k, v))
        return out

    def _mark(self, t, reads, writes):
        k, v = t
        for x in reads:
            if x.b.r.get(k, 0) < v:
                x.b.r[k] = v
        for x in writes:
            x.b.w = t
            x.b.r = {}

    def op(self, e, fn, R=(), W=()):
        waits = self._waits(e, R, W)
        k = self.cur[e]
        if self.cnt[k] >= self.SEM_ROLL:
            self._new_csem(e)
            k = self.cur[e]
        self.cnt[k] += 1
        t = (k, self.cnt[k])
        self.ops[e].append((waits, fn, (k, 1)))
        self._mark(t, R, W)
        self.n_ins += 1
        return t

    def dma(self, out, in_, R=(), W=(), q="sp", is_output=False, **kw):
        k = self.dma_keys[self.dma_rr % self.NDMA]
        self.dma_rr += 1
        extra = [(k, self.cnt[k])] if self.cnt[k] > 0 else []
        waits = self._waits(q, R, W, extra)
        self.cnt[k] += 16
        t = (k, self.cnt[k])

        def fn(eng, out=out, in_=in_, kw=kw):
            return eng.dma_start(out=out, in_=in_, **kw)
        self.ops[q].append((waits, fn, (k, 16)))
        self._mark(t, R, W)
        if is_output:
            self.out_tickets.append(t)
        self.n_ins += 1
        return t

    def barrier(self):
        allv = [(k, v) for k, v in self.cnt.items() if v > 0]
        for e in self.engs:
            seen = self.seen[e]
            waits = []
            for k, v in allv:
                if self.owner.get(k) == e and e == "pe":
                    continue
                if seen.get(k, 0) < v:
                    seen[k] = v
                    waits.append((k, v))
            if waits:
                self.ops[e].append((waits, None, None))

    def finish(self):
        self.barrier()
        nc, sems = self.nc, self.sems
        with nc.Block() as block:
            def mk(e):
                def body(eng):
                    for waits, fn, inc in self.ops[e]:
                        for k, v in waits:
                            eng.wait_ge(sems[k], v)
                        if fn is not None:
                            fn(eng).then_inc(sems[inc[0]], inc[1])
                return body
            block.sync(mk("sp"))
            block.scalar(mk("act"))
            block.vector(mk("dve"))
            block.gpsimd(mk("pool"))
            block.tensor(mk("pe"))


class KB:
    ARENA_WORDS = 52000

    def __init__(self, nlayers=DEPTH, debug=()):
        self.nlayers = nlayers
        self.debug = set(debug)
        nc = self.nc = bass.Bass("TRN2", target_bir_lowering=False)
        self.fw = FW(nc)
        arena = nc.alloc_sbuf_tensor("arena", [128, self.ARENA_WORDS], F32)
        self.arena_addr = int(nc.lookup_mloc(arena).addr)
        self.ntile = 0
        self.base = 0
        self.off = 0
        self.psum = nc.alloc_psum_tensor("psum", [128, 4096], F32).ap()
        self.PS = [T(self.psum[:, i * 512:(i + 1) * 512]) for i in range(8)]
        self.din = {}
        self.rr = 0

    def inp(self, name, shape):
        ap = self.nc.dram_tensor(name, list(shape), F32, kind="ExternalInput").ap()
        self.din[name] = ap
        return ap

    def scratch(self, name, shape):
        kind = "ExternalOutput" if name in self.debug else "Internal"
        return self.nc.dram_tensor(name, list(shape), F32, kind=kind).ap()

    def tile(self, shape, dt=F32):
        n = int(np.prod(shape[1:]))
        assert self.off + n <= self.ARENA_WORDS, ("SBUF overflow", self.off, n)
        self.ntile += 1
        h = self.nc.alloc_sbuf_tensor_at(f"t{self.ntile}", [int(x) for x in shape], dt, offset=self.arena_addr + 4 * self.off)
        self.off += (n + 7) // 8 * 8
        return T(h.ap())

    def new_phase(self):
        self.fw.barrier()
        self.off = self.base

    def op(self, e, fn, R=(), W=()):
        return self.fw.op(e, fn, R, W)

    def mm(self, out, o_ap, lt, l_ap, rt, r_ap, start=True, stop=True):
        self.fw.op("pe", lambda e: e.matmul(o_ap, lhsT=l_ap, rhs=r_ap, start=start, stop=stop), [lt, rt], [out])

    def tp(self, out, o_ap, it, i_ap, ident):
        self.fw.op("pe", lambda e: e.transpose(o_ap, i_ap, ident.ap[0:i_ap.shape[0], 0:i_ap.shape[0]]), [it, ident], [out])

    def tt(self, e, out, o_ap, a, a_ap, b, b_ap, op):
        self.fw.op(e, lambda g: g.tensor_tensor(o_ap, a_ap, b_ap, op), [a, b], [out])

    def ts(self, e, out, o_ap, a, a_ap, s1, s2, op0, op1=None, R=()):
        if op1 is None:
            self.fw.op(e, lambda g: g.tensor_scalar(o_ap, a_ap, s1, None, op0), [a] + list(R), [out])
        else:
            self.fw.op(e, lambda g: g.tensor_scalar(o_ap, a_ap, s1, s2, op0, op1), [a] + list(R), [out])

    def stt(self, e, out, o_ap, a, a_ap, sc, b, b_ap, op0, op1, R=()):
        e = "dve"
        self.fw.op(e, lambda g: g.scalar_tensor_tensor(o_ap, a_ap, sc, b_ap, op0, op1), [a, b] + list(R), [out])

    def act(self, out, o_ap, a, a_ap, func, bias=None, scale=None, R=()):
        kw = {}
        if bias is not None:
            kw["bias"] = bias
        if scale is not None:
            kw["scale"] = scale
        self.fw.op("act", lambda g: g.activation(o_ap, a_ap, func, **kw), [a] + list(R), [out])

    def cp(self, e, out, o_ap, a, a_ap):
        if e == "act":
            self.fw.op("act", lambda g: g.copy(o_ap, a_ap), [a], [out])
        else:
            self.fw.op(e, lambda g: g.tensor_copy(o_ap, a_ap), [a], [out])

    def alt(self, engines=("dve", "act")):
        self.rr += 1
        return engines[self.rr % len(engines)]

    def ld(self, out, o_ap, src, q="sp"):
        self.fw.dma(o_ap, src, W=[out], q=q)

    def st(self, dst, it, i_ap, q="act", is_output=False):
        self.fw.dma(dst, i_ap, R=[it], q=q, is_output=is_output)

    def rsqrt_from(self, out, o_ap, src, s_ap, inv_n):
        self.act(out, o_ap, src, s_ap, AF.Ln, bias=self.epsT.ap[:, 0:1], scale=inv_n, R=[self.epsT])
        self.act(out, o_ap, out, o_ap, AF.Exp, scale=-0.5)

    def build(self):
        nc = self.nc
        L = self.nlayers
        xin = self.inp("xin", [NT, D])
        cT_d = self.inp("cT", [128, 16, 2])
        ada_w = self.inp("ada_w", [DEPTH, D, 3 * D])
        adabT = self.inp("adabT", [DEPTH, 128, 48])
        ngT = self.inp("ngT", [DEPTH, 128, 16])
        w_in = self.inp("w_in", [DEPTH, D, 3 * D])
        w_out = self.inp("w_out", [DEPTH, D, D])
        consts = self.inp("consts", [6, 128, 128])
        s5p = self.inp("s5p", [2, 2, 128, 3, 32])
        s5B = self.inp("s5B", [2, 2, 2, 32, 128, 128])
        s5C = self.inp("s5C", [2, 2, 2, 32, 128, 128])
        s5d = self.inp("s5d", [2, 128, 8])
        s5tau = self.inp("s5tau", [4, 128, TC])
        wglu = self.inp("wglu", [2, 1024, 1024])
        daq = self.inp("daq", [2, 128, 3])
        lamT = self.inp("lamT", [2, 64, 4])
        rope = self.inp("rope", [2, 128, 2048])
        wa1 = self.inp("wa1", [2, 2, D, 16])
        wa2 = self.inp("wa2", [2, 2, 16, 1024])
        baT = self.inp("baT", [2, 2, 128, 8])
        glag = self.inp("glag", [2, 128, 4])
        gmask = self.inp("gmask", [2, 128, NT])
        out_d = nc.dram_tensor("out", [2048, D], F32, kind="ExternalOutput").ap()

        self.XT = self.scratch("XT", [D, NT])
        self.PROJT = self.scratch("PROJT", [3 * D, NT])
        self.VTOK = self.scratch("VTOK", [NT, D])
        self.MRGT = self.scratch("MRGT", [D, NT])
        self.GT = self.scratch("GT", [1024, NT])
        self.LAT = self.scratch("LAT", [2, 1024, NT])

        self.cst = self.tile([128, 6, 128])
        self.ld(self.cst, self.cst.ap, consts.rearrange("c p n -> p c n"))
        self.ident = T(self.cst.ap[:, 0, :]); self.ident.b = self.cst.b
        self.onesr = self.tile([128, 128], F32R)
        self.blk64r = self.tile([128, 128], F32R)
        self.protr = self.tile([128, 128], F32R)
        self.cp("dve", self.onesr, self.onesr.ap, self.cst, self.cst.ap[:, 1, :])
        self.cp("dve", self.blk64r, self.blk64r.ap, self.cst, self.cst.ap[:, 2, :])
        self.cp("dve", self.protr, self.protr.ap, self.cst, self.cst.ap[:, 3, :])
        self.epsT = self.tile([128, 1])
        self.op("dve", lambda g: g.memset(self.epsT.ap, EPS), W=[self.epsT])
        self.modv = self.tile([128, 48, 2])
        self.gs = self.tile([128, 16, 2])
        self.cT = self.tile([128, 16, 2])
        self.sc = self.tile([128, 16, 2])
        self.ld(self.cT, self.cT.ap, cT_d)
        self.act(self.sc, self.sc.ap, self.cT, self.cT.ap, AF.Silu)
        self.base = self.off

        self.phase_in_transpose(xin)
        for l in range(L):
            j = l // 2
            self.phase_mod(l, ada_w, adabT, ngT)
            if l % 2 == 0:
                self.phase_inproj(l, w_in, tok_chunks=())
                if "stop_proj" in self.debug:
                    break
                self.phase_s5(j, s5p, s5B, s5C, s5d, s5tau, wglu)
                if "stop_s5" in self.debug:
                    break
                lam_init = 0.8 - 0.6 * math.exp(-0.3 * l)
                self.phase_da(j, daq, lamT, rope, lam_init)
                if "stop_da" in self.debug:
                    break
            else:
                self.phase_inproj(l, w_in, tok_chunks=(), gla=(j, wa1, wa2, baT))
                self.phase_gla(j, glag, gmask)
                if "stop_gla" in self.debug:
                    break
            self.phase_outproj(l, w_out)
        self.phase_out_transpose(out_d)
        self.fw.finish()
        return nc

    def phase_in_transpose(self, xin):
        self.new_phase()
        xt = [self.tile([128, D]) for _ in range(2)]
        stg = [self.tile([128, 16, 128]) for _ in range(2)]
        XTv = self.XT.rearrange("(kc p) t -> p kc t", p=128)
        for tt in range(18):
            x = xt[tt % 2]
            s = stg[tt % 2]
            self.ld(x, x.ap, xin[tt * 128:(tt + 1) * 128, :])
            for g4 in range(4):
                ps = self.PS[g4 % 4]
                for i in range(4):
                    kc = g4 * 4 + i
                    self.tp(ps, ps.ap[:, i * 128:(i + 1) * 128], x, x.ap[:, kc * 128:(kc + 1) * 128], self.ident)
                e = "act" if g4 % 2 else "dve"
                self.cp(e, s, s.ap[:, g4 * 4:(g4 + 1) * 4, :], ps, ps.ap.rearrange("p (a b) -> p a b", b=128))
            self.st(XTv[:, :, tt * 128:(tt + 1) * 128], s, s.ap)

    def phase_out_transpose(self, out_d):
        self.new_phase()
        xt = [self.tile([128, 16, 128]) for _ in range(2)]
        stg = [self.tile([128, D]) for _ in range(2)]
        XTv = self.XT.rearrange("(kc p) t -> p kc t", p=128)
        for tt in range(16):
            x = xt[tt % 2]
            s = stg[tt % 2]
            t0 = NCTX + tt * 128
            self.ld(x, x.ap, XTv[:, :, t0:t0 + 128])
            for g4 in range(4):
                ps = self.PS[g4 % 4]
                for i in range(4):
                    kc = g4 * 4 + i
                    self.tp(ps, ps.ap[:, i * 128:(i + 1) * 128], x, x.ap[:, kc, :], self.ident)
                e = "act" if g4 % 2 else "dve"
                self.cp(e, s, s.ap[:, g4 * 512:(g4 + 1) * 512], ps, ps.ap)
            self.st(out_d[tt * 128:(tt + 1) * 128, :], s, s.ap, is_output=True)

    def phase_mod(self, l, ada_w, adabT, ngT):
        self.new_phase()
        wb = [self.tile([128, 16, 512]) for _ in range(2)]
        adab = self.tile([128, 48])
        ng = self.tile([128, 16])
        self.ld(adab, adab.ap, adabT[l])
        self.ld(ng, ng.ap, ngT[l])
        Wv = ada_w[l].rearrange("(kc p) n -> p kc n", p=128)
        pm = self.PS[0]
        for cc in range(12):
            w = wb[cc % 2]
            self.ld(w, w.ap, Wv[:, :, cc * 512:(cc + 1) * 512])
            for jj in range(4):
                j = cc * 4 + jj
                for kc in range(16):
                    self.mm(pm, pm.ap[:, 2 * j:2 * j + 2], w, w.ap[:, kc, jj * 128:(jj + 1) * 128],
                            self.sc, self.sc.ap[:, kc, :], start=(kc == 0), stop=(kc == 15))
        pm3 = pm.ap[:, 0:96].rearrange("p (j s) -> p j s", s=2)
        for s in range(2):
            self.tt("dve", self.modv, self.modv.ap[:, :, s], pm, pm3[:, :, s], adab, adab.ap, ALU.add)
        for s in range(2):
            self.stt("dve", self.gs, self.gs.ap[:, :, s], self.modv, self.modv.ap[:, 16:32, s], 1.0, ng, ng.ap, ALU.add, ALU.mult)

    def phase_inproj(self, l, w_in, tok_chunks=(), gla=None):
        self.new_phase()
        hR = self.tile([128, 16, NT], F32R)
        hb = [T(hR.ap) for _ in BLKS]
        self._wraw_B = self.tile([128, 16, 128])
        xc = [self.tile([128, 512]) for _ in range(3)]
        tm = [self.tile([128, 512]) for _ in range(2)]
        sq = [self.tile([128, 512], F32R) for _ in range(2)]
        rstd = self.tile([128, 512])
        XTv = self.XT.rearrange("(kc p) t -> p kc t", p=128)
        xi = 0
        for bi, (t0, n, s) in enumerate(BLKS):
            h = hb[bi]
            ps = self.PS[bi % 2]
            for kc in range(16):
                x = xc[xi % 3]; xi += 1
                self.ld(x, x.ap[:, 0:n], XTv[:, kc, t0:t0 + n])
                q = sq[kc % 2]
                self.act(q, q.ap[:, 0:n], x, x.ap[:, 0:n], AF.Square)
                self.mm(ps, ps.ap[:, 0:n], self.onesr, self.onesr.ap, q, q.ap[:, 0:n], start=(kc == 0), stop=(kc == 15))
            self.rsqrt_from(rstd, rstd.ap[:, 0:n], ps, ps.ap[:, 0:n], 1.0 / D)
            for kc in range(16):
                x = xc[xi % 3]; xi += 1
                self.ld(x, x.ap[:, 0:n], XTv[:, kc, t0:t0 + n])
                t = tm[kc % 2]
                self.tt("dve", t, t.ap[:, 0:n], x, x.ap[:, 0:n], rstd, rstd.ap[:, 0:n], ALU.mult)
                self.ts("dve", h, hR.ap[:, kc, t0:t0 + n], t, t.ap[:, 0:n],
                        self.gs.ap[:, kc, s:s + 1], self.modv.ap[:, kc, s:s + 1], ALU.mult, ALU.add, R=[self.gs, self.modv])
        if "hT" in self.debug:
            dbg = self.scratch("hT", [D, NT])
            dt_ = self._wraw_B
            for tt in range(18):
                self.cp("dve", dt_, dt_.ap, hb[0 if tt < 2 else 1 + (tt - 2) // 4], hR.ap[:, :, tt * 128:(tt + 1) * 128])
                self.fw.dma(dbg.rearrange("(kc p) t -> p kc t", p=128)[:, :, tt * 128:(tt + 1) * 128], dt_.ap, R=[dt_], q="act")
        wraw = self._wraw_B
        wr = [self.tile([128, 16, 128], F32R) for _ in range(2)]
        ostb = [self.tile([128, n]) for (t0, n, s) in BLKS]
        Wv = w_in[l].rearrange("(kc p) n -> p kc n", p=128)
        VT = self.VTOK.rearrange("(tt p) n -> p tt n", p=128)
        tok_chunks = set(tok_chunks)
        pi = 0
        for j in range(48):
            w = wr[j % 2]
            self.ld(wraw, wraw.ap, Wv[:, :, j * 128:(j + 1) * 128])
            self.cp("pool", w, w.ap, wraw, wraw.ap)
            if j in tok_chunks:
                jj = j - min(tok_chunks)
                for g in range(5):
                    ps = self.PS[2 + pi % 4]; pi += 1
                    tts = list(range(g * 4, min(g * 4 + 4, 18)))
                    o = ostb[g if len(tts) == 4 and g > 0 else (1 if g == 0 else 0)]
                    for ii, tt in enumerate(tts):
                        bi = 0 if tt < 2 else 1 + (tt - 2) // 4
                        for kc in range(16):
                            self.mm(ps, ps.ap[:, ii * 128:(ii + 1) * 128], hb[bi], hR.ap[:, kc, tt * 128:(tt + 1) * 128],
                                    w, w.ap[:, kc, :], start=(kc == 0), stop=(kc == 15))
                    nn = len(tts) * 128
                    self.cp("act" if g % 2 else "dve", o, o.ap[:, 0:nn], ps, ps.ap[:, 0:nn])
                    self.st(VT[:, tts[0]:tts[-1] + 1, jj * 128:(jj + 1) * 128], o, o.ap[:, 0:nn].rearrange("p (a b) -> p a b", b=128))
            else:
                for bi, (t0, n, s) in enumerate(BLKS):
                    ps = self.PS[2 + pi % 4]; pi += 1
                    o = ostb[bi]
                    for kc in range(16):
                        self.mm(ps, ps.ap[:, 0:n], w, w.ap[:, kc, :], hb[bi], hR.ap[:, kc, t0:t0 + n], start=(kc == 0), stop=(kc == 15))
                    self.cp("act" if bi % 2 else "dve", o, o.ap, ps, ps.ap[:, 0:n])
                    self.st(self.PROJT[j * 128:(j + 1) * 128, t0:t0 + n], o, o.ap)
        if gla is not None:
            self.gla_gate(gla, hb, hR, wraw, wr, ostb, sq, tm)

    def gla_gate(self, gla, hb, hR, wraw, wr, ostb, sq, tm):
        j, wa1, wa2, baT = gla
        nba = self.tile([128, 8])
        wflat = wraw.ap.rearrange("p a b -> p (a b)")
        w2 = T(wr[1].ap.rearrange("p a b -> p (a b)")[:, 0:1024]); w2.b = wr[1].b
        w1 = wr[0]
        lr = sq[0]
        k = 0
        for d in range(2):
            self.op("pool", lambda g: g.memset(wraw.ap, 0.0), W=[wraw])
            self.ld(wraw, wraw.ap[:, :, 0:16], wa1[j, d].rearrange("(kc p) r -> p kc r", p=128))
            self.cp("pool", w1, w1.ap, wraw, wraw.ap)
            self.op("pool", lambda g: g.memset(wraw.ap, 0.0), W=[wraw])
            self.ld(wraw, wflat[0:16, 0:1024], wa2[j, d])
            self.cp("pool", w2, w2.ap, wraw, wflat[:, 0:1024])
            self.ld(nba, nba.ap, baT[j, d])
            self.ts("dve", nba, nba.ap, nba, nba.ap, -1.0, None, ALU.mult)
            for bi, (t0, n, s) in enumerate(BLKS):
                ps = self.PS[bi % 2]
                for kc in range(16):
                    self.mm(ps, ps.ap[:, 0:n], w1, w1.ap[:, kc, :], hb[bi], hR.ap[:, kc, t0:t0 + n], start=(kc == 0), stop=(kc == 15))
                self.cp("dve", lr, lr.ap[:, 0:n], ps, ps.ap[:, 0:n])
                for c in range(8):
                    o = ostb[1 + k % 4]
                    t_ = tm[k % 2]
                    ps2 = self.PS[2 + k % 4]
                    k += 1
                    self.mm(ps2, ps2.ap[:, 0:n], w2, w2.ap[:, c * 128:(c + 1) * 128], lr, lr.ap[:, 0:n])
                    self.act(t_, t_.ap[:, 0:n], ps2, ps2.ap[:, 0:n], AF.Exp, bias=nba.ap[:, c:c + 1], scale=-1.0, R=[nba])
                    self.act(t_, t_.ap[:, 0:n], t_, t_.ap[:, 0:n], AF.Ln, bias=1.0)
                    self.ts("dve", o, o.ap[:, 0:n], t_, t_.ap[:, 0:n], -1.0 / 16.0, None, ALU.mult)
                    self.st(self.LAT[d, c * 128:(c + 1) * 128, t0:t0 + n], o, o.ap[:, 0:n])

    def phase_outproj(self, l, w_out):
        self.new_phase()
        mR = self.tile([128, 16, NT], F32R)
        mb = [T(mR.ap) for _ in BLKS]
        xc = [self.tile([128, 512]) for _ in range(3)]
        Mv = self.MRGT.rearrange("(kc p) t -> p kc t", p=128)
        xi = 0
        for bi, (t0, n, s) in enumerate(BLKS):
            for kc in range(16):
                x = xc[xi % 3]; xi += 1
                self.ld(x, x.ap[:, 0:n], Mv[:, kc, t0:t0 + n])
                self.cp(self.alt(), mb[bi], mR.ap[:, kc, t0:t0 + n], x, x.ap[:, 0:n])
        wraw = self.tile([128, 16, 128])
        wr = [self.tile([128, 16, 128], F32R) for _ in range(2)]
        xo = [self.tile([128, n]) for (t0, n, s) in BLKS]
        Wv = w_out[l].rearrange("(kc p) n -> p kc n", p=128)
        pi = 0
        for j in range(16):
            w = wr[j % 2]
            self.ld(wraw, wraw.ap, Wv[:, :, j * 128:(j + 1) * 128])
            self.cp("pool", w, w.ap, wraw, wraw.ap)
            for bi, (t0, n, s) in enumerate(BLKS):
                x = xo[bi]
                self.ld(x, x.ap, self.XT[j * 128:(j + 1) * 128, t0:t0 + n])
                ps = self.PS[pi % 4]; pi += 1
                for kc in range(16):
                    self.mm(ps, ps.ap[:, 0:n], w, w.ap[:, kc, :], mb[bi], mR.ap[:, kc, t0:t0 + n], start=(kc == 0), stop=(kc == 15))
                self.stt("dve", x, x.ap, ps, ps.ap[:, 0:n], self.modv.ap[:, 32 + j, s:s + 1], x, x.ap,
                         ALU.mult, ALU.add, R=[self.modv])
                self.st(self.XT[j * 128:(j + 1) * 128, t0:t0 + n], x, x.ap)

    def sincos(self, ang, n, sin_out=None, cos_out=None, scr=None):
        u, ki, rd = scr
        for out, shift in ((sin_out, 0.0), (cos_out, PI / 2)):
            if out is None:
                continue
            self.ts("dve", u, u.ap[:, 0:n], ang, ang.ap[:, 0:n], 1.0 / (2 * PI), shift / (2 * PI), ALU.mult, ALU.add)
            self.cp("dve", ki, ki.ap[:, 0:n], u, u.ap[:, 0:n])
            self.cp("dve", u, u.ap[:, 0:n], ki, ki.ap[:, 0:n])
            self.stt("dve", rd, rd.ap[:, 0:n], u, u.ap[:, 0:n], -2 * PI, ang, ang.ap[:, 0:n], ALU.mult, ALU.add)
            self.ts("dve", rd, rd.ap[:, 0:n], rd, rd.ap[:, 0:n], shift, 3.1415925, ALU.add, ALU.min)
            self.ts("dve", rd, rd.ap[:, 0:n], rd, rd.ap[:, 0:n], -3.1415925, None, ALU.max)
            self.act(out, out.ap[:, 0:n], rd, rd.ap[:, 0:n], AF.Sin)

    def phase_s5(self, j, s5p, s5B, s5C, s5d, s5tau, wglu):
        self.new_phase()
        W = 8 * TC
        tau = self.tile([128, 4, TC])
        self.ld(tau, tau.ap, s5tau.rearrange("k p t -> p k t"))
        dsk = self.tile([128, 8])
        self.ld(dsk, dsk.ap, s5d[j])
        t1, t2, t3, t4 = [self.tile([128, W]) for _ in range(4)]
        ki = self.tile([128, W], I32)
        scr = (t1, ki, t2)
        ang = t3
        PR = []
        for d in range(2):
            prm = self.tile([128, 3, 32])
            self.ld(prm, prm.ap, s5p[j, d])
            names = "dt r th sn cs fre fim nfre c1 s1 t1 t2 den".split()
            c = {k: self.tile([128, 32]) for k in names}
            are, aim, ldt = prm.ap[:, 0, :], prm.ap[:, 1, :], prm.ap[:, 2, :]
            self.act(c["dt"], c["dt"].ap, prm, ldt, AF.Exp)
            self.tt("dve", c["t1"], c["t1"].ap, prm, are, c["dt"], c["dt"].ap, ALU.mult)
            self.act(c["r"], c["r"].ap, c["t1"], c["t1"].ap, AF.Exp)
            self.tt("dve", c["th"], c["th"].ap, prm, aim, c["dt"], c["dt"].ap, ALU.mult)
            self.sincos(c["th"], 32, c["sn"], c["cs"], scr)
            self.tt("dve", c["t1"], c["t1"].ap, c["r"], c["r"].ap, c["cs"], c["cs"].ap, ALU.mult)
            self.ts("dve", c["t1"], c["t1"].ap, c["t1"], c["t1"].ap, -1.0, None, ALU.add)
            self.tt("dve", c["t2"], c["t2"].ap, c["r"], c["r"].ap, c["sn"], c["sn"].ap, ALU.mult)
            self.tt("dve", c["den"], c["den"].ap, prm, are, prm, are, ALU.mult)
            self.tt("dve", c["fre"], c["fre"].ap, prm, aim, prm, aim, ALU.mult)
            self.tt("dve", c["den"], c["den"].ap, c["den"], c["den"].ap, c["fre"], c["fre"].ap, ALU.add)
            self.op("dve", lambda g, t=c["den"]: g.reciprocal(t.ap, t.ap), [c["den"]], [c["den"]])
            self.tt("dve", c["fre"], c["fre"].ap, c["t1"], c["t1"].ap, prm, are, ALU.mult)
            self.tt("dve", c["fim"], c["fim"].ap, c["t2"], c["t2"].ap, prm, aim, ALU.mult)
            self.tt("dve", c["fre"], c["fre"].ap, c["fre"], c["fre"].ap, c["fim"], c["fim"].ap, ALU.add)
            self.tt("dve", c["fre"], c["fre"].ap, c["fre"], c["fre"].ap, c["den"], c["den"].ap, ALU.mult)
            self.tt("dve", c["fim"], c["fim"].ap, c["t2"], c["t2"].ap, prm, are, ALU.mult)
            self.tt("dve", c["nfre"], c["nfre"].ap, c["t1"], c["t1"].ap, prm, aim, ALU.mult)
            self.tt("dve", c["fim"], c["fim"].ap, c["fim"], c["fim"].ap, c["nfre"], c["nfre"].ap, ALU.subtract)
            self.tt("dve", c["fim"], c["fim"].ap, c["fim"], c["fim"].ap, c["den"], c["den"].ap, ALU.mult)
            self.ts("dve", c["nfre"], c["nfre"].ap, c["fre"], c["fre"].ap, -1.0, None, ALU.mult)
            self.ts("dve", c["t1"], c["t1"].ap, c["th"], c["th"].ap, float(TC), None, ALU.mult)
            self.sincos(c["t1"], 32, c["s1"], c["c1"], scr)
            self.tt("dve", c["c1"], c["c1"].ap, c["c1"], c["c1"].ap, c["r"], c["r"].ap, ALU.mult)
            self.tt("dve", c["s1"], c["s1"].ap, c["s1"], c["s1"].ap, c["r"], c["r"].ap, ALU.mult)
            PR.append(c)
        sn, cs, nsn, wr, wi, Rm = [self.tile([128, W]) for _ in range(6)]
        bre, bim = self.tile([128, W]), self.tile([128, W])
        ncs = self.tile([128, W])
        Xre = [self.tile([128, W]) for _ in range(2)]
        Xim = [self.tile([128, W]) for _ in range(2)]
        gre = [self.tile([128, W]) for _ in range(2)]
        gim = [self.tile([128, W]) for _ in range(2)]
        uu = [[self.tile([128, W], F32R) for _ in range(4)] for _ in range(2)]
        cr = [self.tile([128, 8]) for _ in range(4)]
        ustg = self.tile([128, NT])
        ur = self.tile([128, 2, NT], F32R)
        yt = self.tile([128, 2, NT])
        BCraw = self.tile([128, 16, 128])
        Br = self.tile([128, 16, 128], F32R)
        Cr = self.tile([128, 16, 128], F32R)
        PRb, PIb = [self.PS[0], self.PS[1]], [self.PS[2], self.PS[3]]
        PYs = [self.PS[4], self.PS[5]]
        pr_ap, pi_ap = self.psum[:, 0:W], self.psum[:, 2 * 512:2 * 512 + W]
        fseq = list(range(18))
        bseq = [1, 0] + list(range(17, 1, -1))
        v3 = lambda t, k: t.ap.rearrange("p (s t) -> p s t", t=TC)[:, :, k]
        for sg in range(4):
            for o in range(2):
                oc = 2 * sg + o
                self.ld(ustg, ustg.ap, self.PROJT[oc * 128:(oc + 1) * 128, :])
                self.cp("dve", ur, ur.ap[:, o, :], ustg, ustg.ap)
            for d in range(2):
                c = PR[d]
                for src, dstR in ((s5B, Br), (s5C, Cr)):
                    for ri in range(2):
                        self.ld(BCraw, BCraw.ap[:, ri::2, :], src[j, d, ri, sg * 8:(sg + 1) * 8].rearrange("s k m -> k s m"))
                    self.cp("pool", dstR, dstR.ap, BCraw, BCraw.ap)
                tauD = tau.ap[:, d, :]
                maskD = tau.ap[:, 2 + d, :]
                for i in range(8):
                    st = sg * 8 + i
                    sl = slice(i * TC, (i + 1) * TC)
                    self.ts("dve", ang, ang.ap[:, sl], tau, tauD, c["th"].ap[:, st:st + 1], None, ALU.mult, R=[c["th"]])
                    self.act(Rm, Rm.ap[:, sl], tau, maskD, AF.Copy, scale=c["r"].ap[:, st:st + 1], R=[c["r"]])
                self.sincos(ang, W, sn, cs, scr)
                self.act(nsn, nsn.ap, sn, sn.ap, AF.Copy, scale=-1.0)
                self.act(ncs, ncs.ap, cs, cs.ap, AF.Copy, scale=-1.0)
                for i in range(8):
                    st = sg * 8 + i
                    sl = slice(i * TC, (i + 1) * TC)
                    self.act(wr, wr.ap[:, sl], cs, cs.ap[:, sl], AF.Copy, scale=c["fre"].ap[:, st:st + 1], R=[c["fre"]])
                    self.act(wi, wi.ap[:, sl], cs, cs.ap[:, sl], AF.Copy, scale=c["fim"].ap[:, st:st + 1], R=[c["fim"]])
                    self.stt("dve", wr, wr.ap[:, sl], sn, sn.ap[:, sl], c["fim"].ap[:, st:st + 1], wr, wr.ap[:, sl], ALU.mult, ALU.add, R=[c["fim"]])
                    self.stt("dve", wi, wi.ap[:, sl], sn, sn.ap[:, sl], c["nfre"].ap[:, st:st + 1], wi, wi.ap[:, sl], ALU.mult, ALU.add, R=[c["nfre"]])
                first = 0 if d == 0 else TC - 1
                last = TC - 1 if d == 0 else 0
                seq = fseq if d == 0 else bseq
                c1 = c["c1"].ap[:, sg * 8:(sg + 1) * 8]
                s1 = c["s1"].ap[:, sg * 8:(sg + 1) * 8]

                def s1_pe(qi):
                    t0 = seq[qi] * TC
                    for i in range(8):
                        for ri, (pb, pap) in enumerate(((PRb, pr_ap), (PIb, pi_ap))):
                            self.mm(pb[i // 4], pap[:, i * TC:(i + 1) * TC], Br, Br.ap[:, 2 * i + ri, :], ur, ur.ap[:, i // 4, t0:t0 + TC])
                    self.fw.op("act", lambda g: g.copy(bre.ap, pr_ap), PRb, [bre])
                    self.fw.op("act", lambda g: g.copy(bim.ap, pi_ap), PIb, [bim])

                def s1_dve(qi):
                    xr, xi_ = Xre[qi % 2], Xim[qi % 2]
                    self.tt("dve", t1, t1.ap, wr, wr.ap, bre, bre.ap, ALU.mult)
                    self.tt("dve", t2, t2.ap, wi, wi.ap, bim, bim.ap, ALU.mult)
                    self.tt("dve", t3, t3.ap, wr, wr.ap, bim, bim.ap, ALU.mult)
                    self.tt("dve", t4, t4.ap, wi, wi.ap, bre, bre.ap, ALU.mult)
                    self.tt("dve", xr, xr.ap, t1, t1.ap, t2, t2.ap, ALU.subtract)
                    self.tt("dve", xi_, xi_.ap, t3, t3.ap, t4, t4.ap, ALU.add)

                def s2_dve(qi):
                    xr, xi_ = Xre[qi % 2], Xim[qi % 2]
                    g_re, g_im = gre[qi % 2], gim[qi % 2]
                    p_re, p_im = gre[(qi + 1) % 2], gim[(qi + 1) % 2]
                    u1, u2, u3, u4 = uu[qi % 2]
                    if qi > 0:
                        a, b_, e_, f_ = cr
                        self.tt("dve", a, a.ap, p_re, v3(p_re, last), c["c1"], c1, ALU.mult)
                        self.tt("dve", b_, b_.ap, p_im, v3(p_im, last), c["s1"], s1, ALU.mult)
                        self.tt("dve", e_, e_.ap, p_re, v3(p_re, last), c["s1"], s1, ALU.mult)
                        self.tt("dve", f_, f_.ap, p_im, v3(p_im, last), c["c1"], c1, ALU.mult)
                        self.tt("dve", a, a.ap, a, a.ap, b_, b_.ap, ALU.subtract)
                        self.tt("dve", e_, e_.ap, e_, e_.ap, f_, f_.ap, ALU.add)
                        self.tt("dve", xr, v3(xr, first), xr, v3(xr, first), a, a.ap, ALU.add)
                        self.tt("dve", xi_, v3(xi_, first), xi_, v3(xi_, first), e_, e_.ap, ALU.add)
                    rv = (lambda ap: ap) if d == 0 else (lambda ap: ap[:, ::-1])
                    self.op("dve", lambda g: g.tensor_tensor_scan(rv(g_re.ap), rv(Rm.ap), rv(xr.ap), 0.0, ALU.mult, ALU.add), [Rm, xr], [g_re])
                    self.op("dve", lambda g: g.tensor_tensor_scan(rv(g_im.ap), rv(Rm.ap), rv(xi_.ap), 0.0, ALU.mult, ALU.add), [Rm, xi_], [g_im])
                    self.tt("dve", u1, u1.ap, cs, cs.ap, g_re, g_re.ap, ALU.mult)
                    self.tt("dve", u2, u2.ap, nsn, nsn.ap, g_im, g_im.ap, ALU.mult)
                    self.tt("dve", u3, u3.ap, nsn, nsn.ap, g_re, g_re.ap, ALU.mult)
                    self.tt("dve", u4, u4.ap, ncs, ncs.ap, g_im, g_im.ap, ALU.mult)

                def s2_pe(qi):
                    t0 = seq[qi] * TC
                    u1, u2, u3, u4 = uu[qi % 2]
                    PY = PYs[qi % 2]
                    for o in range(2):
                        k = 0
                        for i in range(o * 4, o * 4 + 4):
                            for ri, ht in ((0, u1), (0, u2), (1, u3), (1, u4)):
                                self.mm(PY, PY.ap[:, o * TC:(o + 1) * TC], Cr, Cr.ap[:, 2 * i + ri, :], ht, ht.ap[:, i * TC:(i + 1) * TC],
                                        start=(k == 0), stop=(k == 15))
                                k += 1
                    py3 = PY.ap[:, 0:2 * TC].rearrange("p (o t) -> p o t", t=TC)
                    if d == 0:
                        self.cp("act", yt, yt.ap[:, :, t0:t0 + TC], PY, py3)
                    else:
                        self.tt("dve", yt, yt.ap[:, :, t0:t0 + TC], yt, yt.ap[:, :, t0:t0 + TC], PY, py3, ALU.add)

                s1_pe(0)
                s1_dve(0)
                for qi in range(18):
                    if qi + 1 < 18:
                        s1_pe(qi + 1)
                    s2_dve(qi)
                    s2_pe(qi)
                    if qi + 1 < 18:
                        s1_dve(qi + 1)
            for o in range(2):
                oc = 2 * sg + o
                self.ld(ustg, ustg.ap, self.PROJT[oc * 128:(oc + 1) * 128, :])
                self.stt("dve", yt, yt.ap[:, o, :], ustg, ustg.ap, dsk.ap[:, oc:oc + 1], yt, yt.ap[:, o, :], ALU.mult, ALU.add, R=[dsk])
                self.act(ustg, ustg.ap, yt, yt.ap[:, o, :], AF.Gelu_apprx_tanh)
                self.st(self.GT[oc * 128:(oc + 1) * 128, :], ustg, ustg.ap)
        self.new_phase()
        gR = self.tile([128, 8, NT], F32R)
        wgR = self.tile([128, 8, 1024], F32R)
        stg = [self.tile([128, 1024]) for _ in range(2)]
        Gv = self.GT.rearrange("(kc p) t -> p kc t", p=128)
        Wg = wglu[j].rearrange("(kc p) n -> p kc n", p=128)
        for kc in range(8):
            x = stg[kc % 2]
            self.ld(x, x.ap, Wg[:, kc, :])
            self.cp(self.alt(), wgR, wgR.ap[:, kc, :], x, x.ap)
        k = 0
        for kc in range(8):
            for (t0, n, s) in BLKS:
                x = stg[k % 2]; k += 1
                self.ld(x, x.ap[:, 0:n], Gv[:, kc, t0:t0 + n])
                self.cp(self.alt(), gR, gR.ap[:, kc, t0:t0 + n], x, x.ap[:, 0:n])
        gF = [self.tile([128, 512]) for _ in range(2)]
        zs = [self.tile([128, 512]) for _ in range(2)]
        tg = [self.tile([128, 512]) for _ in range(2)]
        ob = [self.tile([128, 512]) for _ in range(2)]
        k = 0
        for ncz in range(8):
            for (t0, n, s) in BLKS:
                ps = self.PS[k % 4]
                g_, z_, t_, o_ = gF[k % 2], zs[k % 2], tg[k % 2], ob[k % 2]
                k += 1
                self.ld(g_, g_.ap[:, 0:n], self.GT[ncz * 128:(ncz + 1) * 128, t0:t0 + n])
                self.ld(z_, z_.ap[:, 0:n], self.PROJT[1024 + ncz * 128:1024 + (ncz + 1) * 128, t0:t0 + n])
                for kc in range(8):
                    self.mm(ps, ps.ap[:, 0:n], wgR, wgR.ap[:, kc, ncz * 128:(ncz + 1) * 128], gR, gR.ap[:, kc, t0:t0 + n], start=(kc == 0), stop=(kc == 7))
                self.act(t_, t_.ap[:, 0:n], ps, ps.ap[:, 0:n], AF.Sigmoid)
                self.tt("dve", t_, t_.ap[:, 0:n], t_, t_.ap[:, 0:n], g_, g_.ap[:, 0:n], ALU.mult)
                self.act(z_, z_.ap[:, 0:n], z_, z_.ap[:, 0:n], AF.Silu)
                self.tt("dve", o_, o_.ap[:, 0:n], t_, t_.ap[:, 0:n], z_, z_.ap[:, 0:n], ALU.mult)
                self.st(self.MRGT[ncz * 128:(ncz + 1) * 128, t0:t0 + n], o_, o_.ap[:, 0:n])

    def phase_da(self, j, daq, lamT, rope, lam_init):
        self.new_phase()
        dq = self.tile([128, 3])
        self.ld(dq, dq.ap, daq[j])
        sgc = self.tile([128, 1])
        self.ts("dve", sgc, sgc.ap, dq, dq.ap[:, 2:3], 1.0 - lam_init, None, ALU.mult)
        lt = self.tile([64, 4])
        self.ld(lt, lt.ap, lamT[j])
        pr = self.tile([64, 2])
        self.tt("dve", pr, pr.ap[:, 0:1], lt, lt.ap[:, 0:1], lt, lt.ap[:, 1:2], ALU.mult)
        self.tt("dve", pr, pr.ap[:, 1:2], lt, lt.ap[:, 2:3], lt, lt.ap[:, 3:4], ALU.mult)
        p6, p7 = self.PS[6], self.PS[7]
        self.mm(p6, p6.ap[:, 0:2], self.cst, self.cst.ap[0:64, 1, :], pr, pr.ap)
        le = self.tile([128, 2])
        self.act(le, le.ap, p6, p6.ap[:, 0:2], AF.Exp)
        nlam = self.tile([128, 1])
        self.tt("dve", nlam, nlam.ap, le, le.ap[:, 1:2], le, le.ap[:, 0:1], ALU.subtract)
        self.ts("dve", nlam, nlam.ap, nlam, nlam.ap, -lam_init, None, ALU.add)
        cosT = self.tile([128, 2048]); sinT = self.tile([128, 2048])
        self.ld(cosT, cosT.ap, rope[0]); self.ld(sinT, sinT.ap, rope[1])
        q0 = [self.tile([128, NT], F32R) for _ in range(2)]
        q1 = [self.tile([128, NT], F32R) for _ in range(2)]
        kr = [self.tile([128, NT], F32R) for _ in range(2)]
        vr = [self.tile([128, 18, 128], F32R) for _ in range(2)]
        zd = [self.tile([128, NT]) for _ in range(2)]
        mo = [self.tile([128, NT]) for _ in range(2)]
        for q in q0 + q1:
            self.ts("dve", q, q.ap[:, 0:2048], cosT, cosT.ap, 0.0, None, ALU.mult)
            self.ts("dve", q, q.ap[:, 2048:NT], cosT, cosT.ap[:, 0:NT - 2048], 0.0, None, ALU.mult)
        qraw = self.tile([128, NT]); kraw = self.tile([128, NT])
        vT = self.tile([128, NT])
        sq = [self.tile([128, 512], F32R) for _ in range(2)]
        rstd = [self.tile([128, 512]) for _ in range(2)]
        qg = [self.tile([128, 512], F32R) for _ in range(2)]
        t1 = [self.tile([128, 512]) for _ in range(2)]
        t2 = [self.tile([128, 512]) for _ in range(2)]
        pT = [self.tile([128, 512], F32R) for _ in range(3)]
        rr = [self.tile([128, 512]) for _ in range(2)]
        am = [[self.tile([128, 512]) for _ in range(2)] for _ in range(2)]
        rs2 = self.tile([128, 512])
        VT = self.VTOK.rearrange("(tt p) n -> p tt n", p=128)
        qblks = [(0, 256, [0, 1])] + [(256 + 512 * i, 512, list(range(18))) for i in range(4)]
        st_ = {"cnt": 0, "u": 0, "g": 0}

        ots = [self.tile([128, 512]) for _ in range(2)]

        def load_head(hd):
            b_ = hd % 2
            self.ld(qraw, qraw.ap, self.PROJT[2048 + hd * 128:2048 + (hd + 1) * 128, :])
            self.ld(kraw, kraw.ap, self.PROJT[3072 + hd * 128:3072 + (hd + 1) * 128, :])
            self.ld(zd[b_], zd[b_].ap, self.PROJT[5120 + hd * 128:5120 + (hd + 1) * 128, :])
            self.ld(vT, vT.ap, self.PROJT[4096 + hd * 128:4096 + (hd + 1) * 128, :])

        def vtr_stages(hd):
            b_ = hd % 2
            out = []
            for g4 in range(5):
                tts = list(range(g4 * 4, min(g4 * 4 + 4, 18)))

                def st_a(tts=tts):
                    for ii, tt in enumerate(tts):
                        self.tp(p7, p7.ap[:, ii * 128:(ii + 1) * 128], vT, vT.ap[:, tt * 128:(tt + 1) * 128], self.ident)

                def st_b(tts=tts):
                    nn = len(tts)
                    self.cp("act", vr[b_], vr[b_].ap[:, tts[0]:tts[0] + nn, :], p7, p7.ap[:, 0:nn * 128].rearrange("p (a b) -> p a b", b=128))
                out += [st_a, st_b]
            return out

        def silu_head(hd):
            b_ = hd % 2
            self.act(zd[b_], zd[b_].ap, zd[b_], zd[b_].ap, AF.Silu)

        from collections import deque
        bg = deque()

        def prep_stages(hd, which, bi):
            b_ = hd % 2
            raw = qraw if which == 0 else kraw
            t0, n, s = BLKS[bi]
            u = st_["u"]; st_["u"] += 1
            q_, rs_, qg_, t1_, t2_ = sq[u % 2], rstd[u % 2], qg[u % 2], t1[u % 2], t2[u % 2]
            qgf = qg_.ap.bitcast(F32)
            r0_ = t0 - NCTX

            def s0():
                self.tt("dve", q_, q_.ap[:, 0:n], raw, raw.ap[:, t0:t0 + n], raw, raw.ap[:, t0:t0 + n], ALU.mult)

            def s1():
                self.mm(p6, p6.ap[:, 0:n], self.blk64r, self.blk64r.ap, q_, q_.ap[:, 0:n])

            def s2():
                self.rsqrt_from(rs_, rs_.ap[:, 0:n], p6, p6.ap[:, 0:n], 1.0 / 64)

            def s3():
                self.stt("dve", qg_, qg_.ap[:, 0:n], raw, raw.ap[:, t0:t0 + n], dq.ap[:, which:which + 1], rs_, rs_.ap[:, 0:n],
                         ALU.mult, ALU.mult, R=[dq])
                if s != 0:
                    if which == 0:
                        self.cp("dve", q0[b_], q0[b_].ap[0:64, t0:t0 + n], qg_, qgf[0:64, 0:n])
                        self.cp("dve", q1[b_], q1[b_].ap[64:128, t0:t0 + n], qg_, qgf[64:128, 0:n])
                    else:
                        self.cp("dve", kr[b_], kr[b_].ap[:, t0:t0 + n], qg_, qgf[:, 0:n])

            def s4():
                self.mm(p7, p7.ap[:, 0:n], self.protr, self.protr.ap, qg_, qg_.ap[:, 0:n])
                self.tt("dve", t1_, t1_.ap[:, 0:n], qg_, qgf[:, 0:n], cosT, cosT.ap[:, r0_:r0_ + n], ALU.mult)

            def s5():
                self.tt("dve", t2_, t2_.ap[:, 0:n], p7, p7.ap[:, 0:n], sinT, sinT.ap[:, r0_:r0_ + n], ALU.mult)
                if which == 0:
                    self.tt("dve", q0[b_], q0[b_].ap[0:64, t0:t0 + n], t1_, t1_.ap[0:64, 0:n], t2_, t2_.ap[0:64, 0:n], ALU.add)
                    self.tt("dve", q1[b_], q1[b_].ap[64:128, t0:t0 + n], t1_, t1_.ap[64:128, 0:n], t2_, t2_.ap[64:128, 0:n], ALU.add)
                else:
                    self.tt("dve", kr[b_], kr[b_].ap[:, t0:t0 + n], t1_, t1_.ap[:, 0:n], t2_, t2_.ap[:, 0:n], ALU.add)
            return [s0, s1, s2, s3] + ([s4, s5] if s == 0 else [])

        sqt = self.tile([128, 512], F32R)

        def tail_stages(hd, t0, n, a0, a1, qbi, last):
            b_ = hd % 2
            ot = ots[qbi % 2]

            def t0_():
                self.stt("dve", ot, ot.ap[:, 0:n], a1, a1.ap[:, 0:n], nlam.ap[:, 0:1], a0, a0.ap[:, 0:n], ALU.mult, ALU.add, R=[nlam])
                self.tt("dve", sqt, sqt.ap[:, 0:n], ot, ot.ap[:, 0:n], ot, ot.ap[:, 0:n], ALU.mult)

            def t1_():
                self.mm(p6, p6.ap[:, 0:n], self.onesr, self.onesr.ap, sqt, sqt.ap[:, 0:n])

            def t2_():
                self.rsqrt_from(rs2, rs2.ap[:, 0:n], p6, p6.ap[:, 0:n], 1.0 / 128)

            def t3_():
                self.stt("dve", ot, ot.ap[:, 0:n], ot, ot.ap[:, 0:n], sgc.ap[:, 0:1], rs2, rs2.ap[:, 0:n], ALU.mult, ALU.mult, R=[sgc])
                self.tt("dve", mo[b_], mo[b_].ap[:, t0:t0 + n], ot, ot.ap[:, 0:n], zd[b_], zd[b_].ap[:, t0:t0 + n], ALU.mult)
                if last:
                    self.st(self.MRGT[1024 + hd * 128:1024 + (hd + 1) * 128, :], mo[b_], mo[b_].ap)
            return [t0_, t1_, t2_, t3_]

        def flush():
            while bg:
                bg.popleft()()

        units = [(w, bi) for w in range(2) for bi in range(5)]
        load_head(0)
        silu_head(0)
        for f in vtr_stages(0):
            f()
        for w, bi in units:
            for f in prep_stages(0, w, bi):
                f()
        for hd in range(8):
            b_ = hd % 2
            if hd + 1 < 8:
                load_head(hd + 1)
            ui = 0
            for qbi, (t0, n, keys) in enumerate(qblks):
                for m in range(2):
                    qm = q0[b_] if m == 0 else q1[b_]
                    g = st_["g"]; st_["g"] += 1
                    Om, RSm = self.PS[2 + 2 * (g % 2)], self.PS[3 + 2 * (g % 2)]
                    nk = len(keys)
                    base = st_["cnt"]; st_["cnt"] += nk

                    def smm(ki):
                        sp = self.PS[(base + ki) % 2]
                        kt = keys[ki]
                        self.mm(sp, sp.ap[:, 0:n], kr[b_], kr[b_].ap[:, kt * 128:(kt + 1) * 128], qm, qm.ap[:, t0:t0 + n])
                    smm(0)
                    for ki, kt in enumerate(keys):
                        sp = self.PS[(base + ki) % 2]
                        p_ = pT[(base + ki) % 3]
                        if ki + 1 < nk:
                            smm(ki + 1)
                        self.act(p_, p_.ap[:, 0:n], sp, sp.ap[:, 0:n], AF.Exp, scale=0.125)
                        self.mm(Om, Om.ap[:, 0:n], vr[b_], vr[b_].ap[:, kt, :], p_, p_.ap[:, 0:n], start=(ki == 0), stop=(ki == nk - 1))
                        self.mm(RSm, RSm.ap[:, 0:n], self.onesr, self.onesr.ap, p_, p_.ap[:, 0:n], start=(ki == 0), stop=(ki == nk - 1))
                        if bg:
                            bg.popleft()()
                    r_, a_ = rr[m], am[(g // 2) % 2][m]
                    self.op("dve", lambda g_, r_=r_, RSm=RSm, n=n: g_.reciprocal(r_.ap[:, 0:n], RSm.ap[:, 0:n]), [RSm], [r_])
                    self.tt("dve", a_, a_.ap[:, 0:n], Om, Om.ap[:, 0:n], r_, r_.ap[:, 0:n], ALU.mult)
                    if hd + 1 < 8 and ui < len(units):
                        if ui == 1:
                            bg.append(lambda hd=hd: silu_head(hd + 1))
                        bg.extend(prep_stages(hd + 1, *units[ui])); ui += 1
                a0, a1 = am[((st_["g"] - 1) // 2) % 2]
                bg.extend(tail_stages(hd, t0, n, a0, a1, qbi, qbi == len(qblks) - 1))
                if qbi == 1 and hd + 1 < 8:
                    bg.extend(vtr_stages(hd + 1))
            flush()

    def phase_gla(self, j, glag, gmask):
        self.new_phase()
        CH = 128
        NCH = NT // CH
        scale = 256.0 ** -0.5
        gg = self.tile([128, 4])
        self.ld(gg, gg.ap, glag[j])
        gm = [self.tile([128, NT]) for _ in range(2)]
        for d in range(2):
            self.ld(gm[d], gm[d].ap, gmask[d])
        qraw = self.tile([128, 2, NT]); kraw = self.tile([128, 2, NT])
        OT = self.tile([128, 4, NT])
        la = self.tile([128, NT]); bt = self.tile([128, NT])
        qd = self.tile([128, 2, NT], F32R); kd = self.tile([128, 2, NT], F32R)
        ebend = self.tile([128, 2, NCH])
        S = self.tile([128, 2, 512]); Sr = self.tile([128, 2, 512], F32R)
        vraw = [self.tile([128, 4, CH]) for _ in range(2)]
        vr = [self.tile([CH, 512], F32R) for _ in range(2)]
        scTs = [self.tile([CH, CH], F32R) for _ in range(3)]
        kcts = [self.tile([128, 2, CH]) for _ in range(3)]
        kendTs = [self.tile([CH, 256], F32R) for _ in range(3)]
        sq = [self.tile([128, 512], F32R) for _ in range(2)]
        rstd = self.tile([128, 512])
        zt = [self.tile([128, 512]) for _ in range(2)]
        yt = [self.tile([128, 512]) for _ in range(2)]
        ob = [self.tile([128, 512]) for _ in range(2)]
        psSs, psT, psO, psU, psN = [self.PS[0], self.PS[7]], self.PS[1], [self.PS[2], self.PS[3]], [self.PS[4], self.PS[5]], self.PS[6]
        nctx = NCTX // CH
        fseq = list(range(NCH))
        bseq = list(range(nctx - 1, -1, -1)) + list(range(NCH - 1, nctx - 1, -1))
        vi = 0
        for hd in range(4):
            self.ld(qraw, qraw.ap, self.PROJT[hd * 256:(hd + 1) * 256, :].rearrange("(c p) t -> p c t", p=128))
            self.ld(kraw, kraw.ap, self.PROJT[1024 + hd * 256:1024 + (hd + 1) * 256, :].rearrange("(c p) t -> p c t", p=128))
            for d in range(2):
                last = CH - 1 if d == 0 else 0
                for dkc in range(2):
                    r0 = hd * 256 + dkc * 128
                    self.ld(la, la.ap, self.LAT[d, r0:r0 + 128, :])
                    if d == 0:
                        self.op("dve", lambda g: g.tensor_tensor_scan(bt.ap, gm[0].ap, la.ap, 0.0, ALU.mult, ALU.add), [gm[0], la], [bt])
                    else:
                        self.op("dve", lambda g: g.tensor_tensor_scan(bt.ap[:, ::-1], gm[1].ap[:, ::-1], la.ap[:, ::-1], 0.0, ALU.mult, ALU.add), [gm[1], la], [bt])
                    self.act(la, la.ap, bt, bt.ap, AF.Exp)
                    self.stt("dve", qd, qd.ap[:, dkc, :], qraw, qraw.ap[:, dkc, :], scale, la, la.ap, ALU.mult, ALU.mult)
                    self.cp("dve", ebend, ebend.ap[:, dkc, :], la, la.ap[:, last::CH])
                    self.act(bt, bt.ap, bt, bt.ap, AF.Exp, scale=-1.0)
                    self.tt("dve", kd, kd.ap[:, dkc, :], kraw, kraw.ap[:, dkc, :], bt, bt.ap, ALU.mult)
                self.op("pool", lambda g: g.memset(S.ap, 0.0), W=[S])
                self.ts("dve", Sr, Sr.ap, S, S.ap, 0.0, None, ALU.mult)
                tri = self.cst.ap[0:CH, 4 + d, 0:CH]
                kdf, qdf = kd.ap.bitcast(F32), qd.ap.bitcast(F32)
                for qi, c in enumerate(fseq if d == 0 else bseq):
                    t0 = c * CH
                    va, v_ = vraw[vi % 2], vr[vi % 2]
                    vi += 1
                    self.ld(va, va.ap, self.PROJT[2048 + hd * 512:2048 + (hd + 1) * 512, t0:t0 + CH].rearrange("(c p) t -> p c t", p=128))
                    for dvc in range(4):
                        self.tp(psN, psN.ap[0:CH, dvc * 128:(dvc + 1) * 128], va, va.ap[:, dvc, :], self.ident)
                    self.cp("act", v_, v_.ap, psN, psN.ap[0:CH, 0:512])
                    scT, kct, kendT, psS = scTs[qi % 3], kcts[qi % 3], kendTs[qi % 3], psSs[qi % 2]
                    tof = (qi % 2) * 256
                    for dkc in range(2):
                        self.mm(psS, psS.ap[0:CH, 0:CH], kd, kd.ap[:, dkc, t0:t0 + CH], qd, qd.ap[:, dkc, t0:t0 + CH], start=(dkc == 0), stop=(dkc == 1))
                    self.tt("dve", scT, scT.ap, psS, psS.ap[0:CH, 0:CH], self.cst, tri, ALU.mult)
                    for dkc in range(2):
                        self.act(kct, kct.ap[:, dkc, :], kd, kdf[:, dkc, t0:t0 + CH], AF.Copy, scale=ebend.ap[:, dkc, c:c + 1], R=[ebend])
                        self.tp(psT, psT.ap[0:CH, tof + dkc * 128:tof + (dkc + 1) * 128], kct, kct.ap[:, dkc, :], self.ident)
                    self.cp("act", kendT, kendT.ap, psT, psT.ap[0:CH, tof:tof + 256])
                    po = psO[qi % 2]
                    for dvc in range(4):
                        oap = po.ap[:, dvc * CH:(dvc + 1) * CH]
                        self.mm(po, oap, v_, v_.ap[:, dvc * 128:(dvc + 1) * 128], scT, scT.ap, start=True, stop=False)
                        self.mm(po, oap, Sr, Sr.ap[:, 0, dvc * 128:(dvc + 1) * 128], qd, qd.ap[:, 0, t0:t0 + CH], start=False, stop=False)
                        self.mm(po, oap, Sr, Sr.ap[:, 1, dvc * 128:(dvc + 1) * 128], qd, qd.ap[:, 1, t0:t0 + CH], start=False, stop=True)
                    po3 = po.ap[:, 0:4 * CH].rearrange("p (a b) -> p a b", b=CH)
                    if d == 0:
                        self.cp("act", OT, OT.ap[:, :, t0:t0 + CH], po, po3)
                    else:
                        self.tt("dve", OT, OT.ap[:, :, t0:t0 + CH], OT, OT.ap[:, :, t0:t0 + CH], po, po3, ALU.add)
                    for dkc in range(2):
                        pu = psU[dkc]
                        self.mm(pu, pu.ap, kendT, kendT.ap[:, dkc * 128:(dkc + 1) * 128], v_, v_.ap)
                        self.stt("dve", S, S.ap[:, dkc, :], S, S.ap[:, dkc, :], ebend.ap[:, dkc, c:c + 1], pu, pu.ap, ALU.mult, ALU.add, R=[ebend])
                        self.cp("dve", Sr, Sr.ap[:, dkc, :], S, S.ap[:, dkc, :])
            k = 0
            for (t0, n, s) in BLKS:
                for dvc in range(4):
                    q_ = sq[dvc % 2]
                    self.act(q_, q_.ap[:, 0:n], OT, OT.ap[:, dvc, t0:t0 + n], AF.Square)
                    self.mm(psN, psN.ap[:, 0:n], self.onesr, self.onesr.ap, q_, q_.ap[:, 0:n], start=(dvc == 0), stop=(dvc == 3))
                self.rsqrt_from(rstd, rstd.ap[:, 0:n], psN, psN.ap[:, 0:n], 1.0 / 512)
                for dvc in range(4):
                    z_, y_, o_ = zt[k % 2], yt[k % 2], ob[k % 2]
                    k += 1
                    zr = 4096 + hd * 512 + dvc * 128
                    self.ld(z_, z_.ap[:, 0:n], self.PROJT[zr:zr + 128, t0:t0 + n])
                    self.stt("dve", y_, y_.ap[:, 0:n], OT, OT.ap[:, dvc, t0:t0 + n], gg.ap[:, dvc:dvc + 1], rstd, rstd.ap[:, 0:n], ALU.mult, ALU.mult, R=[gg])
                    self.act(z_, z_.ap[:, 0:n], z_, z_.ap[:, 0:n], AF.Silu)
                    self.tt("dve", o_, o_.ap[:, 0:n], y_, y_.ap[:, 0:n], z_, z_.ap[:, 0:n], ALU.mult)
                    mr = hd * 512 + dvc * 128
                    self.st(self.MRGT[mr:mr + 128, t0:t0 + n], o_, o_.ap[:, 0:n])


def _consts():
    c = np.zeros((6, 128, 128), np.float32)
    c[0] = np.eye(128)
    c[1] = 1.0
    c[2, :64, :64] = 1.0
    c[2, 64:, 64:] = 1.0
    for m in range(2):
        o = m * 64
        for i in range(16):
            c[3, o + 16 + i, o + i] = -1.0
            c[3, o + i, o + 16 + i] = 1.0
            c[3, o + 48 + i, o + 32 + i] = -1.0
            c[3, o + 32 + i, o + 48 + i] = 1.0
    s = np.arange(128)
    c[4] = (s[:, None] <= s[None, :])
    c[5] = (s[:, None] >= s[None, :])
    return c


def _rope_tables():
    GRID_W = 64
    L = 2048
    row = np.repeat(np.arange(L // GRID_W, dtype=np.float32), GRID_W)
    col = np.tile(np.arange(GRID_W, dtype=np.float32), L // GRID_W)
    n_freq = 16
    inv_freq = (np.float32(10000.0) ** (-np.arange(n_freq, dtype=np.float32) / np.float32(n_freq))).astype(np.float32)
    ang_r = row[:, None] * inv_freq
    ang_c = col[:, None] * inv_freq
    ang = np.concatenate([ang_r, ang_r, ang_c, ang_c], axis=-1).astype(np.float32)
    cos, sin = np.cos(ang).astype(np.float32), np.sin(ang).astype(np.float32)
    t = np.zeros((2, 128, L), np.float32)
    t[0] = np.concatenate([cos.T, cos.T], axis=0)
    t[1] = np.concatenate([sin.T, sin.T], axis=0)
    return t


def _prep_shared(inp):
    f = lambda a: np.ascontiguousarray(a, dtype=np.float32)
    sh = {}
    sh["ada_w"] = f(inp["ada_w"])
    sh["adabT"] = f(inp["ada_b"].reshape(DEPTH, 48, 128).transpose(0, 2, 1))
    sh["ngT"] = f(inp["norm_g"].reshape(DEPTH, 16, 128).transpose(0, 2, 1))
    sh["w_in"] = f(np.stack([inp["ev_w_in"][0], inp["od_w_in"][0], inp["ev_w_in"][1], inp["od_w_in"][1]]))
    sh["w_out"] = f(np.stack([inp["ev_w_out"][0], inp["od_w_out"][0], inp["ev_w_out"][1], inp["od_w_out"][1]]))
    sh["consts"] = _consts()
    p = np.zeros((2, 2, 128, 3, 32), np.float32)
    for k, name in enumerate(["s5_a_re", "s5_a_im"]):
        a = inp[name].reshape(2, 2, 32, 2, 64)
        p[:, :, :, k, :] = a.transpose(0, 1, 3, 4, 2).reshape(2, 2, 128, 32)
    ldt = np.broadcast_to(inp["s5_log_dt"].reshape(2, 2, 32, 2, 1), (2, 2, 32, 2, 64))
    p[:, :, :, 2, :] = ldt.transpose(0, 1, 3, 4, 2).reshape(2, 2, 128, 32)
    sh["s5p"] = p
    Bm = np.zeros((2, 2, 2, 32, 128, 128), np.float32)
    Cm = np.zeros((2, 2, 2, 32, 128, 128), np.float32)
    for ri, (bn, cn) in enumerate([("s5_b_re", "s5_c_re"), ("s5_b_im", "s5_c_im")]):
        b = inp[bn]
        c = inp[cn]
        for st in range(32):
            for gi in range(2):
                g = 2 * st + gi
                g8 = g % 8
                Bm[:, :, ri, st, g8 * 16:(g8 + 1) * 16, gi * 64:(gi + 1) * 64] = b[:, :, g].transpose(0, 1, 3, 2)
                Cm[:, :, ri, st, gi * 64:(gi + 1) * 64, g8 * 16:(g8 + 1) * 16] = c[:, :, g].transpose(0, 1, 3, 2)
    sh["s5B"], sh["s5C"] = Bm, Cm
    sh["s5d"] = f(inp["s5_d"].reshape(2, 8, 128).transpose(0, 2, 1))
    tau = np.zeros((4, 128, TC), np.float32)
    tau[0] = np.arange(TC)[None, :]
    tau[1] = (TC - 1 - np.arange(TC))[None, :]
    tau[2] = 1.0; tau[2, :, 0] = 0.0
    tau[3] = 1.0; tau[3, :, TC - 1] = 0.0
    sh["s5tau"] = tau
    sh["wglu"] = f(inp["s5_w_glu"])
    dq = np.zeros((2, 128, 3), np.float32)
    dq[:, :, 0] = np.tile(inp["da_qn_g"], (1, 2))
    dq[:, :, 1] = np.tile(inp["da_kn_g"], (1, 2))
    dq[:, :, 2] = inp["da_subln_g"]
    sh["daq"] = dq
    sh["lamT"] = f(inp["da_lam"].transpose(0, 2, 1))
    sh["rope"] = _rope_tables()
    sh["wa1"] = f(inp["gla_wa1"])
    sh["wa2"] = f(inp["gla_wa2"])
    sh["baT"] = f(inp["gla_ba"].reshape(2, 2, 8, 128).transpose(0, 1, 3, 2))
    sh["glag"] = f(inp["gla_norm_g"].reshape(2, 4, 128).transpose(0, 2, 1))
    gm = np.ones((2, 128, NT), np.float32)
    gm[0, :, 0::128] = 0.0
    gm[1, :, 127::128] = 0.0
    sh["gmask"] = gm
    return sh


def _prep_core(inp, b):
    d = {}
    d["xin"] = np.ascontiguousarray(np.concatenate([inp["ctx"][b], inp["x"][b]], axis=0), dtype=np.float32)
    cT = np.zeros((128, 16, 2), np.float32)
    cT[:, :, 0] = inp["c"][b].reshape(16, 128).T
    cT[:, :, 1] = inp["c_ctx"].reshape(16, 128).T
    d["cT"] = cT
    return d


_NC_CACHE = {}


def kernel(**inputs):
    inp = {k: np.asarray(v) for k, v in inputs.items()}
    if "full" not in _NC_CACHE:
        _NC_CACHE["full"] = KB().build()
    nc = _NC_CACHE["full"]
    sh = _prep_shared(inp)
    in_maps = []
    for core in range(8):
        m = dict(sh)
        m.update(_prep_core(inp, core % 4))
        in_maps.append(m)
    res = run_bass_kernel_spmd(nc, in_maps, core_ids=list(range(8)))
    out = np.stack([np.asarray(res.results[b]["out"]) for b in range(4)], axis=0)
    return out.astype(np.float32)
```
